# Optimizing a Trainium2 kernel written in Bass

```python
import jax
import jax.numpy as jnp
from jax import lax
import numpy as np

D_MODEL = 4096
BATCH = 4
SEQ = 4096
DEPTH = 1

N_ATTN_HEADS = 16
HEAD_DIM = 128
ATTN_WIDTH = N_ATTN_HEADS * HEAD_DIM
N_CONV_GROUPS = 16
CONV_GROUP_DIM = 128
CONV_WIDTH = N_CONV_GROUPS * CONV_GROUP_DIM
CONV_K = 3
FFN_CONV_K = 3
D_FF = 7 * D_MODEL // 2
ROT_DIM = HEAD_DIM // 4
ROPE_THETA = 500000.0
MOBA_BLOCK = 256
MOBA_TOPK = 3
Q_CHUNK = 16
N_BRANCH = 2
N_MOD = 6
EPS = 1e-6
IN_PROJ_WIDTH = 3 * ATTN_WIDTH + 3 * CONV_WIDTH + N_BRANCH * D_MODEL

kernel_name = 'hybrid_moba_shortconv_convffn_adaln'


def rms_norm(x, g):
    xf = x.astype(jnp.float32)
    y = xf * lax.rsqrt(jnp.mean(xf * xf, axis=-1, keepdims=True) + EPS)
    return (y * g.astype(jnp.float32)).astype(x.dtype)


def modulate(h, shift, scale):
    return h * (1 + scale[:, None, :]) + shift[:, None, :]


def causal_dwconv(x, w):
    k = w.shape[0]
    return lax.conv_general_dilated(
        x, w[:, None, :].astype(x.dtype), window_strides=(1,), padding=[(k - 1, 0)],
        dimension_numbers=('NWC', 'WIO', 'NWC'), feature_group_count=x.shape[-1])


def partial_rope(x, positions):
    half = ROT_DIM // 2
    inv_freq = ROPE_THETA ** (-jnp.arange(0, ROT_DIM, 2, dtype=jnp.float32) / ROT_DIM)
    ang = positions.astype(jnp.float32)[:, None, :, None] * inv_freq
    cos, sin = jnp.cos(ang), jnp.sin(ang)
    xr = x[..., :ROT_DIM].astype(jnp.float32)
    x1, x2 = xr[..., :half], xr[..., half:]
    rot = jnp.concatenate([x1 * cos - x2 * sin, x2 * cos + x1 * sin], axis=-1).astype(x.dtype)
    return jnp.concatenate([rot, x[..., ROT_DIM:]], axis=-1)


def moba_attention(q, k, v):
    b, h, s, d = q.shape
    nb = -(-s // MOBA_BLOCK)
    s_pad = nb * MOBA_BLOCK
    pad = ((0, 0), (0, 0), (0, s_pad - s), (0, 0))
    q, k, v = jnp.pad(q, pad), jnp.pad(k, pad), jnp.pad(v, pad)
    scale = d ** -0.5
    k_blk = k.reshape(b, h, nb, MOBA_BLOCK, d)
    v_blk = v.reshape(b, h, nb, MOBA_BLOCK, d)
    k_mean = jnp.mean(k_blk.astype(jnp.float32), axis=3)
    q_pos = jnp.arange(s_pad)
    q_blk = q_pos // MOBA_BLOCK
    gate = jnp.einsum('bhsd,bhnd->bhsn', q.astype(jnp.float32), k_mean)
    fully_past = jnp.arange(nb)[None, :] < q_blk[:, None]
    gate = jnp.where(fully_past, gate, -jnp.inf)
    n_sel = max(min(MOBA_TOPK, nb - 1), 1)
    _, sel = lax.top_k(gate, n_sel)
    sel_valid = sel < q_blk[:, None]
    nc = s_pad // Q_CHUNK

    def to_chunks(t):
        t = t.reshape(b, h, nc, Q_CHUNK, *t.shape[3:])
        return jnp.moveaxis(t, 2, 0)

    bi = jnp.arange(b)[:, None, None]
    hi = jnp.arange(h)[None, :, None]
    kpos_local = jnp.arange(MOBA_BLOCK)

    def attend_chunk(args):
        q_c, sel_c, valid_c, pos_c = args
        start = (pos_c[0] // MOBA_BLOCK) * MOBA_BLOCK
        k_own = lax.dynamic_slice_in_dim(k, start, MOBA_BLOCK, axis=2)
        v_own = lax.dynamic_slice_in_dim(v, start, MOBA_BLOCK, axis=2)
        s_own = jnp.einsum('bhqd,bhkd->bhqk', q_c, k_own).astype(jnp.float32) * scale
        s_own = jnp.where((start + kpos_local)[None, :] <= pos_c[:, None], s_own, -jnp.inf)
        sel_flat = sel_c.reshape(b, h, Q_CHUNK * n_sel)
        k_sel = k_blk[bi, hi, sel_flat].reshape(b, h, Q_CHUNK, n_sel * MOBA_BLOCK, d)
        v_sel = v_blk[bi, hi, sel_flat].reshape(b, h, Q_CHUNK, n_sel * MOBA_BLOCK, d)
        s_sel = jnp.einsum('bhqd,bhqkd->bhqk', q_c, k_sel).astype(jnp.float32) * scale
        s_sel = jnp.where(jnp.repeat(valid_c, MOBA_BLOCK, axis=-1), s_sel, -jnp.inf)
        p = jax.nn.softmax(jnp.concatenate([s_own, s_sel], axis=-1), axis=-1).astype(v.dtype)
        return (jnp.einsum('bhqk,bhkd->bhqd', p[..., :MOBA_BLOCK], v_own)
                + jnp.einsum('bhqk,bhqkd->bhqd', p[..., MOBA_BLOCK:], v_sel))

    out = lax.map(attend_chunk, (to_chunks(q), to_chunks(sel), to_chunks(sel_valid),
                                 q_pos.reshape(nc, Q_CHUNK)))
    out = jnp.moveaxis(out, 0, 2).reshape(b, h, s_pad, d)
    return out[:, :, :s]


def setup_inputs(seed: int = 0) -> dict:
    key = jax.random.key(seed)
    ks = jax.random.split(key, 16)
    f32 = jnp.float32
    nrm = lambda k, shape, s: jax.random.normal(k, shape, f32) * s
    return {
        'x': nrm(ks[0], (BATCH, SEQ, D_MODEL), 1.0),
        'c': nrm(ks[1], (BATCH, D_MODEL), 1.0),
        'positions': jnp.broadcast_to(jnp.arange(SEQ, dtype=jnp.int32)[None, :], (BATCH, SEQ)),
        'w_ada': nrm(ks[2], (DEPTH, D_MODEL, N_MOD * D_MODEL), 0.5 * D_MODEL ** -0.5),
        'b_ada': nrm(ks[3], (DEPTH, N_MOD * D_MODEL), 0.01),
        'g_mix': 1.0 + nrm(ks[4], (DEPTH, D_MODEL), 0.05),
        'w_in': nrm(ks[5], (DEPTH, D_MODEL, IN_PROJ_WIDTH), D_MODEL ** -0.5),
        'conv_w': nrm(ks[6], (DEPTH, CONV_K, CONV_WIDTH), CONV_K ** -0.5),
        'w_attn_out': nrm(ks[7], (DEPTH, ATTN_WIDTH, D_MODEL), ATTN_WIDTH ** -0.5),
        'w_conv_out': nrm(ks[8], (DEPTH, CONV_WIDTH, D_MODEL), CONV_WIDTH ** -0.5),
        'w_o': nrm(ks[9], (DEPTH, D_MODEL, D_MODEL), D_MODEL ** -0.5),
        'g_ffn': 1.0 + nrm(ks[10], (DEPTH, D_MODEL), 0.05),
        'w_up': nrm(ks[11], (DEPTH, D_MODEL, 2 * D_FF), D_MODEL ** -0.5),
        'ffn_conv_w': nrm(ks[12], (DEPTH, FFN_CONV_K, 2 * D_FF), FFN_CONV_K ** -0.5),
        'w_down': nrm(ks[13], (DEPTH, D_FF, D_MODEL), D_FF ** -0.5),
        'g_final': 1.0 + nrm(ks[14], (D_MODEL,), 0.05),
    }


def reference(x, c, positions, w_ada, b_ada, g_mix, w_in, conv_w, w_attn_out, w_conv_out,
              w_o, g_ffn, w_up, ffn_conv_w, w_down, g_final):
    b, s, _ = x.shape
    a, cw = ATTN_WIDTH, CONV_WIDTH
    split_at = [a, 2 * a, 3 * a, 3 * a + cw, 3 * a + 2 * cw, 3 * a + 3 * cw, 3 * a + 3 * cw + D_MODEL]
    c_act = jax.nn.silu(c)
    for layer in range(DEPTH):
        mod = c_act @ w_ada[layer] + b_ada[layer]
        shift1, scale1, gate1, shift2, scale2, gate2 = jnp.split(mod, N_MOD, axis=-1)

        h = modulate(rms_norm(x, g_mix[layer]), shift1, scale1)
        proj = h @ w_in[layer]
        q, k, v, cb, cc, cu, g_att, g_cnv = jnp.split(proj, split_at, axis=-1)

        to_heads = lambda t: t.reshape(b, s, N_ATTN_HEADS, HEAD_DIM).transpose(0, 2, 1, 3)
        qh = partial_rope(to_heads(q), positions)
        kh = partial_rope(to_heads(k), positions)
        attn = moba_attention(qh, kh, to_heads(v))
        y_att = attn.transpose(0, 2, 1, 3).reshape(b, s, ATTN_WIDTH) @ w_attn_out[layer]

        y_cnv = (cb * causal_dwconv(cc * cu, conv_w[layer])) @ w_conv_out[layer]

        merged = jax.nn.sigmoid(g_att) * y_att + jax.nn.sigmoid(g_cnv) * y_cnv
        x = x + gate1[:, None, :] * (merged @ w_o[layer])

        h2 = modulate(rms_norm(x, g_ffn[layer]), shift2, scale2)
        up = causal_dwconv(h2 @ w_up[layer], ffn_conv_w[layer])
        u_gate, u_val = jnp.split(up, 2, axis=-1)
        x = x + gate2[:, None, :] * ((jax.nn.silu(u_gate) * u_val) @ w_down[layer])
    return rms_norm(x, g_final)
```

```python
import contextlib
import math
import os
import numpy as np
import concourse.bass as bass
import concourse.mybir as mybir
from concourse.bass_utils import run_bass_kernel_spmd

F32 = mybir.dt.float32
BF16 = mybir.dt.bfloat16
I32 = mybir.dt.int32
ALU = mybir.AluOpType
AF = mybir.ActivationFunctionType
AX = mybir.AxisListType

QUEUES = ('pe', 'act', 'dve', 'pool', 'sp')
EPOCH = 20000
NSLOT = 8


class Buf:
    __slots__ = ('w', 'r', 'rd', 'name')

    def __init__(self, name=''):
        self.w = None
        self.r = {}
        self.rd = []
        self.name = name


class Op:
    __slots__ = ('eng', 'fn', 'deps', 'sig', 'idx', 'seq', 'dma', 'slot', 'slotval')

    def __init__(self, eng, fn, dma):
        self.eng = eng
        self.fn = fn
        self.deps = []
        self.sig = False
        self.idx = -1
        self.seq = -1
        self.dma = dma
        self.slot = -1
        self.slotval = 0


class Prog:
    def __init__(self, nc):
        self.nc = nc
        self.q = {k: [] for k in QUEUES}
        self.ndma = {k: 0 for k in QUEUES}
        self.pending_dma = []

    def add(self, eng, fn, reads=(), writes=(), dma=False):
        op = Op(eng, fn, dma)
        op.idx = len(self.q[eng])
        deps = {}

        def need(d):
            if d is None or d is op:
                return
            if d.dma:
                deps[id(d)] = d
                return
            if d.eng == 'pe' and eng == 'pe' and not dma:
                return
            k = d.eng
            if k not in deps or deps[k].idx < d.idx:
                deps[k] = d

        for b in reads:
            need(b.w)
        for b in writes:
            need(b.w)
            for r in b.r.values():
                need(r)
            for r in b.rd:
                need(r)
        for b in reads:
            if dma:
                b.rd.append(op)
            else:
                b.r[eng] = op
        for b in writes:
            b.w = op
            b.r = {}
            b.rd = []
        op.deps = list(deps.values())
        for d in op.deps:
            d.sig = True
        if dma:
            i = self.ndma[eng]
            self.ndma[eng] = i + 1
            op.slot = i % NSLOT
            op.slotval = 16 * (i // NSLOT + 1)
            op.sig = True
            self.pending_dma.append(op)
        self.q[eng].append(op)
        return op

    def wait_ops(self, eng, ops):
        op = Op(eng, None, False)
        op.idx = len(self.q[eng])
        op.deps = [o for o in ops if o is not None]
        for d in op.deps:
            d.sig = True
        self.q[eng].append(op)
        return op

    def barrier(self):
        lasts = []
        for k in QUEUES:
            for o in reversed(self.q[k]):
                if not o.dma and o.fn is not None:
                    lasts.append(o)
                    break
        dmas = list(self.pending_dma)
        self.pending_dma = []
        for k in QUEUES:
            self.wait_ops(k, [o for o in lasts if o.eng != k] + dmas)

    def emit(self):
        nc = self.nc
        nsig = {}
        for k in QUEUES:
            s = 0
            for o in self.q[k]:
                if o.sig and not o.dma and o.fn is not None:
                    s += 1
                    o.seq = s
            nsig[k] = s
        with contextlib.ExitStack() as st:
            csem = {}
            for k in QUEUES:
                n = (nsig[k] + EPOCH - 1) // EPOCH
                csem[k] = [st.enter_context(nc.semaphore(f"c_{k}_{i}")) for i in range(max(n, 1))]
            dsem = {}
            for k in QUEUES:
                if self.ndma[k]:
                    dsem[k] = [st.enter_context(nc.semaphore(f"d_{k}_{i}")) for i in range(NSLOT)]
            block = st.enter_context(nc.Block())

            def run(k):
                def body(e):
                    cw = {kk: 0 for kk in QUEUES}
                    dw = {}

                    def wait_for(d):
                        if d.dma:
                            key = (d.eng, d.slot)
                            if dw.get(key, 0) >= d.slotval:
                                return
                            dw[key] = d.slotval
                            e.wait_ge(dsem[d.eng][d.slot], d.slotval)
                        else:
                            if cw[d.eng] >= d.seq:
                                return
                            cw[d.eng] = d.seq
                            ep, v = divmod(d.seq - 1, EPOCH)
                            e.wait_ge(csem[d.eng][ep], v + 1)

                    for o in self.q[k]:
                        for d in o.deps:
                            wait_for(d)
                        if o.fn is None:
                            continue
                        if o.dma:
                            if o.slotval > 16:
                                key = (k, o.slot)
                                if dw.get(key, 0) < o.slotval - 16:
                                    dw[key] = o.slotval - 16
                                    e.wait_ge(dsem[k][o.slot], o.slotval - 16)
                            ins = o.fn(e)
                            ins.then_inc(dsem[k][o.slot], 16)
                        else:
                            ins = o.fn(e)
                            if o.sig:
                                ep, v = divmod(o.seq - 1, EPOCH)
                                ins.then_inc(csem[k][ep], 1)
                return body

            if self.q['sp']:
                block.sync(run('sp'))
            if self.q['pe']:
                block.tensor(run('pe'))
            if self.q['act']:
                block.scalar(run('act'))
            if self.q['dve']:
                block.vector(run('dve'))
            if self.q['pool']:
                block.gpsimd(run('pool'))


class Cfg:
    def __init__(s, D=4096, S=4096, B=4, H=16, DFF=14336):
        s.D, s.S, s.B, s.H, s.DFF = D, S, B, H, DFF
        s.AW = H * 128
        s.CW = H * 128
        s.KC = D // 128
        s.NH = S // 2
        s.HALO = 4
        s.NE = s.NH + s.HALO
        s.NB = s.NH // 256
        s.NT = s.NH // 512
        s.JF = DFF // 128
        s.NBT = 2 * s.NB
        s.INW = 3 * s.AW + 3 * s.CW + 2 * D


ROPE_THETA = 500000.0
EPS = 1e-6
NEG = -30000.0


class Ring:
    def __init__(self, items):
        self.items = items
        self.i = 0

    def next(self):
        it = self.items[self.i % len(self.items)]
        self.i += 1
        return it


def build(cfg, debug=False):
    c = cfg
    D, KC, NH, NE, HALO, H, NB, NT, JF, NBT = c.D, c.KC, c.NH, c.NE, c.HALO, c.H, c.NB, c.NT, c.JF, c.NBT
    HC = H
    nc = bass.Bass("TRN2", target_bir_lowering=False)
    P = Prog(nc)
    skind = "ExternalOutput" if debug else "Internal"

    def din(name, shape, dt=F32):
        return nc.dram_tensor(name, list(shape), dt, kind="ExternalInput").ap()

    def dscr(name, shape, dt):
        return nc.dram_tensor(name, list(shape), dt, kind=skind).ap()

    xe = din("xe", [D, NE])
    xp = din("xp", [D, NH])
    cT = din("cT", [128, KC])
    pos_e = din("pos_e", [32, NE], I32)
    pos_p = din("pos_p", [32, NH], I32)
    cst = din("cst", [128, 148])
    cmat = din("cmat", [128, 128 + 32 + 128 + 128])
    w_ada = din("w_ada", [D, 6 * D])
    b_ada = din("b_ada", [1, 6 * D])
    vecs = din("vecs", [128, 3 * KC])
    wq = din("wq", [H, 128, KC, 128])
    wk = din("wk", [H, 128, KC, 128])
    wv = din("wv", [H, 128, KC, 128])
    wcb = din("wcb", [HC, 128, KC, 128])
    wcc = din("wcc", [HC, 128, KC, 128])
    wcu = din("wcu", [HC, 128, KC, 128])
    wga = din("wga", [KC, 128, KC, 128])
    wgc = din("wgc", [KC, 128, KC, 128])
    cw = din("cw", [128, HC * 3])
    wao = din("wao", [KC, 128, HC, 128])
    wco = din("wco", [KC, 128, HC, 128])
    wo = din("wo", [KC, 128, KC, 128])
    wup = din("wup", [2 * JF, 128, KC, 128])
    fcw = din("fcw", [128, 2 * JF * 3])
    wdn = din("wdn", [KC, 128, JF, 128])
    outT = nc.dram_tensor("outT", [D, NH], F32, kind="ExternalOutput").ap()

    QT = dscr("QT", [H, 128, NE], BF16)
    KT = dscr("KT", [H, 128, 2 * NH], BF16)
    VV = dscr("VV", [H, 2 * NH, 128], BF16)
    CVT = dscr("CVT", [HC, 128, NE], BF16)
    SGA = dscr("SGA", [KC, 128, NE], BF16)
    SGC = dscr("SGC", [KC, 128, NE], BF16)
    MT = dscr("MT", [KC, 128, NE], BF16)
    XM = dscr("XM", [KC, 128, NE], F32)
    ATs = dscr("ATs", [JF, 128, NH], BF16)
    XO = dscr("XO", [KC, 128, NH], F32)
    WDB = dscr("WDB", [KC, 128, JF, 128], BF16)
    B_WDB = [Buf() for _ in range(KC)]
    ATTd = dscr("ATTd", [H, 128, NE], BF16) if debug else None
    B_QT = [Buf() for _ in range(H)]
    B_KT = [[Buf(), Buf()] for _ in range(H)]
    B_VV = [[Buf(), Buf()] for _ in range(H)]
    B_CVT = [Buf() for _ in range(HC)]
    B_SGA = [Buf() for _ in range(KC)]
    B_SGC = [Buf() for _ in range(KC)]
    B_MT = [Buf() for _ in range(KC)]
    B_XM = [Buf() for _ in range(KC)]
    B_ATs = [Buf() for _ in range(JF)]
    B_XO = [[Buf() for _ in range(NT)] for _ in range(KC)]
    B_OUT = Buf()

    def mm(out, lhsT, rhs, start, stop, r, w):
        return P.add('pe', lambda e: e.matmul(out, lhsT, rhs, start=start, stop=stop), r, w)

    def tr(out, in_, ident, r, w):
        return P.add('pe', lambda e: e.transpose(out, in_, ident), r, w)

    def act(out, in_, func, r, w, bias=None, scale=None, accum_out=None):
        kw = {}
        if bias is not None:
            kw['bias'] = bias
        if scale is not None:
            kw['scale'] = scale
        if accum_out is not None:
            kw['accum_out'] = accum_out
        return P.add('act', lambda e: e.activation(out=out, in_=in_, func=func, **kw), r, w)

    def ts(eng, out, in0, s1, s2, op0, op1, r, w):
        if op1 is None:
            return P.add(eng, lambda e: e.tensor_scalar(out=out, in0=in0, scalar1=s1, scalar2=None, op0=op0), r, w)
        return P.add(eng, lambda e: e.tensor_scalar(out=out, in0=in0, scalar1=s1, scalar2=s2, op0=op0, op1=op1), r, w)

    def tt(eng, out, in0, in1, op, r, w):
        return P.add(eng, lambda e: e.tensor_tensor(out=out, in0=in0, in1=in1, op=op), r, w)

    def stt(eng, out, in0, scalar, in1, op0, op1, r, w):
        eng = 'dve'
        return P.add(eng, lambda e: e.scalar_tensor_tensor(out=out, in0=in0, scalar=scalar, in1=in1, op0=op0, op1=op1), r, w)

    def cp(eng, out, in_, r, w):
        if eng == 'act':
            return P.add('act', lambda e: e.copy(out=out, in_=in_), r, w)
        return P.add(eng, lambda e: e.tensor_copy(out=out, in_=in_), r, w)

    def mset(eng, ap, val, w):
        return P.add(eng, lambda e: e.memset(ap, val), (), w)

    def dma(q, out, in_, r, w):
        return P.add(q, lambda e: e.dma_start(out=out, in_=in_), r, w, dma=True)

    tiles_ext = [(0, HALO)] + [(HALO + 512 * i, 512) for i in range(NT)]
    tiles_own = [(HALO + 512 * i, 512) for i in range(NT)]
    tiles_prev = [(512 * i, 512) for i in range(NT)]

    with contextlib.ExitStack() as gst:
        def sb(name, shape, dt, st=None):
            return (st or gst).enter_context(nc.sbuf_tensor(name, list(shape), dt))

        ps = [gst.enter_context(nc.psum_tensor(f"ps{i}", [128, 512], F32)) for i in range(8)]
        B_ps = [Buf(f"ps{i}") for i in range(8)]
        main_ring = Ring(list(range(7)))
        MINI = 16
        B_mini = [B_ps[7] for _ in range(512 // MINI)]
        mini_ring = Ring(list(range(512 // MINI)))

        busy_banks = set()

        held_banks = set()

        def main_bank():
            while True:
                i = main_ring.next()
                if i not in busy_banks and i not in held_banks:
                    return i

        def psum_for(n):
            if n <= MINI:
                i = mini_ring.next()
                return ps[7][:, i * MINI:i * MINI + n], B_mini[i]
            i = main_bank()
            return ps[i][:, 0:n], B_ps[i]

        cst_sb = sb("cst_sb", [128, 148], F32)
        cm_sb = sb("cm_sb", [128, 416], BF16)
        ident = cm_sb[:, 0:128]
        permT = cm_sb[:, 128:160]
        maskneg = cm_sb[:, 160:288]
        ones = cm_sb[:, 288:416]
        vec_sb = sb("vec_sb", [128, 3 * KC], F32)
        modT = sb("modT", [128, 6 * KC], F32)
        a1 = sb("a1", [128, KC], F32)
        a2 = sb("a2", [128, KC], F32)
        kmT = sb("kmT", [128, H, NBT], F32)
        kmTb = sb("kmTb", [128, H, NBT], BF16)
        B_cst, B_cm, B_vec, B_modT, B_a1, B_a2 = Buf(), Buf(), Buf(), Buf(), Buf(), Buf()
        ones32 = sb("ones32", [128, 128], F32)
        B_ones32 = Buf()
        B_km = [Buf() for _ in range(H)]
        B_kmb = [Buf() for _ in range(H)]
        WR = 3
        B_wr = [Buf() for _ in range(WR)]
        wring = Ring(list(range(WR)))

        invf = cst_sb[0:32, 0:1]
        hflag = cst_sb[:, 1:2]
        epsc = cst_sb[:, 146:147]

        def EB(idx):
            return cst_sb[:, 2 + 16 * idx: 2 + 16 * idx + NBT]

        s1 = modT[:, 0 * KC:1 * KC]
        g1 = modT[:, 2 * KC:3 * KC]
        s2 = modT[:, 3 * KC:4 * KC]
        g2 = modT[:, 5 * KC:6 * KC]

        mset('dve', ones32[:], 1.0, [B_ones32])
        dma('sp', cst_sb[:], cst, (), [B_cst])
        dma('pool', cm_sb[:], cmat, (), [B_cm])
        dma('sp', vec_sb[:], vecs, (), [B_vec])

        w_items = []
        for h in range(H):
            w_items += [(wk[h], KC), (wv[h], KC)]
        for h in range(H):
            w_items += [(wq[h], KC), (wk[h], KC), (wv[h], KC)]
        for j in range(HC):
            w_items += [(wcc[j], KC), (wcu[j], KC), (wcb[j], KC)]
        for m in range(KC):
            w_items.append((wga[m], KC))
        for m in range(KC):
            w_items.append((wgc[m], KC))
        for m in range(KC):
            w_items += [(wao[m], HC), (wco[m], HC)]
        for m in range(KC):
            w_items.append((wo[m], KC))
        for j in range(JF):
            w_items += [(wup[j], KC), (wup[JF + j], KC)]
        w_state = {'issued': 0, 'taken': 0, 'slot': {}}

        def load_w(wdram_m, kcn):
            t = w_state['taken']
            assert w_items[t][0] is wdram_m or True
            ahead = int(os.environ.get("KPREF", "3"))
            while w_state['issued'] < min(t + ahead, len(w_items)):
                ii = w_state['issued']
                ap_, kn_ = w_items[ii]
                i = wring.next()
                dma('pool', wring_t[i][:, 0:kn_, :], ap_, (), [B_wr[i]])
                w_state['slot'][ii] = i
                w_state['issued'] = ii + 1
            i = w_state['slot'][t]
            assert w_items[t][1] == kcn
            w_state['taken'] = t + 1
            return wring_t[i], B_wr[i]

        pending = []

        def fill_step():
            while pending:
                try:
                    next(pending[0])
                    return
                except StopIteration:
                    pending.pop(0)

        def after_proj(post):
            if post is None:
                while pending:
                    fill_step()
                return
            pending.append(post())

        def proj(wt, wb, kcn, src, srcb, tiles, evac):
            regs = [psum_for(n) for (_, n) in tiles]
            order = sorted(range(len(tiles)), key=lambda i: (tiles[i][1] <= MINI, i))
            for (pap_, pb_) in regs:
                for bi_ in range(7):
                    if pb_ is B_ps[bi_]:
                        busy_banks.add(bi_)
            for kc in range(kcn):
                for i in order:
                    (c0, n), (pap, pb) = tiles[i], regs[i]
                    mm(pap, wt[:, kc, :], src[:, kc, c0:c0 + n], kc == 0, kc == kcn - 1, [wb, srcb[kc]], [pb])
                if kc >= 2:
                    fill_step()
            while pending:
                fill_step()
            busy_banks.clear()
            for i in ([order[-1]] + order[:-1]) if tiles[order[-1]][1] <= MINI else order:
                (c0, n), (pap, pb) = tiles[i], regs[i]
                evac(c0, n, pap, pb)

        c_act = sb("c_act", [128, KC], BF16)
        c_act32 = sb("c_act32", [128, KC], F32)
        B_ca32 = Buf()
        one11 = sb("one11", [1, 1], F32)
        B_wa2 = [Buf() for _ in range(3)]
        B_brow2, B_mrow2, B_modT2 = Buf(), Buf(), Buf()
        p0b_bufs = {}
        B_ca, B_one = Buf(), Buf()
        wst = contextlib.ExitStack()
        wring_t = [sb(f"wr{i}", [128, max(KC, HC), 128], BF16, wst) for i in range(WR)]
        with contextlib.ExitStack() as st:
            c_sb = sb("c_sb", [128, KC], F32, st)
            CWID = 2048
            wa_t = [sb(f"wa{i}", [128, CWID], BF16, st) for i in range(3)]
            B_wa = [Buf() for _ in range(3)]
            wa_ring = Ring([0, 1, 2])
            brow = sb("brow", [1, CWID], F32, st)
            mrow = sb("mrow", [1, CWID], F32, st)
            B_c, B_brow, B_mrow = Buf(), Buf(), Buf()
            dma('sp', c_sb[:], cT, (), [B_c])
            act(c_act32[:], c_sb[:], AF.Silu, [B_c], [B_ca32])
            cp('dve', c_act[:], c_act32[:], [B_ca32], [B_ca])
            mset('dve', one11[:], 1.0, [B_one])
            ncol = 2 * D
            modps = ps[7][:, 0:2 * KC]
            B_modps = B_ps[7]
            for c0 in range(0, ncol, CWID):
                wdt = min(CWID, ncol - c0)
                nb_ = (wdt + 511) // 512
                banks = [main_ring.next() for _ in range(nb_)]
                dma('sp', brow[:, 0:wdt], b_ada[:, c0:c0 + wdt], (), [B_brow])
                for kc in range(KC):
                    i = wa_ring.next()
                    dma('pool', wa_t[i][:, 0:wdt], w_ada[kc * 128:(kc + 1) * 128, c0:c0 + wdt], (), [B_wa[i]])
                    for bi, bk in enumerate(banks):
                        n = min(512, wdt - bi * 512)
                        mm(ps[bk][0:1, 0:n], c_act[:, kc:kc + 1], wa_t[i][:, bi * 512:bi * 512 + n],
                           kc == 0, kc == KC - 1, [B_ca, B_wa[i]], [B_ps[bk]])
                for bi, bk in enumerate(banks):
                    n = min(512, wdt - bi * 512)
                    tt('dve', mrow[:, bi * 512:bi * 512 + n], ps[bk][0:1, 0:n], brow[:, bi * 512:bi * 512 + n], ALU.add,
                       [B_ps[bk], B_brow], [B_mrow])
                for j in range(wdt // 128):
                    col = (c0 // 128) + j
                    mm(modps[:, col:col + 1], mrow[0:1, j * 128:(j + 1) * 128], one11[0:1, 0:1], True, True,
                       [B_mrow, B_one], [B_modps])
            cp('dve', modT[:, 0:2 * KC], modps, [B_modps], [B_modT])
            stt('dve', a1[:], modT[:, 1 * KC:2 * KC], 1.0, vec_sb[:, 0:KC], ALU.add, ALU.mult, [B_modT, B_vec], [B_a1])
            P.barrier()

        def p0b_gen():
            wa2_t, brow2, mrow2 = p0b_bufs['wa2'], p0b_bufs['brow2'], p0b_bufs['mrow2']
            G = min(4, KC)
            grp = [(c0, k0) for c0 in range(2 * D, 6 * D, 512) for k0 in range(0, KC, G)]
            NR = len(wa2_t)

            def issue(gidx):
                if gidx < len(grp):
                    c0_, k0_ = grp[gidx]
                    i_ = gidx % NR
                    dma('pool', wa2_t[i_][:], w_ada[k0_ * 128:(k0_ + G) * 128, c0_:c0_ + 512].rearrange("(k p) c -> p k c", p=128),
                        (), [B_wa2[i_]])
            for g0 in range(NR - 1):
                issue(g0)
            for gidx, (c0, k0) in enumerate(grp):
                issue(gidx + NR - 1)
                i = gidx % NR
                if k0 == 0:
                    dma('sp', brow2[:], b_ada[:, c0:c0 + 512], (), [B_brow2])
                for kk in range(G):
                    kc = k0 + kk
                    mm(ps[3][0:1, 0:512], c_act[:, kc:kc + 1], wa2_t[i][:, kk, :], kc == 0, kc == KC - 1, [B_ca, B_wa2[i]], [B_ps[3]])
                    yield
                if k0 + G >= KC:
                    tt('dve', mrow2[:], ps[3][0:1, 0:512], brow2[:], ALU.add, [B_ps[3], B_brow2], [B_mrow2])
                    for q in range(4):
                        mm(ps[7][:, 260 + q:261 + q], mrow2[0:1, q * 128:(q + 1) * 128], one11[0:1, 0:1], True, True,
                           [B_mrow2, B_one], [B_ps[7]])
                    col = c0 // 128
                    cp('dve', modT[:, col:col + 4], ps[7][:, 260:264], [B_ps[7]], [B_modT2])
                    yield

        def make_tables(st, pos_dram, N, name, tiles):
            cosT = sb(name + "cos", [32, N], F32, st)
            sinT = sb(name + "sin", [32, N], F32, st)
            B_cos, B_sin = Buf(), Buf()
            C1 = 6.28125
            C2 = 2 * math.pi - C1
            PI_LO = 3.1415925
            with contextlib.ExitStack() as st2:
                pi_ = sb(name + "pi", [32, 512], I32, st2)
                ang = sb(name + "ang", [32, 512], F32, st2)
                aa = sb(name + "aa", [32, 512], F32, st2)
                tq = sb(name + "tq", [32, 512], F32, st2)
                B_pi, B_ang, B_aa, B_tq = Buf(), Buf(), Buf(), Buf()
                for (c0, n) in tiles:
                    dma('sp', pi_[:, 0:n], pos_dram[:, c0:c0 + n], (), [B_pi])
                    cp('dve', ang[:, 0:n], pi_[:, 0:n], [B_pi], [B_ang])
                    ts('dve', ang[:, 0:n], ang[:, 0:n], invf, None, ALU.mult, None, [B_ang, B_cst], [B_ang])
                    for (dst, B_dst, shift) in ((sinT, B_sin, 0.0), (cosT, B_cos, math.pi / 2)):
                        ts('dve', aa[:, 0:n], ang[:, 0:n], shift, None, ALU.add, None, [B_ang], [B_aa])
                        ts('dve', tq[:, 0:n], aa[:, 0:n], 1.0 / (2 * math.pi), None, ALU.mult, None, [B_aa], [B_tq])
                        cp('dve', pi_[:, 0:n], tq[:, 0:n], [B_tq], [B_pi])
                        cp('dve', tq[:, 0:n], pi_[:, 0:n], [B_pi], [B_tq])
                        stt('dve', aa[:, 0:n], tq[:, 0:n], -C1, aa[:, 0:n], ALU.mult, ALU.add, [B_tq, B_aa], [B_aa])
                        stt('dve', aa[:, 0:n], tq[:, 0:n], -C2, aa[:, 0:n], ALU.mult, ALU.add, [B_tq, B_aa], [B_aa])
                        ts('dve', aa[:, 0:n], aa[:, 0:n], -PI_LO, PI_LO, ALU.max, ALU.min, [B_aa], [B_aa])
                        act(dst[:, c0:c0 + n], aa[:, 0:n], AF.Sin, [B_aa], [B_dst])
                P.barrier()
            return cosT, sinT, B_cos, B_sin

        def make_h(st, x_dram, N, tiles, a_vec, B_a, s_vec, name, src_is_scratch_bufs=None, rstd_out=None):
            hT = sb(name + "hT", [128, KC, N], BF16, st)
            B_h = [Buf() for _ in range(KC)]
            with contextlib.ExitStack() as st2:
                xs = [sb(name + f"xs{i}", [128, N], F32, st2) for i in range(3)]
                B_xs = [Buf(), Buf(), Buf()]
                sq = [sb(name + f"sq{i}", [128, N], BF16, st2) for i in range(2)]
                B_sq = [Buf(), Buf()]
                rstd = sb(name + "rstd", [128, N], F32, st2)
                B_rstd = Buf()
                regs = [psum_for(n) for (_, n) in tiles]
                for kc in range(KC):
                    i = kc % 3
                    j2 = kc % 2
                    rb = [src_is_scratch_bufs[kc]] if src_is_scratch_bufs else []
                    dma('sp' if kc % 2 == 0 else 'pool', xs[i][:], x_dram[kc], rb, [B_xs[i]])
                    act(sq[j2][:], xs[i][:], AF.Square, [B_xs[i]], [B_sq[j2]])
                    for (c0, n), (pap, pb) in zip(tiles, regs):
                        mm(pap, ones, sq[j2][:, c0:c0 + n], kc == 0, kc == KC - 1, [B_cm, B_sq[j2]], [pb])
                for (c0, n), (pap, pb) in zip(tiles, regs):
                    act(rstd[:, c0:c0 + n], pap, AF.Sqrt, [pb, B_cst], [B_rstd], bias=epsc, scale=1.0 / D)
                P.add('dve', lambda e: e.reciprocal(out=rstd[:], in_=rstd[:]), [B_rstd], [B_rstd])
                for kc in range(KC):
                    i = (kc + KC) % 3
                    rb = [src_is_scratch_bufs[kc]] if src_is_scratch_bufs else []
                    dma('sp' if kc % 2 == 0 else 'pool', xs[i][:], x_dram[kc], rb, [B_xs[i]])
                    stt('dve', xs[i][:], xs[i][:], a_vec[:, kc:kc + 1], rstd[:], ALU.mult, ALU.mult,
                        [B_xs[i], B_a, B_rstd], [B_xs[i]])
                    act(hT[:, kc, :], xs[i][:], AF.Identity, [B_xs[i], B_modT], [B_h[kc]], bias=s_vec[:, kc:kc + 1])
                P.barrier()
            return hT, B_h

        def rope_gen(stg, B_st, tiles, src_c0, cosT, sinT, B_cos, B_sin, rt, B_rt):
            for ti, (c0, n) in enumerate(tiles):
                pap, pb = psum_for(n)
                d0 = c0 - src_c0
                mm(pap[0:32, :], permT, stg[:, d0:d0 + n], True, True, [B_cm] + B_st(ti), [pb])
                k = ti % 2
                cp('act', rt[k][:, 0:n], pap[0:32, :], [pb], [B_rt[k]])
                tt('dve', rt[k][:, 0:n], rt[k][:, 0:n], sinT[:, c0:c0 + n], ALU.mult, [B_rt[k], B_sin], [B_rt[k]])
                tt('dve', rt[k + 2][:, 0:n], stg[0:32, d0:d0 + n], cosT[:, c0:c0 + n], ALU.mult, B_st(ti) + [B_cos], [B_rt[k + 2]])
                tt('dve', stg[0:32, d0:d0 + n], rt[k][:, 0:n], rt[k + 2][:, 0:n], ALU.add, [B_rt[k], B_rt[k + 2]], B_st(ti))
                yield

        def head_kv(st_tiles, hT, B_h, cosT, sinT, B_cos, B_sin, h, col0, half, stg_ring, vt_ring, rt, B_rt, tiles, src_c0):
            wt, wb = load_w(wk[h], KC)
            kst, B_kst = stg_ring.next()
            tix = {c0: ti for ti, (c0, n) in enumerate(tiles)}

            def evk(c0, n, pap, pb):
                cp('act', kst[:, c0 - src_c0:c0 - src_c0 + n], pap, [pb], [B_kst[tix[c0]], B_kst[tix[c0] + 1]])
            proj(wt, wb, KC, hT, B_h, tiles, evk)

            def post_k():
                yield from rope_gen(kst, lambda ti: [B_kst[ti], B_kst[ti + 1]], tiles, src_c0, cosT, sinT, B_cos, B_sin, rt, B_rt)
                nt_ = 5
                P.add('dve', lambda e: e.tensor_reduce(out=kmT[:, h, half * NB:(half + 1) * NB],
                                                       in_=kst[:, 0:NH].rearrange("p (b k) -> p b k", k=256),
                                                       axis=AX.X, op=ALU.add), B_kst[0:nt_], [B_km[h]])
                ts('dve', kmT[:, h, half * NB:(half + 1) * NB], kmT[:, h, half * NB:(half + 1) * NB], 1.0 / 256, None,
                   ALU.mult, None, [B_km[h]], [B_km[h]])
                dma('sp', KT[h][:, col0:col0 + NH], kst[:, 0:NH], B_kst[0:nt_], [B_KT[h][half]])
                if half == 1:
                    cp('dve', kmTb[:, h, :], kmT[:, h, :], [B_km[h]], [B_kmb[h]])
                yield
            after_proj(post_k)
            wt, wb = load_w(wv[h], KC)
            vst, B_vst = stg_ring.next()

            def evv(c0, n, pap, pb):
                cp('act', vst[:, c0 - src_c0:c0 - src_c0 + n], pap, [pb], [B_vst[tix[c0]], B_vst[tix[c0] + 1]])
            proj(wt, wb, KC, hT, B_h, tiles, evv)

            def post_v():
                vt, B_vt = vt_ring.next()
                for g in range(NH // 1024):
                    i = main_bank()
                    held_banks.add(i)
                    pb16 = ps[i].bitcast(BF16)
                    for t in range(8):
                        tok = g * 1024 + t * 128
                        tr(pb16[:, t * 128:(t + 1) * 128], vst[:, tok:tok + 128], ident, [B_vst[tok // 512], B_vst[tok // 512 + 1], B_cm], [B_ps[i]])
                        if t % 2 == 1 and t < 7:
                            yield
                    cp('dve', vt[:, g * 8:(g + 1) * 8, :], pb16[:, 0:1024].rearrange("p (t d) -> p t d", d=128), [B_ps[i]], [B_vt])
                    held_banks.discard(i)
                    yield
                dma('sp', VV[h][col0:col0 + NH, :].rearrange("(t p) d -> p t d", p=128), vt[:], [B_vt], [B_VV[h][half]])
                yield
            after_proj(post_v)

        with contextlib.ExitStack() as st:
            xp_c = [xp[kc * 128:(kc + 1) * 128, :] for kc in range(KC)]
            hTp, B_hp = make_h(st, xp_c, NH, tiles_prev, a1, B_a1, s1, "p1")
            cosP, sinP, B_cosP, B_sinP = make_tables(st, pos_p, NH, "tp", tiles_prev)
            stg = [sb(f"p1stg{i}", [128, NH], BF16, st) for i in range(2)]
            stg_ring = Ring([(stg[i], [Buf() for _ in range(5)]) for i in range(2)])
            vts = [sb(f"p1vt{i}", [128, NH // 128, 128], BF16, st) for i in range(2)]
            vt_ring = Ring([(vts[i], Buf()) for i in range(2)])
            rt = [sb(f"p1rt{i}", [32, 512], F32, st) for i in range(4)]
            B_rt = [Buf() for _ in range(4)]
            for h in range(H):
                head_kv(st, hTp, B_hp, cosP, sinP, B_cosP, B_sinP, h, 0, 0, stg_ring, vt_ring, rt, B_rt, tiles_prev, 0)
            after_proj(None)
            P.barrier()

        with contextlib.ExitStack() as st:
            xe_c = [xe[kc * 128:(kc + 1) * 128, :] for kc in range(KC)]
            hTe, B_he = make_h(st, xe_c, NE, tiles_ext, a1, B_a1, s1, "p2")
            with contextlib.ExitStack() as st2:
                cosE, sinE, B_cosE, B_sinE = make_tables(st2, pos_e, NE, "te", tiles_ext)
                stg = [sb(f"p2stg{i}", [128, NE], BF16, st2) for i in range(2)]
                stg_ring = Ring([(stg[i], [Buf() for _ in range(5)]) for i in range(2)])
                vts = [sb(f"p2vt{i}", [128, NH // 128, 128], BF16, st2) for i in range(2)]
                vt_ring = Ring([(vts[i], Buf()) for i in range(2)])
                rt = [sb(f"p2rt{i}", [32, 512], F32, st2) for i in range(4)]
                B_rt = [Buf() for _ in range(4)]
                for h in range(H):
                    wt, wb = load_w(wq[h], KC)
                    qst, B_qst = stg_ring.next()

                    tixe = {c0: ti for ti, (c0, n) in enumerate(tiles_ext)}

                    def evq(c0, n, pap, pb, qst=qst, B_qst=B_qst):
                        cp('act', qst[:, c0:c0 + n], pap, [pb], [B_qst[tixe[c0]]])
                    proj(wt, wb, KC, hTe, B_he, tiles_ext, evq)

                    def post_q(qst=qst, B_qst=B_qst, h=h):
                        yield from rope_gen(qst, lambda ti: [B_qst[ti]], tiles_ext, 0, cosE, sinE, B_cosE, B_sinE, rt, B_rt)
                        dma('sp', QT[h], qst[:], B_qst[0:5], [B_QT[h]])
                        yield
                    after_proj(post_q)
                    head_kv(st2, hTe, B_he, cosE, sinE, B_cosE, B_sinE, h, NH, 1, stg_ring, vt_ring, rt, B_rt, tiles_own, HALO)
                after_proj(None)
                P.barrier()
            with contextlib.ExitStack() as st2:
                ccst = sb("ccst", [128, NE], F32, st2)
                ust = sb("ust", [128, NE], F32, st2)
                yst = sb("yst", [128, NE], F32, st2)
                cvst = [sb(f"cvst{i}", [128, NE], BF16, st2) for i in range(2)]
                cw_sb = sb("cw_sb", [128, HC * 3], F32, st2)
                B_cc, B_u, B_y, B_cw = Buf(), Buf(), Buf(), Buf()
                B_cv = [Buf(), Buf()]
                dma('sp', cw_sb[:], cw, (), [B_cw])
                mset('dve', yst[:], 0.0, [B_y])
                for j in range(HC):
                    wt, wb = load_w(wcc[j], KC)
                    proj(wt, wb, KC, hTe, B_he, tiles_ext,
                         lambda c0, n, pap, pb: cp('act', ccst[:, c0:c0 + n], pap, [pb], [B_cc]))
                    wt, wb = load_w(wcu[j], KC)
                    proj(wt, wb, KC, hTe, B_he, tiles_ext,
                         lambda c0, n, pap, pb: tt('dve', ust[:, c0:c0 + n], pap, ccst[:, c0:c0 + n], ALU.mult, [pb, B_cc], [B_u]))
                    ts('dve', ust[:, 0:HALO], ust[:, 0:HALO], hflag, None, ALU.mult, None, [B_u, B_cst], [B_u])
                    ts('dve', yst[:, 2:NE], ust[:, 2:NE], cw_sb[:, 3 * j + 2:3 * j + 3], None, ALU.mult, None, [B_u, B_cw], [B_y])
                    stt('pool', yst[:, 2:NE], ust[:, 1:NE - 1], cw_sb[:, 3 * j + 1:3 * j + 2], yst[:, 2:NE], ALU.mult, ALU.add,
                        [B_u, B_cw, B_y], [B_y])
                    stt('pool', yst[:, 2:NE], ust[:, 0:NE - 2], cw_sb[:, 3 * j:3 * j + 1], yst[:, 2:NE], ALU.mult, ALU.add,
                        [B_u, B_cw, B_y], [B_y])
                    wt, wb = load_w(wcb[j], KC)
                    k = j % 2
                    proj(wt, wb, KC, hTe, B_he, tiles_ext,
                         lambda c0, n, pap, pb, k=k: tt('dve', cvst[k][:, c0:c0 + n], pap, yst[:, c0:c0 + n], ALU.mult, [pb, B_y], [B_cv[k]]))
                    dma('sp', CVT[j], cvst[k][:], [B_cv[k]], [B_CVT[j]])
                P.barrier()
            with contextlib.ExitStack() as st2:
                gst_ = [sb(f"gst{i}", [128, NE], BF16, st2) for i in range(2)]
                B_g = [Buf(), Buf()]
                GK = min(4, KC)
                wa3 = [sb(f"wa3_{i}", [128, GK, 512], F32, st2) for i in range(3)]
                B_wa3 = [Buf() for _ in range(3)]
                acc3 = sb("acc3", [128, 512], F32, st2)
                brow3 = sb("brow3", [1, 512], F32, st2)
                mrow3 = sb("mrow3", [1, 512], F32, st2)
                B_acc3, B_brow3, B_mrow3 = Buf(), Buf(), Buf()

                def p0c_gen():
                    gidx = 0
                    for c0 in range(2 * D, 6 * D, 512):
                        dma('sp', brow3[:], b_ada[:, c0:c0 + 512], (), [B_brow3])
                        for k0 in range(0, KC, GK):
                            i = gidx % 3
                            gidx += 1
                            dma('sp', wa3[i][:], w_ada[k0 * 128:(k0 + GK) * 128, c0:c0 + 512].rearrange("(k p) c -> p k c", p=128),
                                (), [B_wa3[i]])
                            for kk in range(GK):
                                kc = k0 + kk
                                if kc == 0:
                                    ts('dve', acc3[:], wa3[i][:, kk, :], c_act32[:, kc:kc + 1], None, ALU.mult, None,
                                       [B_wa3[i], B_ca32], [B_acc3])
                                else:
                                    stt('dve', acc3[:], wa3[i][:, kk, :], c_act32[:, kc:kc + 1], acc3[:], ALU.mult, ALU.add,
                                        [B_wa3[i], B_ca32, B_acc3], [B_acc3])
                                yield
                        pap, pb = psum_for(512)
                        mm(pap[0:1, :], ones32[:, 0:1], acc3[:], True, True, [B_ones32, B_acc3], [pb])
                        tt('dve', mrow3[:], pap[0:1, :], brow3[:], ALU.add, [pb, B_brow3], [B_mrow3])
                        pm, pbm = psum_for(4)
                        for q in range(4):
                            mm(pm[:, q:q + 1], mrow3[0:1, q * 128:(q + 1) * 128], one11[0:1, 0:1], True, True,
                               [B_mrow3, B_one], [pbm])
                        col = c0 // 128
                        cp('dve', modT[:, col:col + 4], pm, [pbm], [B_modT2])
                        yield
                p0c = p0c_gen()
                p0c_total = (4 * D // 512) * (KC + 1)
                p0c_per = -(-p0c_total // (2 * KC))

                def p0c_advance(nsteps):
                    for _ in range(nsteps):
                        try:
                            next(p0c)
                        except StopIteration:
                            return
                idx = 0
                for (wd_, dst, B_dst) in ((wga, SGA, B_SGA), (wgc, SGC, B_SGC)):
                    for m in range(KC):
                        wt, wb = load_w(wd_[m], KC)
                        k = idx % 2
                        idx += 1
                        proj(wt, wb, KC, hTe, B_he, tiles_ext,
                             lambda c0, n, pap, pb, k=k: act(gst_[k][:, c0:c0 + n], pap, AF.Sigmoid, [pb], [B_g[k]]))
                        dma('sp', dst[m], gst_[k][:], [B_g[k]], [B_dst[m]])
                        p0c_advance(p0c_per)
                p0c_advance(10 ** 9)
                stt('dve', a2[:], modT[:, 4 * KC:5 * KC], 1.0, vec_sb[:, KC:2 * KC], ALU.add, ALU.mult, [B_modT2, B_vec], [B_a2])
                P.barrier()

        st34 = contextlib.ExitStack()
        attT = sb("attT", [128, H, NE], BF16, st34)
        B_att = [Buf() for _ in range(H)]
        with contextlib.ExitStack() as st:
            NKT = 2 * NH // 128
            q_t = [sb(f"q_t{i}", [128, NE], BF16, st) for i in range(2)]
            k_t = [sb(f"k_t{i}", [128, 2 * NH], BF16, st) for i in range(2)]
            v_t = [sb(f"v_t{i}", [128, NKT, 129], BF16, st) for i in range(2)]
            B_q = [Buf(), Buf()]
            B_k = [Buf(), Buf()]
            B_v = [Buf(), Buf()]
            NPT = 4
            pt_t = [sb(f"pt{i}", [128, 512], BF16, st) for i in range(NPT)]
            B_pt = [Buf() for _ in range(NPT)]
            pt_ring = Ring(list(range(NPT)))
            acc = [sb(f"acc{i}", [128, 4, 129], F32, st) for i in range(2)]
            B_acc = [[Buf() for _ in range(4)] for _ in range(2)]
            gm = sb("gm", [128, 4, NBT], F32, st)
            m8 = sb("m8", [128, 4, 8], F32, st)
            sel = [sb(f"sel{i}", [128, 4, NBT], F32, st) for i in range(2)]
            B_gm, B_m8 = Buf(), Buf()
            B_sel = [Buf(), Buf()]
            rcp = sb("rcp", [128, 8], F32, st)
            B_rcp = Buf()
            onrm = [sb(f"onrm{i}", [128, 128], BF16, st) for i in range(2)]
            B_on = [Buf(), Buf()]
            S_ring = Ring([0, 1, 2, 3])
            O_ring = Ring([(4, 5), (6, 7)])
            for i in range(2):
                mset('dve', v_t[i][:, :, 128:129], 1.0, [B_v[i]])
            SCALE = 1.0 / math.sqrt(128.0)
            grp_i = 0
            p0b = None
            p0b_total = (4 * D // 512) * (KC + 1)
            p0b_state = {'done': 0, 'blocks': 0}
            blocks_total = H * sum([NB] + [NB + 2 * g + 2 for g in range(NT)])

            def p0b_advance(final=False):
                return
                p0b_state['blocks'] += 1
                target = p0b_total if final else min(p0b_total, (p0b_state['blocks'] * p0b_total) // blocks_total + 2)
                while p0b_state['done'] < target:
                    try:
                        next(p0b)
                    except StopIteration:
                        p0b_state['done'] = p0b_total
                        break
                    p0b_state['done'] += 1
            def group_info(gkind, qc0, qn):
                if gkind == "halo":
                    return [(qc0, qn)], -1, list(range(NB - 1)), [(NB - 1, ["diag_hi"])]
                g = (qc0 - HALO) // 512
                return ([(qc0 + 128 * i, 128) for i in range(4)], g, list(range(NB + 2 * g)),
                        [(NB + 2 * g, ["diag_lo", "diag_hi", "past", "past"]),
                         (NB + 2 * g + 1, ["none", "none", "diag_lo", "diag_hi"])])

            def load_head(h):
                hb = h % 2
                dma('sp', q_t[hb][:], QT[h], [B_QT[h]], [B_q[hb]])
                dma('sp', k_t[hb][:], KT[h], [B_KT[h][0], B_KT[h][1]], [B_k[hb]])
                dma('sp', v_t[hb][:, :, 0:128], VV[h].rearrange("(t p) d -> p t d", p=128), [B_VV[h][0], B_VV[h][1]], [B_v[hb]])

            def emit_sel(w, gi):
                h, gkind, qc0, qn = w
                hb = h % 2
                qh = q_t[hb]
                qtiles, g, _, _ = group_info(gkind, qc0, qn)
                gpap = ps[7][:, 320:320 + 4 * NBT]
                for i, (tq0, tn) in enumerate(qtiles):
                    mm(gpap[0:tn, i * NBT:(i + 1) * NBT], qh[:, tq0:tq0 + tn], kmTb[:, h, :], True, True,
                       [B_q[hb], B_kmb[h]], [B_ps[7]])
                for i, (tq0, tn) in enumerate(qtiles):
                    ebi = 8 if gkind == "halo" else (2 * g + i // 2)
                    tt('dve', gm[0:tn, i, :], gpap[0:tn, i * NBT:(i + 1) * NBT], EB(ebi)[0:tn, :], ALU.add,
                       [B_ps[7], B_cst], [B_gm])
                    P.add('dve', lambda e, i=i, tn=tn: e.max(out=m8[0:tn, i, :], in_=gm[0:tn, i, :]), [B_gm], [B_m8])
                    ts('dve', m8[0:tn, i, 2:3], m8[0:tn, i, 2:3], -1e29, None, ALU.max, None, [B_m8], [B_m8])
                    ts('dve', sel[gi][0:tn, i, :], gm[0:tn, i, :], m8[0:tn, i, 2:3], None, ALU.is_ge, None,
                       [B_gm, B_m8], [B_sel[gi]])

            def emit_final(w, gi):
                h, gkind, qc0, qn = w
                qtiles, _, _, _ = group_info(gkind, qc0, qn)
                for i, (tq0, tn) in enumerate(qtiles):
                    P.add('dve', lambda e, i=i, tn=tn, gi=gi: e.reciprocal(out=rcp[0:tn, gi * 4 + i:gi * 4 + i + 1], in_=acc[gi][0:tn, i, 128:129]),
                          [B_acc[gi][i]], [B_rcp])
                    k = i % 2
                    ts('dve', onrm[k][0:tn, :], acc[gi][0:tn, i, 0:128], rcp[0:tn, gi * 4 + i:gi * 4 + i + 1], None, ALU.mult, None,
                       [B_acc[gi][i], B_rcp], [B_on[k]])
                    tp16 = ps[7].bitcast(BF16)
                    tpo = tp16[:, 768 + k * 128: 768 + k * 128 + tn]
                    tr(tpo, onrm[k][0:tn, :], ident[0:tn, 0:tn], [B_on[k], B_cm], [B_ps[7]])
                    cp('act', attT[:, h, tq0:tq0 + tn], tpo, [B_ps[7]], [B_att[h]])
                if debug and gkind == "own" and qc0 == HALO + 512 * (NT - 1):
                    dma('sp', ATTd[h], attT[:, h, :], [B_att[h]], [Buf()])

            groups = [("halo", 0, HALO)] + [("own", HALO + 512 * g, 512) for g in range(NT)]
            work = [(h, gk, qc0, qn) for h in range(H) for (gk, qc0, qn) in groups]
            load_head(0)
            emit_sel(work[0], 0)
            for widx, w in enumerate(work):
                h, gkind, qc0, qn = w
                hb = h % 2
                qh, kh, vh = q_t[hb], k_t[hb], v_t[hb]
                gi = widx % 2
                if True:
                    qtiles, g, past, specials = group_info(gkind, qc0, qn)
                    nqt = len(qtiles)
                    first = [True] * nqt

                    def stage_a(j, modes):
                        pts = []
                        for kt in range(2):
                            ktile = 2 * j + kt
                            need_q = []
                            for i, md in enumerate(modes):
                                if md == "past":
                                    need_q.append(i)
                                elif md == "diag_lo" and kt == 0:
                                    need_q.append(i)
                                elif md == "diag_hi":
                                    need_q.append(i)
                            if not need_q:
                                pts.append(None)
                                continue
                            q_lo = qtiles[need_q[0]][0]
                            q_hi = qtiles[need_q[-1]][0] + qtiles[need_q[-1]][1]
                            sb_i = S_ring.next()
                            sp_ = ps[sb_i]
                            off = q_lo - qc0
                            diag_q = [i for i in need_q if (modes[i] == "diag_lo" and kt == 0) or (modes[i] == "diag_hi" and kt == 1)]
                            mm(sp_[:, off:off + (q_hi - q_lo)], kh[:, ktile * 128:(ktile + 1) * 128], qh[:, q_lo:q_hi],
                               True, len(diag_q) == 0, [B_k[hb], B_q[hb]], [B_ps[sb_i]])
                            for di, i in enumerate(diag_q):
                                tq0, tn = qtiles[i]
                                if gkind == "halo":
                                    mrhs = maskneg[:, 128 - HALO:128]
                                else:
                                    mrhs = maskneg[:, 0:tn]
                                mm(sp_[:, tq0 - qc0:tq0 - qc0 + tn], ident, mrhs, False, di == len(diag_q) - 1,
                                   [B_cm], [B_ps[sb_i]])
                            pi = pt_ring.next()
                            act(pt_t[pi][:, off:off + (q_hi - q_lo)], sp_[:, off:off + (q_hi - q_lo)], AF.Exp,
                                [B_ps[sb_i]], [B_pt[pi]], scale=SCALE)
                            pts.append((pi, need_q))
                        return pts

                    def stage_b(j, modes, pts):
                        obanks = O_ring.next()
                        for i, (tq0, tn) in enumerate(qtiles):
                            md = modes[i]
                            if md == "none":
                                continue
                            kts = [kt for kt in range(2) if pts[kt] is not None and i in pts[kt][1]]
                            ob = obanks[i // 2]
                            oc = (i % 2) * 129
                            for n_, kt in enumerate(kts):
                                pi = pts[kt][0]
                                mm(ps[ob][0:tn, oc:oc + 129], pt_t[pi][:, tq0 - qc0:tq0 - qc0 + tn], vh[:, 2 * j + kt, :],
                                   n_ == 0, n_ == len(kts) - 1, [B_pt[pi], B_v[hb]], [B_ps[ob]])
                        for i, (tq0, tn) in enumerate(qtiles):
                            md = modes[i]
                            if md == "none":
                                continue
                            ob = obanks[i // 2]
                            oc = (i % 2) * 129
                            if md == "past":
                                sc = sel[gi][0:tn, i, j:j + 1]
                                rd = [B_ps[ob], B_sel[gi]]
                            else:
                                sc = 1.0
                                rd = [B_ps[ob]]
                            if first[i]:
                                ts('dve', acc[gi][0:tn, i, :], ps[ob][0:tn, oc:oc + 129], sc, None, ALU.mult, None,
                                   rd, [B_acc[gi][i]])
                                first[i] = False
                            else:
                                stt('dve', acc[gi][0:tn, i, :], ps[ob][0:tn, oc:oc + 129], sc, acc[gi][0:tn, i, :], ALU.mult, ALU.add,
                                    rd + [B_acc[gi][i]], [B_acc[gi][i]])

                    blocks = [(j, ["past"] * nqt) for j in past] + list(specials)
                    prev = None
                    for bi, (j, modes) in enumerate(blocks):
                        pts_ = stage_a(j, modes)
                        p0b_advance()
                        if bi == 0 and widx > 0:
                            emit_final(work[widx - 1], 1 - gi)
                        if bi == 0 and gkind == "halo" and h + 1 < H:
                            load_head(h + 1)
                        if bi == min(1, len(blocks) - 1) and widx + 1 < len(work):
                            emit_sel(work[widx + 1], 1 - gi)
                        if prev is not None:
                            stage_b(*prev)
                        prev = (j, modes, pts_)
                    stage_b(*prev)
            emit_final(work[-1], (len(work) - 1) % 2)
            P.barrier()

        with contextlib.ExitStack() as st:
            cv_sb = sb("cv_sb", [128, HC, NE], BF16, st)
            B_cvs = [Buf() for _ in range(HC)]
            for j in range(HC):
                dma('sp', cv_sb[:, j, :], CVT[j], [B_CVT[j]], [B_cvs[j]])
            sg_t = [sb(f"sg_t{i}", [128, NE], BF16, st) for i in range(4)]
            B_sg = [Buf() for _ in range(4)]
            t1 = sb("t1", [128, NE], F32, st)
            B_t1 = Buf()
            t2 = sb("t2", [128, NE], F32, st)
            B_t2 = Buf()
            mst = [sb(f"mst{i}", [128, NE], BF16, st) for i in range(2)]
            B_mst = [Buf(), Buf()]
            for m in range(KC):
                k = m % 2
                dma('sp', sg_t[2 * k][:], SGA[m], [B_SGA[m]], [B_sg[2 * k]])
                dma('sp', sg_t[2 * k + 1][:], SGC[m], [B_SGC[m]], [B_sg[2 * k + 1]])
                wt, wb = load_w(wao[m], HC)
                proj(wt, wb, HC, attT, B_att, tiles_ext,
                     lambda c0, n, pap, pb, k=k: tt('dve', t1[:, c0:c0 + n], pap, sg_t[2 * k][:, c0:c0 + n], ALU.mult,
                                                   [pb, B_sg[2 * k]], [B_t1]))
                wt, wb = load_w(wco[m], HC)
                proj(wt, wb, HC, cv_sb, B_cvs, tiles_ext,
                     lambda c0, n, pap, pb, k=k: tt('dve', t2[:, c0:c0 + n], pap, sg_t[2 * k + 1][:, c0:c0 + n], ALU.mult,
                                                   [pb, B_sg[2 * k + 1]], [B_t2]))
                tt('pool', mst[k][:], t1[:], t2[:], ALU.add, [B_t1, B_t2], [B_mst[k]])
                dma('sp', MT[m], mst[k][:], [B_mst[k]], [B_MT[m]])
            P.barrier()
        st34.close()
        RSTD2 = dscr("RSTD2", [128, NE], F32)
        B_RSTD2 = Buf()
        with contextlib.ExitStack() as st:
            m_sb = sb("m_sb", [128, KC, NE], BF16, st)
            B_ms = [Buf() for _ in range(KC)]
            for kc in range(KC):
                dma('sp', m_sb[:, kc, :], MT[kc], [B_MT[kc]], [B_ms[kc]])
            xs = [sb(f"p4xs{i}", [128, NE], F32, st) for i in range(2)]
            B_xs = [Buf(), Buf()]
            accq = sb("accq", [128, NE], F32, st)
            sqt = sb("sqt", [128, NE], F32, st)
            B_accq, B_sqt = Buf(), Buf()
            mset('pool', accq[:], 0.0, [B_accq])
            xe_c = [xe[kc * 128:(kc + 1) * 128, :] for kc in range(KC)]
            for m in range(KC):
                k = m % 2
                dma('sp', xs[k][:], xe_c[m], (), [B_xs[k]])
                wt, wb = load_w(wo[m], KC)
                proj(wt, wb, KC, m_sb, B_ms, tiles_ext,
                     lambda c0, n, pap, pb, k=k, m=m: stt('dve', xs[k][:, c0:c0 + n], pap, g1[:, m:m + 1], xs[k][:, c0:c0 + n],
                                                         ALU.mult, ALU.add, [pb, B_modT2, B_xs[k]], [B_xs[k]]))
                dma('sp', XM[m], xs[k][:], [B_xs[k]], [B_XM[m]])
                tt('pool', sqt[:], xs[k][:], xs[k][:], ALU.mult, [B_xs[k]], [B_sqt])
                tt('pool', accq[:], accq[:], sqt[:], ALU.add, [B_sqt, B_accq], [B_accq])
            for (c0, n) in tiles_ext:
                pap, pb = psum_for(n)
                mm(pap, ones32[:], accq[:, c0:c0 + n], True, True, [B_ones32, B_accq], [pb])
                act(sqt[:, c0:c0 + n], pap, AF.Sqrt, [pb, B_sqt, B_cst], [B_sqt], bias=epsc, scale=1.0 / D)
            P.add('dve', lambda e: e.reciprocal(out=sqt[:], in_=sqt[:]), [B_sqt], [B_sqt])
            dma('sp', RSTD2, sqt[:], [B_sqt], [B_RSTD2])
            P.barrier()

        with contextlib.ExitStack() as st:
            h2 = sb("h2", [128, KC, NE], BF16, st)
            B_h2 = [Buf() for _ in range(KC)]
            with contextlib.ExitStack() as st2:
                xs = [sb(f"p5xs{i}", [128, NE], F32, st2) for i in range(3)]
                B_xs = [Buf(), Buf(), Buf()]
                rstd2 = sb("rstd2", [128, NE], F32, st2)
                B_rstd2 = Buf()
                dma('sp', rstd2[:], RSTD2, [B_RSTD2], [B_rstd2])
                for kc in range(KC):
                    i = kc % 3
                    dma('sp', xs[i][:], XM[kc], [B_XM[kc]], [B_xs[i]])
                    stt('dve', xs[i][:], xs[i][:], a2[:, kc:kc + 1], rstd2[:], ALU.mult, ALU.mult,
                        [B_xs[i], B_a2, B_rstd2], [B_xs[i]])
                    act(h2[:, kc, :], xs[i][:], AF.Identity, [B_xs[i], B_modT2], [B_h2[kc]], bias=s2[:, kc:kc + 1])
                P.barrier()
            gpre = sb("gpre", [128, NE], F32, st)
            gcv = sb("gcv", [128, NE], F32, st)
            vpre = sb("vpre", [128, NE], F32, st)
            vcv = gpre
            ast = [sb(f"ast{i}", [128, NH], BF16, st) for i in range(2)]
            fcw_sb = sb("fcw_sb", [128, 2 * JF * 3], F32, st)
            B_gpre, B_gcv, B_vpre, B_fcw = Buf(), Buf(), Buf(), Buf()
            B_vcv = B_gpre
            B_ast = [Buf(), Buf()]
            dma('sp', fcw_sb[:], fcw, (), [B_fcw])

            def conv3(eng, dst, src, B_dst, B_src, ch):
                w0 = fcw_sb[:, 3 * ch:3 * ch + 1]
                w1 = fcw_sb[:, 3 * ch + 1:3 * ch + 2]
                w2 = fcw_sb[:, 3 * ch + 2:3 * ch + 3]
                ts(eng, dst[:, HALO:NE], src[:, HALO:NE], w2, None, ALU.mult, None, [B_src, B_fcw], [B_dst])
                stt(eng, dst[:, HALO:NE], src[:, HALO - 1:NE - 1], w1, dst[:, HALO:NE], ALU.mult, ALU.add, [B_src, B_fcw, B_dst], [B_dst])
                stt(eng, dst[:, HALO:NE], src[:, HALO - 2:NE - 2], w0, dst[:, HALO:NE], ALU.mult, ALU.add, [B_src, B_fcw, B_dst], [B_dst])

            for j in range(JF):
                k = j % 2
                if j % max(1, JF // KC) == 0 and j // max(1, JF // KC) < KC:
                    m_ = j // max(1, JF // KC)
                    dma('pool', WDB[m_], wdn[m_], (), [B_WDB[m_]])
                wt, wb = load_w(wup[j], KC)
                proj(wt, wb, KC, h2, B_h2, tiles_ext,
                     lambda c0, n, pap, pb: cp('act', gpre[:, c0:c0 + n], pap, [pb], [B_gpre]))
                ts('dve', gpre[:, 0:HALO], gpre[:, 0:HALO], hflag, None, ALU.mult, None, [B_gpre, B_cst], [B_gpre])
                conv3('dve', gcv, gpre, B_gcv, B_gpre, j)
                act(gcv[:, HALO:NE], gcv[:, HALO:NE], AF.Silu, [B_gcv], [B_gcv])
                wt, wb = load_w(wup[JF + j], KC)
                proj(wt, wb, KC, h2, B_h2, tiles_ext,
                     lambda c0, n, pap, pb: cp('act', vpre[:, c0:c0 + n], pap, [pb], [B_vpre]))
                ts('pool', vpre[:, 0:HALO], vpre[:, 0:HALO], hflag, None, ALU.mult, None, [B_vpre, B_cst], [B_vpre])
                conv3('dve', vcv, vpre, B_vcv, B_vpre, JF + j)
                tt('pool', ast[k][:], gcv[:, HALO:NE], vcv[:, HALO:NE], ALU.mult, [B_gcv, B_vcv], [B_ast[k]])
                dma('sp', ATs[j], ast[k][:], [B_ast[k]], [B_ATs[j]])
            P.barrier()

        wst.close()
        with contextlib.ExitStack() as st:
            a_sb = sb("a_sb", [128, JF, 512], BF16, st)
            B_as = [Buf() for _ in range(JF)]
            wd_t = [sb(f"wd{i}", [128, JF, 128], BF16, st) for i in range(2)]
            B_wd = [Buf(), Buf()]
            xm_t = [sb(f"xm_t{i}", [128, 512], F32, st) for i in range(2)]
            B_xmt = [Buf(), Buf()]
            NXO = 3
            xo_t = [sb(f"xo_t{i}", [128, 512], F32, st) for i in range(NXO)]
            B_xot = [Buf() for _ in range(NXO)]
            xo_ring = Ring(list(range(NXO)))
            sq7 = sb("sq7", [128, 512], F32, st)
            acc7 = sb("acc7", [128, 512], F32, st)
            rstd3 = sb("rstd3", [128, 512], F32, st)
            ot = [sb(f"p7ot{i}", [128, 512], F32, st) for i in range(2)]
            B_sq7, B_acc7, B_r3 = Buf(), Buf(), Buf()
            B_ot = [Buf(), Buf()]

            def fin_gen(n):
                for kc in range(KC):
                    k = xo_ring.next()
                    dma('act', xo_t[k][:], XO[kc][:, n * 512:(n + 1) * 512], [B_XO[kc][n]], [B_xot[k]])
                    if kc == 0:
                        tt('dve', acc7[:], xo_t[k][:], xo_t[k][:], ALU.mult, [B_xot[k]], [B_acc7])
                    else:
                        tt('dve', sq7[:], xo_t[k][:], xo_t[k][:], ALU.mult, [B_xot[k]], [B_sq7])
                        tt('dve', acc7[:], acc7[:], sq7[:], ALU.add, [B_sq7, B_acc7], [B_acc7])
                    yield
                yield
                yield
                pap, pb = psum_for(512)
                mm(pap, ones32[:], acc7[:], True, True, [B_ones32, B_acc7], [pb])
                act(rstd3[:], pap, AF.Sqrt, [pb, B_cst], [B_r3], bias=epsc, scale=1.0 / D)
                P.add('dve', lambda e: e.reciprocal(out=rstd3[:], in_=rstd3[:]), [B_r3], [B_r3])
                yield
                for kc in range(KC):
                    k = xo_ring.next()
                    o = kc % 2
                    dma('act', xo_t[k][:], XO[kc][:, n * 512:(n + 1) * 512], [B_XO[kc][n]], [B_xot[k]])
                    stt('dve', ot[o][:], xo_t[k][:], vec_sb[:, 2 * KC + kc:2 * KC + kc + 1], rstd3[:], ALU.mult, ALU.mult,
                        [B_xot[k], B_vec, B_r3], [B_ot[o]])
                    dma('act', outT[kc * 128:(kc + 1) * 128, n * 512:(n + 1) * 512], ot[o][:], [B_ot[o]], [B_OUT])
                    yield

            def advance(gen, steps):
                for _ in range(steps):
                    try:
                        next(gen)
                    except StopIteration:
                        return

            it = 0
            fin = None
            for n in range(NT):
                for j in range(JF):
                    dma('sp', a_sb[:, j, :], ATs[j][:, n * 512:(n + 1) * 512], [B_ATs[j]], [B_as[j]])
                for m in range(KC):
                    k = it % 2
                    it += 1
                    dma('pool', wd_t[k][:], WDB[m], [B_WDB[m]], [B_wd[k]])
                    dma('sp', xm_t[k][:], XM[m][:, HALO + n * 512:HALO + (n + 1) * 512], [B_XM[m]], [B_xmt[k]])
                    pap, pb = psum_for(512)
                    for j in range(JF):
                        mm(pap, wd_t[k][:, j, :], a_sb[:, j, :], j == 0, j == JF - 1, [B_wd[k], B_as[j]], [pb])
                    stt('dve', xm_t[k][:], pap, g2[:, m:m + 1], xm_t[k][:], ALU.mult, ALU.add, [pb, B_modT2, B_xmt[k]], [B_xmt[k]])
                    dma('sp', XO[m][:, n * 512:(n + 1) * 512], xm_t[k][:], [B_xmt[k]], [B_XO[m][n]])
                    if fin is not None and m >= 1:
                        advance(fin, 4)
                if fin is not None:
                    advance(fin, 1000)
                fin = fin_gen(n)
            advance(fin, 1000)
            P.barrier()

        P.emit()
    return nc


def relay(W, mw=128):
    K, N = W.shape
    return np.ascontiguousarray(W.reshape(K // 128, 128, N // mw, mw).transpose(2, 1, 0, 3))


def vecl(v):
    return np.ascontiguousarray(v.reshape(-1, 128).T)


def host_consts(cfg, is_second):
    NB, NBT = cfg.NB, cfg.NBT
    cst = np.zeros((128, 148), np.float32)
    cst[:, 146] = EPS
    i = np.arange(32)
    cst[0:32, 0] = (ROPE_THETA ** (-(2.0 * (i % 16)) / 32.0)).astype(np.float32)
    cst[:, 1] = 1.0 if is_second else 0.0
    pv = 0.0 if is_second else -1e30
    for ob in range(NB):
        eb = np.full(16, -1e30, np.float32)
        eb[0:NB] = pv
        eb[NB:NB + ob] = 0.0
        cst[:, 2 + 16 * ob:2 + 16 * ob + 16] = eb
    eb = np.full(16, -1e30, np.float32)
    eb[0:NB - 1] = pv
    cst[:, 2 + 16 * 8:2 + 16 * 8 + 16] = eb
    cm = np.zeros((128, 416), np.float32)
    cm[:, 0:128] = np.eye(128, dtype=np.float32)
    for r in range(16):
        cm[r + 16, 128 + r] = -1.0
        cm[r, 128 + 16 + r] = 1.0
    kk = np.arange(128)[:, None]
    qq = np.arange(128)[None, :]
    cm[:, 160:288] = np.where(qq >= kk, 0.0, NEG)
    cm[:, 288:416] = 1.0
    return cst, cm


def prepare_inputs(cfg, x, c, positions, w_ada, b_ada, g_mix, w_in, conv_w, w_attn_out, w_conv_out,
                   w_o, g_ffn, w_up, ffn_conv_w, w_down, g_final):
    D, NH, NE, HALO, AW, CW = cfg.D, cfg.NH, cfg.NE, cfg.HALO, cfg.AW, cfg.CW
    x = np.asarray(x, np.float32)
    w_in0 = np.asarray(w_in[0], np.float32)
    o = 0
    parts = {}
    for name, wdt in (("wq", AW), ("wk", AW), ("wv", AW), ("wcb", CW), ("wcc", CW), ("wcu", CW), ("wga", D), ("wgc", D)):
        parts[name] = relay(w_in0[:, o:o + wdt])
        o += wdt
    shared = dict(parts)
    shared["w_ada"] = np.ascontiguousarray(np.asarray(w_ada[0], np.float32))
    shared["b_ada"] = np.ascontiguousarray(np.asarray(b_ada[0], np.float32)[None, :])
    shared["vecs"] = np.ascontiguousarray(np.concatenate([vecl(np.asarray(g_mix[0], np.float32)),
                                                          vecl(np.asarray(g_ffn[0], np.float32)),
                                                          vecl(np.asarray(g_final, np.float32))], axis=1))
    cwl = np.asarray(conv_w[0], np.float32)
    shared["cw"] = np.ascontiguousarray(cwl.T.reshape(CW // 128, 128, 3).transpose(1, 0, 2).reshape(128, -1))
    shared["wao"] = relay(np.asarray(w_attn_out[0], np.float32))
    shared["wco"] = relay(np.asarray(w_conv_out[0], np.float32))
    shared["wo"] = relay(np.asarray(w_o[0], np.float32))
    shared["wup"] = relay(np.asarray(w_up[0], np.float32))
    fl = np.asarray(ffn_conv_w[0], np.float32)
    shared["fcw"] = np.ascontiguousarray(fl.T.reshape(-1, 128, 3).transpose(1, 0, 2).reshape(128, -1))
    shared["wdn"] = relay(np.asarray(w_down[0], np.float32))
    in_maps = []
    pos = np.asarray(positions, np.int32)
    for core in range(2 * cfg.B):
        b, half = divmod(core, 2)
        xT = x[b].T
        m = dict(shared)
        if half == 0:
            xe = np.zeros((D, NE), np.float32)
            xe[:, HALO:] = xT[:, 0:NH]
            pe = np.zeros((NE,), np.int32)
            pe[HALO:] = pos[b, 0:NH]
        else:
            xe = np.ascontiguousarray(xT[:, NH - HALO:2 * NH])
            pe = pos[b, NH - HALO:2 * NH]
        m["xe"] = xe
        m["xp"] = np.ascontiguousarray(xT[:, 0:NH])
        m["cT"] = vecl(np.asarray(c[b], np.float32))
        m["pos_e"] = np.ascontiguousarray(np.broadcast_to(pe[None, :], (32, NE)))
        m["pos_p"] = np.ascontiguousarray(np.broadcast_to(pos[b, 0:NH][None, :], (32, NH)))
        cst, cm = host_consts(cfg, half == 1)
        m["cst"] = cst
        m["cmat"] = cm
        in_maps.append(m)
    return in_maps


_NC_CACHE = {}


def run(cfg, inputs, debug=False):
    key = (cfg.D, cfg.S, cfg.B, cfg.H, cfg.DFF, debug)
    if key not in _NC_CACHE:
        _NC_CACHE[key] = build(cfg, debug)
    nc = _NC_CACHE[key]
    in_maps = prepare_inputs(cfg, **inputs)
    ncores = 2 * cfg.B
    res = run_bass_kernel_spmd(nc, in_maps, core_ids=list(range(ncores)))
    out = np.empty((cfg.B, cfg.S, cfg.D), np.float32)
    for core in range(ncores):
        b, half = divmod(core, 2)
        out[b, half * cfg.NH:(half + 1) * cfg.NH, :] = np.asarray(res.results[core]["outT"]).T
    return out, res


def kernel(**inputs):
    cfg = Cfg()
    out, _ = run(cfg, inputs)
    return out
```

```python
import contextlib
import math
import os
import numpy as np
import concourse.bass as bass
import concourse.mybir as mybir
from concourse.bass_utils import run_bass_kernel_spmd

F32 = mybir.dt.float32
BF16 = mybir.dt.bfloat16
I32 = mybir.dt.int32
ALU = mybir.AluOpType
AF = mybir.ActivationFunctionType
AX = mybir.AxisListType

QUEUES = ('pe', 'act', 'dve', 'pool', 'sp')
EPOCH = 20000
NSLOT = 8


class Buf:
    __slots__ = ('w', 'r', 'rd', 'name')

    def __init__(self, name=''):
        self.w = None
        self.r = {}
        self.rd = []
        self.name = name


class Op:
    __slots__ = ('eng', 'fn', 'deps', 'sig', 'idx', 'seq', 'dma', 'slot', 'slotval')

    def __init__(self, eng, fn, dma):
        self.eng = eng
        self.fn = fn
        self.deps = []
        self.sig = False
        self.idx = -1
        self.seq = -1
        self.dma = dma
        self.slot = -1
        self.slotval = 0


class Prog:
    def __init__(self, nc):
        self.nc = nc
        self.q = {k: [] for k in QUEUES}
        self.ndma = {k: 0 for k in QUEUES}
        self.pending_dma = []

    def add(self, eng, fn, reads=(), writes=(), dma=False):
        op = Op(eng, fn, dma)
        op.idx = len(self.q[eng])
        deps = {}

        def need(d):
            if d is None or d is op:
                return
            if d.dma:
                deps[id(d)] = d
                return
            if d.eng == 'pe' and eng == 'pe' and not dma:
                return
            k = d.eng
            if k not in deps or deps[k].idx < d.idx:
                deps[k] = d

        for b in reads:
            need(b.w)
        for b in writes:
            need(b.w)
            for r in b.r.values():
                need(r)
            for r in b.rd:
                need(r)
        for b in reads:
            if dma:
                b.rd.append(op)
            else:
                b.r[eng] = op
        for b in writes:
            b.w = op
            b.r = {}
            b.rd = []
        op.deps = list(deps.values())
        for d in op.deps:
            d.sig = True
        if dma:
            i = self.ndma[eng]
            self.ndma[eng] = i + 1
            op.slot = i % NSLOT
            op.slotval = 16 * (i // NSLOT + 1)
            op.sig = True
            self.pending_dma.append(op)
        self.q[eng].append(op)
        return op

    def wait_ops(self, eng, ops):
        op = Op(eng, None, False)
        op.idx = len(self.q[eng])
        op.deps = [o for o in ops if o is not None]
        for d in op.deps:
            d.sig = True
        self.q[eng].append(op)
        return op

    def barrier(self):
        lasts = []
        for k in QUEUES:
            for o in reversed(self.q[k]):
                if not o.dma and o.fn is not None:
                    lasts.append(o)
                    break
        dmas = list(self.pending_dma)
        self.pending_dma = []
        for k in QUEUES:
            self.wait_ops(k, [o for o in lasts if o.eng != k] + dmas)

    def emit(self):
        nc = self.nc
        nsig = {}
        for k in QUEUES:
            s = 0
            for o in self.q[k]:
                if o.sig and not o.dma and o.fn is not None:
                    s += 1
                    o.seq = s
            nsig[k] = s
        with contextlib.ExitStack() as st:
            csem = {}
            for k in QUEUES:
                n = (nsig[k] + EPOCH - 1) // EPOCH
                csem[k] = [st.enter_context(nc.semaphore(f"c_{k}_{i}")) for i in range(max(n, 1))]
            dsem = {}
            for k in QUEUES:
                if self.ndma[k]:
                    dsem[k] = [st.enter_context(nc.semaphore(f"d_{k}_{i}")) for i in range(NSLOT)]
            block = st.enter_context(nc.Block())

            def run(k):
                def body(e):
                    cw = {kk: 0 for kk in QUEUES}
                    dw = {}

                    def wait_for(d):
                        if d.dma:
                            key = (d.eng, d.slot)
                            if dw.get(key, 0) >= d.slotval:
                                return
                            dw[key] = d.slotval
                            e.wait_ge(dsem[d.eng][d.slot], d.slotval)
                        else:
                            if cw[d.eng] >= d.seq:
                                return
                            cw[d.eng] = d.seq
                            ep, v = divmod(d.seq - 1, EPOCH)
                            e.wait_ge(csem[d.eng][ep], v + 1)

                    for o in self.q[k]:
                        for d in o.deps:
                            wait_for(d)
                        if o.fn is None:
                            continue
                        if o.dma:
                            if o.slotval > 16:
                                key = (k, o.slot)
                                if dw.get(key, 0) < o.slotval - 16:
                                    dw[key] = o.slotval - 16
                                    e.wait_ge(dsem[k][o.slot], o.slotval - 16)
                            ins = o.fn(e)
                            ins.then_inc(dsem[k][o.slot], 16)
                        else:
                            ins = o.fn(e)
                            if o.sig:
                                ep, v = divmod(o.seq - 1, EPOCH)
                                ins.then_inc(csem[k][ep], 1)
                return body

            if self.q['sp']:
                block.sync(run('sp'))
            if self.q['pe']:
                block.tensor(run('pe'))
            if self.q['act']:
                block.scalar(run('act'))
            if self.q['dve']:
                block.vector(run('dve'))
            if self.q['pool']:
                block.gpsimd(run('pool'))


class Cfg:
    def __init__(s, D=4096, S=4096, B=4, H=16, DFF=14336):
        s.D, s.S, s.B, s.H, s.DFF = D, S, B, H, DFF
        s.AW = H * 128
        s.CW = H * 128
        s.KC = D // 128
        s.NH = S // 2
        s.HALO = 4
        s.NE = s.NH + s.HALO
        s.NB = s.NH // 256
        s.NT = s.NH // 512
        s.JF = DFF // 128
        s.NBT = 2 * s.NB
        s.INW = 3 * s.AW + 3 * s.CW + 2 * D


ROPE_THETA = 500000.0
EPS = 1e-6
NEG = -30000.0


class Ring:
    def __init__(self, items):
        self.items = items
        self.i = 0

    def next(self):
        it = self.items[self.i % len(self.items)]
        self.i += 1
        return it


def build(cfg, debug=False):
    c = cfg
    D, KC, NH, NE, HALO, H, NB, NT, JF, NBT = c.D, c.KC, c.NH, c.NE, c.HALO, c.H, c.NB, c.NT, c.JF, c.NBT
    HC = H
    nc = bass.Bass("TRN2", target_bir_lowering=False)
    P = Prog(nc)
    skind = "ExternalOutput" if debug else "Internal"

    def din(name, shape, dt=F32):
        return nc.dram_tensor(name, list(shape), dt, kind="ExternalInput").ap()

    def dscr(name, shape, dt):
        return nc.dram_tensor(name, list(shape), dt, kind=skind).ap()

    xe = din("xe", [D, NE])
    xp = din("xp", [D, NH])
    cT = din("cT", [128, KC])
    pos_e = din("pos_e", [32, NE], I32)
    pos_p = din("pos_p", [32, NH], I32)
    cst = din("cst", [128, 148])
    cmat = din("cmat", [128, 512])
    w_ada = din("w_ada", [D, 6 * D])
    b_ada = din("b_ada", [1, 6 * D])
    vecs = din("vecs", [128, 3 * KC])
    wq = din("wq", [H, 128, KC, 128])
    wk = din("wk", [H, 128, KC, 128])
    wv = din("wv", [H, 128, KC, 128])
    wcb = din("wcb", [HC, 128, KC, 128])
    wcc = din("wcc", [HC, 128, KC, 128])
    wcu = din("wcu", [HC, 128, KC, 128])
    wga = din("wga", [KC, 128, KC, 128])
    wgc = din("wgc", [KC, 128, KC, 128])
    cw = din("cw", [128, HC * 3])
    wao = din("wao", [KC, 128, HC, 128])
    wco = din("wco", [KC, 128, HC, 128])
    wo = din("wo", [KC, 128, KC, 128])
    wup = din("wup", [2 * JF, 128, KC, 128])
    fcw = din("fcw", [128, 2 * JF * 3])
    wdn = din("wdn", [KC, 128, JF, 128])
    outT = nc.dram_tensor("outT", [D, NH], F32, kind="ExternalOutput").ap()

    QT = dscr("QT", [H, 128, NE], BF16)
    KT = dscr("KT", [H, 128, 2 * NH], BF16)
    VV = dscr("VV", [H, 2 * NH, 128], BF16)
    CVT = dscr("CVT", [HC, 128, NE], BF16)
    SGA = dscr("SGA", [KC, 128, NE], BF16)
    SGC = dscr("SGC", [KC, 128, NE], BF16)
    MT = dscr("MT", [KC, 128, NE], BF16)
    XM = dscr("XM", [KC, 128, NE], F32)
    ATs = dscr("ATs", [JF, 128, NH], BF16)
    XO = dscr("XO", [KC, 128, NH], F32)
    WDB = dscr("WDB", [KC, 128, JF, 128], BF16)
    B_WDB = [Buf() for _ in range(KC)]
    ATTd = dscr("ATTd", [H, 128, NE], BF16) if debug else None
    B_QT = [Buf() for _ in range(H)]
    B_KT = [[Buf(), Buf()] for _ in range(H)]
    B_VV = [[Buf(), Buf()] for _ in range(H)]
    B_CVT = [Buf() for _ in range(HC)]
    B_SGA = [Buf() for _ in range(KC)]
    B_SGC = [Buf() for _ in range(KC)]
    B_MT = [Buf() for _ in range(KC)]
    B_XM = [Buf() for _ in range(KC)]
    B_ATs = [Buf() for _ in range(JF)]
    B_XO = [[Buf() for _ in range(NT)] for _ in range(KC)]
    B_OUT = Buf()

    def mm(out, lhsT, rhs, start, stop, r, w):
        return P.add('pe', lambda e: e.matmul(out, lhsT, rhs, start=start, stop=stop), r, w)

    def tr(out, in_, ident, r, w):
        return P.add('pe', lambda e: e.transpose(out, in_, ident), r, w)

    def act(out, in_, func, r, w, bias=None, scale=None, accum_out=None):
        kw = {}
        if bias is not None:
            kw['bias'] = bias
        if scale is not None:
            kw['scale'] = scale
        if accum_out is not None:
            kw['accum_out'] = accum_out
        return P.add('act', lambda e: e.activation(out=out, in_=in_, func=func, **kw), r, w)

    def ts(eng, out, in0, s1, s2, op0, op1, r, w):
        if op1 is None:
            return P.add(eng, lambda e: e.tensor_scalar(out=out, in0=in0, scalar1=s1, scalar2=None, op0=op0), r, w)
        return P.add(eng, lambda e: e.tensor_scalar(out=out, in0=in0, scalar1=s1, scalar2=s2, op0=op0, op1=op1), r, w)

    def tt(eng, out, in0, in1, op, r, w):
        return P.add(eng, lambda e: e.tensor_tensor(out=out, in0=in0, in1=in1, op=op), r, w)

    def stt(eng, out, in0, scalar, in1, op0, op1, r, w):
        eng = 'dve'
        return P.add(eng, lambda e: e.scalar_tensor_tensor(out=out, in0=in0, scalar=scalar, in1=in1, op0=op0, op1=op1), r, w)

    def cp(eng, out, in_, r, w):
        if eng == 'act':
            return P.add('act', lambda e: e.copy(out=out, in_=in_), r, w)
        return P.add(eng, lambda e: e.tensor_copy(out=out, in_=in_), r, w)

    def mset(eng, ap, val, w):
        return P.add(eng, lambda e: e.memset(ap, val), (), w)

    def dma(q, out, in_, r, w):
        return P.add(q, lambda e: e.dma_start(out=out, in_=in_), r, w, dma=True)

    tiles_ext = [(0, HALO)] + [(HALO + 512 * i, 512) for i in range(NT)]
    tiles_own = [(HALO + 512 * i, 512) for i in range(NT)]
    tiles_prev = [(512 * i, 512) for i in range(NT)]

    with contextlib.ExitStack() as gst:
        def sb(name, shape, dt, st=None):
            return (st or gst).enter_context(nc.sbuf_tensor(name, list(shape), dt))

        ps = [gst.enter_context(nc.psum_tensor(f"ps{i}", [128, 512], F32)) for i in range(8)]
        B_ps = [Buf(f"ps{i}") for i in range(8)]
        main_ring = Ring(list(range(7)))
        MINI = 16
        B_mini = [B_ps[7] for _ in range(512 // MINI)]
        mini_ring = Ring(list(range(512 // MINI)))

        busy_banks = set()

        held_banks = set()

        def main_bank():
            while True:
                i = main_ring.next()
                if i not in busy_banks and i not in held_banks:
                    return i

        def psum_for(n):
            if n <= MINI:
                i = mini_ring.next()
                return ps[7][:, i * MINI:i * MINI + n], B_mini[i]
            i = main_bank()
            return ps[i][:, 0:n], B_ps[i]

        cst_sb = sb("cst_sb", [128, 148], F32)
        cm_sb = sb("cm_sb", [128, 512], BF16)
        ident = cm_sb[:, 0:128]
        permT = cm_sb[:, 128:256]
        maskneg = cm_sb[:, 256:384]
        ones = cm_sb[:, 384:512]
        vec_sb = sb("vec_sb", [128, 3 * KC], F32)
        modT = sb("modT", [128, 6 * KC], F32)
        a1 = sb("a1", [128, KC], F32)
        a2 = sb("a2", [128, KC], F32)
        kmT = sb("kmT", [128, H, NBT], F32)
        kmTb = sb("kmTb", [128, H, NBT], BF16)
        B_cst, B_cm, B_vec, B_modT, B_a1, B_a2 = Buf(), Buf(), Buf(), Buf(), Buf(), Buf()
        ones32 = sb("ones32", [128, 128], F32)
        B_ones32 = Buf()
        B_km = [Buf() for _ in range(H)]
        B_kmb = [Buf() for _ in range(H)]
        WR = 3
        B_wr = [Buf() for _ in range(WR)]
        wring = Ring(list(range(WR)))

        invf = cst_sb[0:32, 0:1]
        hflag = cst_sb[:, 1:2]
        epsc = cst_sb[:, 146:147]

        def EB(idx):
            return cst_sb[:, 2 + 16 * idx: 2 + 16 * idx + NBT]

        s1 = modT[:, 0 * KC:1 * KC]
        g1 = modT[:, 2 * KC:3 * KC]
        s2 = modT[:, 3 * KC:4 * KC]
        g2 = modT[:, 5 * KC:6 * KC]

        mset('dve', ones32[:], 1.0, [B_ones32])
        dma('sp', cst_sb[:], cst, (), [B_cst])
        dma('pool', cm_sb[:], cmat, (), [B_cm])
        dma('sp', vec_sb[:], vecs, (), [B_vec])

        w_items = []
        for h in range(H):
            w_items += [(wk[h], KC), (wv[h], KC)]
        for h in range(H):
            w_items += [(wq[h], KC), (wk[h], KC), (wv[h], KC)]
        for j in range(HC):
            w_items += [(wcc[j], KC), (wcu[j], KC), (wcb[j], KC)]
        for m in range(KC):
            w_items.append((wga[m], KC))
        for m in range(KC):
            w_items.append((wgc[m], KC))
        for m in range(KC):
            w_items += [(wao[m], HC), (wco[m], HC)]
        for m in range(KC):
            w_items.append((wo[m], KC))
        for j in range(JF):
            w_items += [(wup[j], KC), (wup[JF + j], KC)]
        w_state = {'issued': 0, 'taken': 0, 'slot': {}}

        def load_w(wdram_m, kcn):
            t = w_state['taken']
            assert w_items[t][0] is wdram_m or True
            ahead = int(os.environ.get("KPREF", "3"))
            while w_state['issued'] < min(t + ahead, len(w_items)):
                ii = w_state['issued']
                ap_, kn_ = w_items[ii]
                i = wring.next()
                dma('pool', wring_t[i][:, 0:kn_, :], ap_, (), [B_wr[i]])
                w_state['slot'][ii] = i
                w_state['issued'] = ii + 1
            i = w_state['slot'][t]
            assert w_items[t][1] == kcn
            w_state['taken'] = t + 1
            return wring_t[i], B_wr[i]

        pending = []

        def fill_step():
            while pending:
                try:
                    next(pending[0])
                    return
                except StopIteration:
                    pending.pop(0)

        def after_proj(post):
            if post is None:
                while pending:
                    fill_step()
                return
            pending.append(post())

        def proj(wt, wb, kcn, src, srcb, tiles, evac):
            regs = [psum_for(n) for (_, n) in tiles]
            order = sorted(range(len(tiles)), key=lambda i: (tiles[i][1] <= MINI, i))
            for (pap_, pb_) in regs:
                for bi_ in range(7):
                    if pb_ is B_ps[bi_]:
                        busy_banks.add(bi_)
            for kc in range(kcn):
                for i in order:
                    (c0, n), (pap, pb) = tiles[i], regs[i]
                    mm(pap, wt[:, kc, :], src[:, kc, c0:c0 + n], kc == 0, kc == kcn - 1, [wb, srcb[kc]], [pb])
                if kc >= 2:
                    fill_step()
            while pending:
                fill_step()
            busy_banks.clear()
            for i in ([order[-1]] + order[:-1]) if tiles[order[-1]][1] <= MINI else order:
                (c0, n), (pap, pb) = tiles[i], regs[i]
                evac(c0, n, pap, pb)

        c_act = sb("c_act", [128, KC], BF16)
        one11 = sb("one11", [1, 1], F32)
        B_wa2 = [Buf() for _ in range(3)]
        B_brow2, B_mrow2, B_modT2 = Buf(), Buf(), Buf()
        p0b_bufs = {}
        B_ca, B_one = Buf(), Buf()
        wst = contextlib.ExitStack()
        wring_t = [sb(f"wr{i}", [128, max(KC, HC), 128], BF16, wst) for i in range(WR)]
        with contextlib.ExitStack() as st:
            c_sb = sb("c_sb", [128, KC], F32, st)
            CWID = 2048
            wa_t = [sb(f"wa{i}", [128, CWID], BF16, st) for i in range(3)]
            B_wa = [Buf() for _ in range(3)]
            wa_ring = Ring([0, 1, 2])
            brow = sb("brow", [1, CWID], F32, st)
            mrow = sb("mrow", [1, CWID], F32, st)
            B_c, B_brow, B_mrow = Buf(), Buf(), Buf()
            dma('sp', c_sb[:], cT, (), [B_c])
            act(c_act[:], c_sb[:], AF.Silu, [B_c], [B_ca])
            mset('dve', one11[:], 1.0, [B_one])
            ncol = 2 * D
            modps = ps[7][:, 0:2 * KC]
            B_modps = B_ps[7]
            for c0 in range(0, ncol, CWID):
                wdt = min(CWID, ncol - c0)
                nb_ = (wdt + 511) // 512
                banks = [main_ring.next() for _ in range(nb_)]
                dma('sp', brow[:, 0:wdt], b_ada[:, c0:c0 + wdt], (), [B_brow])
                for kc in range(KC):
                    i = wa_ring.next()
                    dma('pool', wa_t[i][:, 0:wdt], w_ada[kc * 128:(kc + 1) * 128, c0:c0 + wdt], (), [B_wa[i]])
                    for bi, bk in enumerate(banks):
                        n = min(512, wdt - bi * 512)
                        mm(ps[bk][0:1, 0:n], c_act[:, kc:kc + 1], wa_t[i][:, bi * 512:bi * 512 + n],
                           kc == 0, kc == KC - 1, [B_ca, B_wa[i]], [B_ps[bk]])
                for bi, bk in enumerate(banks):
                    n = min(512, wdt - bi * 512)
                    tt('dve', mrow[:, bi * 512:bi * 512 + n], ps[bk][0:1, 0:n], brow[:, bi * 512:bi * 512 + n], ALU.add,
                       [B_ps[bk], B_brow], [B_mrow])
                for j in range(wdt // 128):
                    col = (c0 // 128) + j
                    mm(modps[:, col:col + 1], mrow[0:1, j * 128:(j + 1) * 128], one11[0:1, 0:1], True, True,
                       [B_mrow, B_one], [B_modps])
            cp('dve', modT[:, 0:2 * KC], modps, [B_modps], [B_modT])
            stt('dve', a1[:], modT[:, 1 * KC:2 * KC], 1.0, vec_sb[:, 0:KC], ALU.add, ALU.mult, [B_modT, B_vec], [B_a1])
            P.barrier()

        def p0b_gen():
            wa2_t, brow2, mrow2 = p0b_bufs['wa2'], p0b_bufs['brow2'], p0b_bufs['mrow2']
            G = min(4, KC)
            grp = [(c0, k0) for c0 in range(2 * D, 6 * D, 512) for k0 in range(0, KC, G)]
            NR = len(wa2_t)

            def issue(gidx):
                if gidx < len(grp):
                    c0_, k0_ = grp[gidx]
                    i_ = gidx % NR
                    dma('pool', wa2_t[i_][:], w_ada[k0_ * 128:(k0_ + G) * 128, c0_:c0_ + 512].rearrange("(k p) c -> p k c", p=128),
                        (), [B_wa2[i_]])
            for g0 in range(NR - 1):
                issue(g0)
            for gidx, (c0, k0) in enumerate(grp):
                issue(gidx + NR - 1)
                i = gidx % NR
                if k0 == 0:
                    dma('sp', brow2[:], b_ada[:, c0:c0 + 512], (), [B_brow2])
                for kk in range(G):
                    kc = k0 + kk
                    mm(ps[3][0:1, 0:512], c_act[:, kc:kc + 1], wa2_t[i][:, kk, :], kc == 0, kc == KC - 1, [B_ca, B_wa2[i]], [B_ps[3]])
                    yield
                if k0 + G >= KC:
                    tt('dve', mrow2[:], ps[3][0:1, 0:512], brow2[:], ALU.add, [B_ps[3], B_brow2], [B_mrow2])
                    for q in range(4):
                        mm(ps[7][:, 260 + q:261 + q], mrow2[0:1, q * 128:(q + 1) * 128], one11[0:1, 0:1], True, True,
                           [B_mrow2, B_one], [B_ps[7]])
                    col = c0 // 128
                    cp('dve', modT[:, col:col + 4], ps[7][:, 260:264], [B_ps[7]], [B_modT2])
                    yield

        def make_tables(st, pos_dram, N, name, tiles):
            cosT = sb(name + "cos", [32, N], F32, st)
            sinT = sb(name + "sin", [32, N], F32, st)
            B_cos, B_sin = Buf(), Buf()
            C1 = 6.28125
            C2 = 2 * math.pi - C1
            PI_LO = 3.1415925
            with contextlib.ExitStack() as st2:
                pi_ = sb(name + "pi", [32, 512], I32, st2)
                ang = sb(name + "ang", [32, 512], F32, st2)
                aa = sb(name + "aa", [32, 512], F32, st2)
                tq = sb(name + "tq", [32, 512], F32, st2)
                B_pi, B_ang, B_aa, B_tq = Buf(), Buf(), Buf(), Buf()
                for (c0, n) in tiles:
                    dma('sp', pi_[:, 0:n], pos_dram[:, c0:c0 + n], (), [B_pi])
                    cp('dve', ang[:, 0:n], pi_[:, 0:n], [B_pi], [B_ang])
                    ts('dve', ang[:, 0:n], ang[:, 0:n], invf, None, ALU.mult, None, [B_ang, B_cst], [B_ang])
                    for (dst, B_dst, shift) in ((sinT, B_sin, 0.0), (cosT, B_cos, math.pi / 2)):
                        ts('dve', aa[:, 0:n], ang[:, 0:n], shift, None, ALU.add, None, [B_ang], [B_aa])
                        ts('dve', tq[:, 0:n], aa[:, 0:n], 1.0 / (2 * math.pi), None, ALU.mult, None, [B_aa], [B_tq])
                        cp('dve', pi_[:, 0:n], tq[:, 0:n], [B_tq], [B_pi])
                        cp('dve', tq[:, 0:n], pi_[:, 0:n], [B_pi], [B_tq])
                        stt('dve', aa[:, 0:n], tq[:, 0:n], -C1, aa[:, 0:n], ALU.mult, ALU.add, [B_tq, B_aa], [B_aa])
                        stt('dve', aa[:, 0:n], tq[:, 0:n], -C2, aa[:, 0:n], ALU.mult, ALU.add, [B_tq, B_aa], [B_aa])
                        ts('dve', aa[:, 0:n], aa[:, 0:n], -PI_LO, PI_LO, ALU.max, ALU.min, [B_aa], [B_aa])
                        act(dst[:, c0:c0 + n], aa[:, 0:n], AF.Sin, [B_aa], [B_dst])
                P.barrier()
            return cosT, sinT, B_cos, B_sin

        def make_h(st, x_dram, N, tiles, a_vec, B_a, s_vec, name, src_is_scratch_bufs=None, rstd_out=None):
            hT = sb(name + "hT", [128, KC, N], BF16, st)
            B_h = [Buf() for _ in range(KC)]
            with contextlib.ExitStack() as st2:
                xs = [sb(name + f"xs{i}", [128, N], F32, st2) for i in range(3)]
                B_xs = [Buf(), Buf(), Buf()]
                sq = [sb(name + f"sq{i}", [128, N], BF16, st2) for i in range(2)]
                B_sq = [Buf(), Buf()]
                rstd = sb(name + "rstd", [128, N], F32, st2)
                B_rstd = Buf()
                regs = [psum_for(n) for (_, n) in tiles]
                for kc in range(KC):
                    i = kc % 3
                    j2 = kc % 2
                    rb = [src_is_scratch_bufs[kc]] if src_is_scratch_bufs else []
                    dma('sp' if kc % 2 == 0 else 'pool', xs[i][:], x_dram[kc], rb, [B_xs[i]])
                    act(sq[j2][:], xs[i][:], AF.Square, [B_xs[i]], [B_sq[j2]])
                    for (c0, n), (pap, pb) in zip(tiles, regs):
                        mm(pap, ones, sq[j2][:, c0:c0 + n], kc == 0, kc == KC - 1, [B_cm, B_sq[j2]], [pb])
                for (c0, n), (pap, pb) in zip(tiles, regs):
                    act(rstd[:, c0:c0 + n], pap, AF.Sqrt, [pb, B_cst], [B_rstd], bias=epsc, scale=1.0 / D)
                P.add('dve', lambda e: e.reciprocal(out=rstd[:], in_=rstd[:]), [B_rstd], [B_rstd])
                for kc in range(KC):
                    i = (kc + KC) % 3
                    rb = [src_is_scratch_bufs[kc]] if src_is_scratch_bufs else []
                    dma('sp' if kc % 2 == 0 else 'pool', xs[i][:], x_dram[kc], rb, [B_xs[i]])
                    stt('dve', xs[i][:], xs[i][:], a_vec[:, kc:kc + 1], rstd[:], ALU.mult, ALU.mult,
                        [B_xs[i], B_a, B_rstd], [B_xs[i]])
                    act(hT[:, kc, :], xs[i][:], AF.Identity, [B_xs[i], B_modT], [B_h[kc]], bias=s_vec[:, kc:kc + 1])
                P.barrier()
            return hT, B_h

        def rope_gen(stg, B_st, tiles, src_c0, cosT, sinT, B_cos, B_sin, rt, B_rt):
            for ti, (c0, n) in enumerate(tiles):
                pap, pb = psum_for(n)
                d0 = c0 - src_c0
                mm(pap, permT, stg[:, d0:d0 + n], True, True, [B_cm] + B_st(ti), [pb])
                k = ti % 2
                cp('act', rt[k][:, 0:n], pap[0:32, :], [pb], [B_rt[k]])
                tt('dve', rt[k][:, 0:n], rt[k][:, 0:n], sinT[:, c0:c0 + n], ALU.mult, [B_rt[k], B_sin], [B_rt[k]])
                tt('dve', rt[k + 2][:, 0:n], stg[0:32, d0:d0 + n], cosT[:, c0:c0 + n], ALU.mult, B_st(ti) + [B_cos], [B_rt[k + 2]])
                tt('dve', stg[0:32, d0:d0 + n], rt[k][:, 0:n], rt[k + 2][:, 0:n], ALU.add, [B_rt[k], B_rt[k + 2]], B_st(ti))
                yield

        def head_kv(st_tiles, hT, B_h, cosT, sinT, B_cos, B_sin, h, col0, half, stg_ring, vt_ring, rt, B_rt, tiles, src_c0):
            wt, wb = load_w(wk[h], KC)
            kst, B_kst = stg_ring.next()
            tix = {c0: ti for ti, (c0, n) in enumerate(tiles)}

            def evk(c0, n, pap, pb):
                cp('act', kst[:, c0 - src_c0:c0 - src_c0 + n], pap, [pb], [B_kst[tix[c0]], B_kst[tix[c0] + 1]])
            proj(wt, wb, KC, hT, B_h, tiles, evk)

            def post_k():
                yield from rope_gen(kst, lambda ti: [B_kst[ti], B_kst[ti + 1]], tiles, src_c0, cosT, sinT, B_cos, B_sin, rt, B_rt)
                nt_ = 5
                P.add('dve', lambda e: e.tensor_reduce(out=kmT[:, h, half * NB:(half + 1) * NB],
                                                       in_=kst[:, 0:NH].rearrange("p (b k) -> p b k", k=256),
                                                       axis=AX.X, op=ALU.add), B_kst[0:nt_], [B_km[h]])
                ts('dve', kmT[:, h, half * NB:(half + 1) * NB], kmT[:, h, half * NB:(half + 1) * NB], 1.0 / 256, None,
                   ALU.mult, None, [B_km[h]], [B_km[h]])
                dma('sp', KT[h][:, col0:col0 + NH], kst[:, 0:NH], B_kst[0:nt_], [B_KT[h][half]])
                if half == 1:
                    cp('dve', kmTb[:, h, :], kmT[:, h, :], [B_km[h]], [B_kmb[h]])
                yield
            after_proj(post_k)
            wt, wb = load_w(wv[h], KC)
            vst, B_vst = stg_ring.next()

            def evv(c0, n, pap, pb):
                cp('act', vst[:, c0 - src_c0:c0 - src_c0 + n], pap, [pb], [B_vst[tix[c0]], B_vst[tix[c0] + 1]])
            proj(wt, wb, KC, hT, B_h, tiles, evv)

            def post_v():
                vt, B_vt = vt_ring.next()
                for g in range(NH // 1024):
                    i = main_bank()
                    held_banks.add(i)
                    pb16 = ps[i].bitcast(BF16)
                    for t in range(8):
                        tok = g * 1024 + t * 128
                        tr(pb16[:, t * 128:(t + 1) * 128], vst[:, tok:tok + 128], ident, [B_vst[tok // 512], B_vst[tok // 512 + 1], B_cm], [B_ps[i]])
                        if t % 2 == 1 and t < 7:
                            yield
                    cp('dve', vt[:, g * 8:(g + 1) * 8, :], pb16[:, 0:1024].rearrange("p (t d) -> p t d", d=128), [B_ps[i]], [B_vt])
                    held_banks.discard(i)
                    yield
                dma('sp', VV[h][col0:col0 + NH, :].rearrange("(t p) d -> p t d", p=128), vt[:], [B_vt], [B_VV[h][half]])
                yield
            after_proj(post_v)

        with contextlib.ExitStack() as st:
            xp_c = [xp[kc * 128:(kc + 1) * 128, :] for kc in range(KC)]
            hTp, B_hp = make_h(st, xp_c, NH, tiles_prev, a1, B_a1, s1, "p1")
            cosP, sinP, B_cosP, B_sinP = make_tables(st, pos_p, NH, "tp", tiles_prev)
            stg = [sb(f"p1stg{i}", [128, NH], BF16, st) for i in range(2)]
            stg_ring = Ring([(stg[i], [Buf() for _ in range(5)]) for i in range(2)])
            vts = [sb(f"p1vt{i}", [128, NH // 128, 128], BF16, st) for i in range(2)]
            vt_ring = Ring([(vts[i], Buf()) for i in range(2)])
            rt = [sb(f"p1rt{i}", [32, 512], F32, st) for i in range(4)]
            B_rt = [Buf() for _ in range(4)]
            for h in range(H):
                head_kv(st, hTp, B_hp, cosP, sinP, B_cosP, B_sinP, h, 0, 0, stg_ring, vt_ring, rt, B_rt, tiles_prev, 0)
            after_proj(None)
            P.barrier()

        with contextlib.ExitStack() as st:
            xe_c = [xe[kc * 128:(kc + 1) * 128, :] for kc in range(KC)]
            hTe, B_he = make_h(st, xe_c, NE, tiles_ext, a1, B_a1, s1, "p2")
            with contextlib.ExitStack() as st2:
                cosE, sinE, B_cosE, B_sinE = make_tables(st2, pos_e, NE, "te", tiles_ext)
                stg = [sb(f"p2stg{i}", [128, NE], BF16, st2) for i in range(2)]
                stg_ring = Ring([(stg[i], [Buf() for _ in range(5)]) for i in range(2)])
                vts = [sb(f"p2vt{i}", [128, NH // 128, 128], BF16, st2) for i in range(2)]
                vt_ring = Ring([(vts[i], Buf()) for i in range(2)])
                rt = [sb(f"p2rt{i}", [32, 512], F32, st2) for i in range(4)]
                B_rt = [Buf() for _ in range(4)]
                for h in range(H):
                    wt, wb = load_w(wq[h], KC)
                    qst, B_qst = stg_ring.next()

                    tixe = {c0: ti for ti, (c0, n) in enumerate(tiles_ext)}

                    def evq(c0, n, pap, pb, qst=qst, B_qst=B_qst):
                        cp('act', qst[:, c0:c0 + n], pap, [pb], [B_qst[tixe[c0]]])
                    proj(wt, wb, KC, hTe, B_he, tiles_ext, evq)

                    def post_q(qst=qst, B_qst=B_qst, h=h):
                        yield from rope_gen(qst, lambda ti: [B_qst[ti]], tiles_ext, 0, cosE, sinE, B_cosE, B_sinE, rt, B_rt)
                        dma('sp', QT[h], qst[:], B_qst[0:5], [B_QT[h]])
                        yield
                    after_proj(post_q)
                    head_kv(st2, hTe, B_he, cosE, sinE, B_cosE, B_sinE, h, NH, 1, stg_ring, vt_ring, rt, B_rt, tiles_own, HALO)
                after_proj(None)
                P.barrier()
            with contextlib.ExitStack() as st2:
                ccst = sb("ccst", [128, NE], F32, st2)
                ust = sb("ust", [128, NE], F32, st2)
                yst = sb("yst", [128, NE], F32, st2)
                cvst = [sb(f"cvst{i}", [128, NE], BF16, st2) for i in range(2)]
                cw_sb = sb("cw_sb", [128, HC * 3], F32, st2)
                B_cc, B_u, B_y, B_cw = Buf(), Buf(), Buf(), Buf()
                B_cv = [Buf(), Buf()]
                dma('sp', cw_sb[:], cw, (), [B_cw])
                mset('dve', yst[:], 0.0, [B_y])
                for j in range(HC):
                    wt, wb = load_w(wcc[j], KC)
                    proj(wt, wb, KC, hTe, B_he, tiles_ext,
                         lambda c0, n, pap, pb: cp('act', ccst[:, c0:c0 + n], pap, [pb], [B_cc]))
                    wt, wb = load_w(wcu[j], KC)
                    proj(wt, wb, KC, hTe, B_he, tiles_ext,
                         lambda c0, n, pap, pb: tt('dve', ust[:, c0:c0 + n], pap, ccst[:, c0:c0 + n], ALU.mult, [pb, B_cc], [B_u]))
                    ts('dve', ust[:, 0:HALO], ust[:, 0:HALO], hflag, None, ALU.mult, None, [B_u, B_cst], [B_u])
                    ts('dve', yst[:, 2:NE], ust[:, 2:NE], cw_sb[:, 3 * j + 2:3 * j + 3], None, ALU.mult, None, [B_u, B_cw], [B_y])
                    stt('pool', yst[:, 2:NE], ust[:, 1:NE - 1], cw_sb[:, 3 * j + 1:3 * j + 2], yst[:, 2:NE], ALU.mult, ALU.add,
                        [B_u, B_cw, B_y], [B_y])
                    stt('pool', yst[:, 2:NE], ust[:, 0:NE - 2], cw_sb[:, 3 * j:3 * j + 1], yst[:, 2:NE], ALU.mult, ALU.add,
                        [B_u, B_cw, B_y], [B_y])
                    wt, wb = load_w(wcb[j], KC)
                    k = j % 2
                    proj(wt, wb, KC, hTe, B_he, tiles_ext,
                         lambda c0, n, pap, pb, k=k: tt('dve', cvst[k][:, c0:c0 + n], pap, yst[:, c0:c0 + n], ALU.mult, [pb, B_y], [B_cv[k]]))
                    dma('sp', CVT[j], cvst[k][:], [B_cv[k]], [B_CVT[j]])
                P.barrier()
            with contextlib.ExitStack() as st2:
                gst_ = [sb(f"gst{i}", [128, NE], BF16, st2) for i in range(2)]
                B_g = [Buf(), Buf()]
                idx = 0
                for (wd_, dst, B_dst) in ((wga, SGA, B_SGA), (wgc, SGC, B_SGC)):
                    for m in range(KC):
                        wt, wb = load_w(wd_[m], KC)
                        k = idx % 2
                        idx += 1
                        proj(wt, wb, KC, hTe, B_he, tiles_ext,
                             lambda c0, n, pap, pb, k=k: act(gst_[k][:, c0:c0 + n], pap, AF.Sigmoid, [pb], [B_g[k]]))
                        dma('sp', dst[m], gst_[k][:], [B_g[k]], [B_dst[m]])
                P.barrier()

        st34 = contextlib.ExitStack()
        attT = sb("attT", [128, H, NE], BF16, st34)
        B_att = [Buf() for _ in range(H)]
        with contextlib.ExitStack() as st:
            NKT = 2 * NH // 128
            q_t = [sb(f"q_t{i}", [128, NE], BF16, st) for i in range(2)]
            k_t = [sb(f"k_t{i}", [128, 2 * NH], BF16, st) for i in range(2)]
            v_t = [sb(f"v_t{i}", [128, NKT, 129], BF16, st) for i in range(2)]
            B_q = [Buf(), Buf()]
            B_k = [Buf(), Buf()]
            B_v = [Buf(), Buf()]
            NPT = 4
            pt_t = [sb(f"pt{i}", [128, 512], BF16, st) for i in range(NPT)]
            B_pt = [Buf() for _ in range(NPT)]
            pt_ring = Ring(list(range(NPT)))
            acc = [sb(f"acc{i}", [128, 4, 129], F32, st) for i in range(2)]
            B_acc = [[Buf() for _ in range(4)] for _ in range(2)]
            gm = sb("gm", [128, 4, NBT], F32, st)
            m8 = sb("m8", [128, 4, 8], F32, st)
            sel = [sb(f"sel{i}", [128, 4, NBT], F32, st) for i in range(2)]
            B_gm, B_m8 = Buf(), Buf()
            B_sel = [Buf(), Buf()]
            rcp = sb("rcp", [128, 8], F32, st)
            B_rcp = Buf()
            onrm = [sb(f"onrm{i}", [128, 128], BF16, st) for i in range(2)]
            B_on = [Buf(), Buf()]
            S_ring = Ring([0, 1, 2])
            O_ring = Ring([(4, 5), (6, 7)])
            for i in range(2):
                mset('dve', v_t[i][:, :, 128:129], 1.0, [B_v[i]])
            SCALE = 1.0 / math.sqrt(128.0)
            grp_i = 0
            p0b_bufs['wa2'] = [sb(f"wa2_{i}", [128, min(4, KC), 512], BF16, st) for i in range(3)]
            p0b_bufs['brow2'] = sb("brow2", [1, 512], F32, st)
            p0b_bufs['mrow2'] = sb("mrow2", [1, 512], F32, st)
            p0b = p0b_gen()
            p0b_total = (4 * D // 512) * (KC + 1)
            p0b_state = {'done': 0, 'blocks': 0}
            blocks_total = H * sum([NB] + [NB + 2 * g + 2 for g in range(NT)])

            def p0b_advance(final=False):
                p0b_state['blocks'] += 1
                target = p0b_total if final else min(p0b_total, (p0b_state['blocks'] * p0b_total) // blocks_total + 2)
                while p0b_state['done'] < target:
                    try:
                        next(p0b)
                    except StopIteration:
                        p0b_state['done'] = p0b_total
                        break
                    p0b_state['done'] += 1
            def group_info(gkind, qc0, qn):
                if gkind == "halo":
                    return [(qc0, qn)], -1, list(range(NB - 1)), [(NB - 1, ["diag_hi"])]
                g = (qc0 - HALO) // 512
                return ([(qc0 + 128 * i, 128) for i in range(4)], g, list(range(NB + 2 * g)),
                        [(NB + 2 * g, ["diag_lo", "diag_hi", "past", "past"]),
                         (NB + 2 * g + 1, ["none", "none", "diag_lo", "diag_hi"])])

            def load_head(h):
                hb = h % 2
                dma('sp', q_t[hb][:], QT[h], [B_QT[h]], [B_q[hb]])
                dma('sp', k_t[hb][:], KT[h], [B_KT[h][0], B_KT[h][1]], [B_k[hb]])
                dma('sp', v_t[hb][:, :, 0:128], VV[h].rearrange("(t p) d -> p t d", p=128), [B_VV[h][0], B_VV[h][1]], [B_v[hb]])

            def emit_sel(w, gi):
                h, gkind, qc0, qn = w
                hb = h % 2
                qh = q_t[hb]
                qtiles, g, _, _ = group_info(gkind, qc0, qn)
                gpap = ps[7][:, 320:320 + 4 * NBT]
                for i, (tq0, tn) in enumerate(qtiles):
                    mm(gpap[0:tn, i * NBT:(i + 1) * NBT], qh[:, tq0:tq0 + tn], kmTb[:, h, :], True, True,
                       [B_q[hb], B_kmb[h]], [B_ps[7]])
                for i, (tq0, tn) in enumerate(qtiles):
                    ebi = 8 if gkind == "halo" else (2 * g + i // 2)
                    tt('dve', gm[0:tn, i, :], gpap[0:tn, i * NBT:(i + 1) * NBT], EB(ebi)[0:tn, :], ALU.add,
                       [B_ps[7], B_cst], [B_gm])
                    P.add('dve', lambda e, i=i, tn=tn: e.max(out=m8[0:tn, i, :], in_=gm[0:tn, i, :]), [B_gm], [B_m8])
                    ts('dve', m8[0:tn, i, 2:3], m8[0:tn, i, 2:3], -1e29, None, ALU.max, None, [B_m8], [B_m8])
                    ts('dve', sel[gi][0:tn, i, :], gm[0:tn, i, :], m8[0:tn, i, 2:3], None, ALU.is_ge, None,
                       [B_gm, B_m8], [B_sel[gi]])

            def emit_final(w, gi):
                h, gkind, qc0, qn = w
                qtiles, _, _, _ = group_info(gkind, qc0, qn)
                for i, (tq0, tn) in enumerate(qtiles):
                    P.add('dve', lambda e, i=i, tn=tn, gi=gi: e.reciprocal(out=rcp[0:tn, gi * 4 + i:gi * 4 + i + 1], in_=acc[gi][0:tn, i, 128:129]),
                          [B_acc[gi][i]], [B_rcp])
                    k = i % 2
                    ts('dve', onrm[k][0:tn, :], acc[gi][0:tn, i, 0:128], rcp[0:tn, gi * 4 + i:gi * 4 + i + 1], None, ALU.mult, None,
                       [B_acc[gi][i], B_rcp], [B_on[k]])
                    tp16 = ps[7].bitcast(BF16)
                    tpo = tp16[:, 768 + k * 128: 768 + k * 128 + tn]
                    tr(tpo, onrm[k][0:tn, :], ident[0:tn, 0:tn], [B_on[k], B_cm], [B_ps[7]])
                    cp('act', attT[:, h, tq0:tq0 + tn], tpo, [B_ps[7]], [B_att[h]])
                if debug and gkind == "own" and qc0 == HALO + 512 * (NT - 1):
                    dma('sp', ATTd[h], attT[:, h, :], [B_att[h]], [Buf()])

            groups = [("halo", 0, HALO)] + [("own", HALO + 512 * g, 512) for g in range(NT)]
            work = [(h, gk, qc0, qn) for h in range(H) for (gk, qc0, qn) in groups]
            load_head(0)
            emit_sel(work[0], 0)
            for widx, w in enumerate(work):
                h, gkind, qc0, qn = w
                hb = h % 2
                qh, kh, vh = q_t[hb], k_t[hb], v_t[hb]
                gi = widx % 2
                if True:
                    qtiles, g, past, specials = group_info(gkind, qc0, qn)
                    nqt = len(qtiles)
                    first = [True] * nqt

                    def stage_a(j, modes):
                        pts = []
                        for kt in range(2):
                            ktile = 2 * j + kt
                            need_q = []
                            for i, md in enumerate(modes):
                                if md == "past":
                                    need_q.append(i)
                                elif md == "diag_lo" and kt == 0:
                                    need_q.append(i)
                                elif md == "diag_hi":
                                    need_q.append(i)
                            if not need_q:
                                pts.append(None)
                                continue
                            q_lo = qtiles[need_q[0]][0]
                            q_hi = qtiles[need_q[-1]][0] + qtiles[need_q[-1]][1]
                            sb_i = S_ring.next()
                            sp_ = ps[sb_i]
                            off = q_lo - qc0
                            diag_q = [i for i in need_q if (modes[i] == "diag_lo" and kt == 0) or (modes[i] == "diag_hi" and kt == 1)]
                            mm(sp_[:, off:off + (q_hi - q_lo)], kh[:, ktile * 128:(ktile + 1) * 128], qh[:, q_lo:q_hi],
                               True, len(diag_q) == 0, [B_k[hb], B_q[hb]], [B_ps[sb_i]])
                            for di, i in enumerate(diag_q):
                                tq0, tn = qtiles[i]
                                if gkind == "halo":
                                    mrhs = maskneg[:, 128 - HALO:128]
                                else:
                                    mrhs = maskneg[:, 0:tn]
                                mm(sp_[:, tq0 - qc0:tq0 - qc0 + tn], ident, mrhs, False, di == len(diag_q) - 1,
                                   [B_cm], [B_ps[sb_i]])
                            pi = pt_ring.next()
                            act(pt_t[pi][:, off:off + (q_hi - q_lo)], sp_[:, off:off + (q_hi - q_lo)], AF.Exp,
                                [B_ps[sb_i]], [B_pt[pi]], scale=SCALE)
                            pts.append((pi, need_q))
                        return pts

                    def stage_b(j, modes, pts):
                        obanks = O_ring.next()
                        for i, (tq0, tn) in enumerate(qtiles):
                            md = modes[i]
                            if md == "none":
                                continue
                            kts = [kt for kt in range(2) if pts[kt] is not None and i in pts[kt][1]]
                            ob = obanks[i // 2]
                            oc = (i % 2) * 129
                            for n_, kt in enumerate(kts):
                                pi = pts[kt][0]
                                mm(ps[ob][0:tn, oc:oc + 129], pt_t[pi][:, tq0 - qc0:tq0 - qc0 + tn], vh[:, 2 * j + kt, :],
                                   n_ == 0, n_ == len(kts) - 1, [B_pt[pi], B_v[hb]], [B_ps[ob]])
                        for i, (tq0, tn) in enumerate(qtiles):
                            md = modes[i]
                            if md == "none":
                                continue
                            ob = obanks[i // 2]
                            oc = (i % 2) * 129
                            if md == "past":
                                sc = sel[gi][0:tn, i, j:j + 1]
                                rd = [B_ps[ob], B_sel[gi]]
                            else:
                                sc = 1.0
                                rd = [B_ps[ob]]
                            if first[i]:
                                ts('dve', acc[gi][0:tn, i, :], ps[ob][0:tn, oc:oc + 129], sc, None, ALU.mult, None,
                                   rd, [B_acc[gi][i]])
                                first[i] = False
                            else:
                                stt('dve', acc[gi][0:tn, i, :], ps[ob][0:tn, oc:oc + 129], sc, acc[gi][0:tn, i, :], ALU.mult, ALU.add,
                                    rd + [B_acc[gi][i]], [B_acc[gi][i]])

                    blocks = [(j, ["past"] * nqt) for j in past] + list(specials)
                    prev = None
                    for bi, (j, modes) in enumerate(blocks):
                        pts_ = stage_a(j, modes)
                        p0b_advance()
                        if bi == 0 and widx > 0:
                            emit_final(work[widx - 1], 1 - gi)
                        if bi == 0 and gkind == "halo" and h + 1 < H:
                            load_head(h + 1)
                        if bi == min(1, len(blocks) - 1) and widx + 1 < len(work):
                            emit_sel(work[widx + 1], 1 - gi)
                        if prev is not None:
                            stage_b(*prev)
                        prev = (j, modes, pts_)
                    stage_b(*prev)
            emit_final(work[-1], (len(work) - 1) % 2)
            p0b_advance(final=True)
            for _ in p0b:
                pass
            stt('dve', a2[:], modT[:, 4 * KC:5 * KC], 1.0, vec_sb[:, KC:2 * KC], ALU.add, ALU.mult, [B_modT2, B_vec], [B_a2])
            P.barrier()

        with contextlib.ExitStack() as st:
            cv_sb = sb("cv_sb", [128, HC, NE], BF16, st)
            B_cvs = [Buf() for _ in range(HC)]
            for j in range(HC):
                dma('sp', cv_sb[:, j, :], CVT[j], [B_CVT[j]], [B_cvs[j]])
            sg_t = [sb(f"sg_t{i}", [128, NE], BF16, st) for i in range(4)]
            B_sg = [Buf() for _ in range(4)]
            t1 = sb("t1", [128, NE], F32, st)
            B_t1 = Buf()
            t2 = sb("t2", [128, NE], F32, st)
            B_t2 = Buf()
            mst = [sb(f"mst{i}", [128, NE], BF16, st) for i in range(2)]
            B_mst = [Buf(), Buf()]
            for m in range(KC):
                k = m % 2
                dma('sp', sg_t[2 * k][:], SGA[m], [B_SGA[m]], [B_sg[2 * k]])
                dma('sp', sg_t[2 * k + 1][:], SGC[m], [B_SGC[m]], [B_sg[2 * k + 1]])
                wt, wb = load_w(wao[m], HC)
                proj(wt, wb, HC, attT, B_att, tiles_ext,
                     lambda c0, n, pap, pb, k=k: tt('dve', t1[:, c0:c0 + n], pap, sg_t[2 * k][:, c0:c0 + n], ALU.mult,
                                                   [pb, B_sg[2 * k]], [B_t1]))
                wt, wb = load_w(wco[m], HC)
                proj(wt, wb, HC, cv_sb, B_cvs, tiles_ext,
                     lambda c0, n, pap, pb, k=k: tt('dve', t2[:, c0:c0 + n], pap, sg_t[2 * k + 1][:, c0:c0 + n], ALU.mult,
                                                   [pb, B_sg[2 * k + 1]], [B_t2]))
                tt('pool', mst[k][:], t1[:], t2[:], ALU.add, [B_t1, B_t2], [B_mst[k]])
                dma('sp', MT[m], mst[k][:], [B_mst[k]], [B_MT[m]])
            P.barrier()
        st34.close()
        RSTD2 = dscr("RSTD2", [128, NE], F32)
        B_RSTD2 = Buf()
        with contextlib.ExitStack() as st:
            m_sb = sb("m_sb", [128, KC, NE], BF16, st)
            B_ms = [Buf() for _ in range(KC)]
            for kc in range(KC):
                dma('sp', m_sb[:, kc, :], MT[kc], [B_MT[kc]], [B_ms[kc]])
            xs = [sb(f"p4xs{i}", [128, NE], F32, st) for i in range(2)]
            B_xs = [Buf(), Buf()]
            accq = sb("accq", [128, NE], F32, st)
            sqt = sb("sqt", [128, NE], F32, st)
            B_accq, B_sqt = Buf(), Buf()
            mset('pool', accq[:], 0.0, [B_accq])
            xe_c = [xe[kc * 128:(kc + 1) * 128, :] for kc in range(KC)]
            for m in range(KC):
                k = m % 2
                dma('sp', xs[k][:], xe_c[m], (), [B_xs[k]])
                wt, wb = load_w(wo[m], KC)
                proj(wt, wb, KC, m_sb, B_ms, tiles_ext,
                     lambda c0, n, pap, pb, k=k, m=m: stt('dve', xs[k][:, c0:c0 + n], pap, g1[:, m:m + 1], xs[k][:, c0:c0 + n],
                                                         ALU.mult, ALU.add, [pb, B_modT2, B_xs[k]], [B_xs[k]]))
                dma('sp', XM[m], xs[k][:], [B_xs[k]], [B_XM[m]])
                tt('pool', sqt[:], xs[k][:], xs[k][:], ALU.mult, [B_xs[k]], [B_sqt])
                tt('pool', accq[:], accq[:], sqt[:], ALU.add, [B_sqt, B_accq], [B_accq])
            for (c0, n) in tiles_ext:
                pap, pb = psum_for(n)
                mm(pap, ones32[:], accq[:, c0:c0 + n], True, True, [B_ones32, B_accq], [pb])
                act(sqt[:, c0:c0 + n], pap, AF.Sqrt, [pb, B_sqt, B_cst], [B_sqt], bias=epsc, scale=1.0 / D)
            P.add('dve', lambda e: e.reciprocal(out=sqt[:], in_=sqt[:]), [B_sqt], [B_sqt])
            dma('sp', RSTD2, sqt[:], [B_sqt], [B_RSTD2])
            P.barrier()

        with contextlib.ExitStack() as st:
            h2 = sb("h2", [128, KC, NE], BF16, st)
            B_h2 = [Buf() for _ in range(KC)]
            with contextlib.ExitStack() as st2:
                xs = [sb(f"p5xs{i}", [128, NE], F32, st2) for i in range(3)]
                B_xs = [Buf(), Buf(), Buf()]
                rstd2 = sb("rstd2", [128, NE], F32, st2)
                B_rstd2 = Buf()
                dma('sp', rstd2[:], RSTD2, [B_RSTD2], [B_rstd2])
                for kc in range(KC):
                    i = kc % 3
                    dma('sp', xs[i][:], XM[kc], [B_XM[kc]], [B_xs[i]])
                    stt('dve', xs[i][:], xs[i][:], a2[:, kc:kc + 1], rstd2[:], ALU.mult, ALU.mult,
                        [B_xs[i], B_a2, B_rstd2], [B_xs[i]])
                    act(h2[:, kc, :], xs[i][:], AF.Identity, [B_xs[i], B_modT2], [B_h2[kc]], bias=s2[:, kc:kc + 1])
                P.barrier()
            gpre = sb("gpre", [128, NE], F32, st)
            gcv = sb("gcv", [128, NE], F32, st)
            vpre = sb("vpre", [128, NE], F32, st)
            vcv = gpre
            ast = [sb(f"ast{i}", [128, NH], BF16, st) for i in range(2)]
            fcw_sb = sb("fcw_sb", [128, 2 * JF * 3], F32, st)
            B_gpre, B_gcv, B_vpre, B_fcw = Buf(), Buf(), Buf(), Buf()
            B_vcv = B_gpre
            B_ast = [Buf(), Buf()]
            dma('sp', fcw_sb[:], fcw, (), [B_fcw])

            def conv3(eng, dst, src, B_dst, B_src, ch):
                w0 = fcw_sb[:, 3 * ch:3 * ch + 1]
                w1 = fcw_sb[:, 3 * ch + 1:3 * ch + 2]
                w2 = fcw_sb[:, 3 * ch + 2:3 * ch + 3]
                ts(eng, dst[:, HALO:NE], src[:, HALO:NE], w2, None, ALU.mult, None, [B_src, B_fcw], [B_dst])
                stt(eng, dst[:, HALO:NE], src[:, HALO - 1:NE - 1], w1, dst[:, HALO:NE], ALU.mult, ALU.add, [B_src, B_fcw, B_dst], [B_dst])
                stt(eng, dst[:, HALO:NE], src[:, HALO - 2:NE - 2], w0, dst[:, HALO:NE], ALU.mult, ALU.add, [B_src, B_fcw, B_dst], [B_dst])

            for j in range(JF):
                k = j % 2
                if j % max(1, JF // KC) == 0 and j // max(1, JF // KC) < KC:
                    m_ = j // max(1, JF // KC)
                    dma('pool', WDB[m_], wdn[m_], (), [B_WDB[m_]])
                wt, wb = load_w(wup[j], KC)
                proj(wt, wb, KC, h2, B_h2, tiles_ext,
                     lambda c0, n, pap, pb: cp('act', gpre[:, c0:c0 + n], pap, [pb], [B_gpre]))
                ts('dve', gpre[:, 0:HALO], gpre[:, 0:HALO], hflag, None, ALU.mult, None, [B_gpre, B_cst], [B_gpre])
                conv3('dve', gcv, gpre, B_gcv, B_gpre, j)
                act(gcv[:, HALO:NE], gcv[:, HALO:NE], AF.Silu, [B_gcv], [B_gcv])
                wt, wb = load_w(wup[JF + j], KC)
                proj(wt, wb, KC, h2, B_h2, tiles_ext,
                     lambda c0, n, pap, pb: cp('act', vpre[:, c0:c0 + n], pap, [pb], [B_vpre]))
                ts('pool', vpre[:, 0:HALO], vpre[:, 0:HALO], hflag, None, ALU.mult, None, [B_vpre, B_cst], [B_vpre])
                conv3('dve', vcv, vpre, B_vcv, B_vpre, JF + j)
                tt('pool', ast[k][:], gcv[:, HALO:NE], vcv[:, HALO:NE], ALU.mult, [B_gcv, B_vcv], [B_ast[k]])
                dma('sp', ATs[j], ast[k][:], [B_ast[k]], [B_ATs[j]])
            P.barrier()

        wst.close()
        with contextlib.ExitStack() as st:
            a_sb = sb("a_sb", [128, JF, 512], BF16, st)
            B_as = [Buf() for _ in range(JF)]
            wd_t = [sb(f"wd{i}", [128, JF, 128], BF16, st) for i in range(2)]
            B_wd = [Buf(), Buf()]
            xm_t = [sb(f"xm_t{i}", [128, 512], F32, st) for i in range(2)]
            B_xmt = [Buf(), Buf()]
            NXO = 3
            xo_t = [sb(f"xo_t{i}", [128, 512], F32, st) for i in range(NXO)]
            B_xot = [Buf() for _ in range(NXO)]
            xo_ring = Ring(list(range(NXO)))
            sq7 = sb("sq7", [128, 512], F32, st)
            acc7 = sb("acc7", [128, 512], F32, st)
            rstd3 = sb("rstd3", [128, 512], F32, st)
            ot = [sb(f"p7ot{i}", [128, 512], F32, st) for i in range(2)]
            B_sq7, B_acc7, B_r3 = Buf(), Buf(), Buf()
            B_ot = [Buf(), Buf()]

            def fin_gen(n):
                for kc in range(KC):
                    k = xo_ring.next()
                    dma('act', xo_t[k][:], XO[kc][:, n * 512:(n + 1) * 512], [B_XO[kc][n]], [B_xot[k]])
                    if kc == 0:
                        tt('dve', acc7[:], xo_t[k][:], xo_t[k][:], ALU.mult, [B_xot[k]], [B_acc7])
                    else:
                        tt('dve', sq7[:], xo_t[k][:], xo_t[k][:], ALU.mult, [B_xot[k]], [B_sq7])
                        tt('dve', acc7[:], acc7[:], sq7[:], ALU.add, [B_sq7, B_acc7], [B_acc7])
                    yield
                yield
                yield
                pap, pb = psum_for(512)
                mm(pap, ones32[:], acc7[:], True, True, [B_ones32, B_acc7], [pb])
                act(rstd3[:], pap, AF.Sqrt, [pb, B_cst], [B_r3], bias=epsc, scale=1.0 / D)
                P.add('dve', lambda e: e.reciprocal(out=rstd3[:], in_=rstd3[:]), [B_r3], [B_r3])
                yield
                for kc in range(KC):
                    k = xo_ring.next()
                    o = kc % 2
                    dma('act', xo_t[k][:], XO[kc][:, n * 512:(n + 1) * 512], [B_XO[kc][n]], [B_xot[k]])
                    stt('dve', ot[o][:], xo_t[k][:], vec_sb[:, 2 * KC + kc:2 * KC + kc + 1], rstd3[:], ALU.mult, ALU.mult,
                        [B_xot[k], B_vec, B_r3], [B_ot[o]])
                    dma('act', outT[kc * 128:(kc + 1) * 128, n * 512:(n + 1) * 512], ot[o][:], [B_ot[o]], [B_OUT])
                    yield

            def advance(gen, steps):
                for _ in range(steps):
                    try:
                        next(gen)
                    except StopIteration:
                        return

            it = 0
            fin = None
            for n in range(NT):
                for j in range(JF):
                    dma('sp', a_sb[:, j, :], ATs[j][:, n * 512:(n + 1) * 512], [B_ATs[j]], [B_as[j]])
                for m in range(KC):
                    k = it % 2
                    it += 1
                    dma('pool', wd_t[k][:], WDB[m], [B_WDB[m]], [B_wd[k]])
                    dma('sp', xm_t[k][:], XM[m][:, HALO + n * 512:HALO + (n + 1) * 512], [B_XM[m]], [B_xmt[k]])
                    pap, pb = psum_for(512)
                    for j in range(JF):
                        mm(pap, wd_t[k][:, j, :], a_sb[:, j, :], j == 0, j == JF - 1, [B_wd[k], B_as[j]], [pb])
                    stt('dve', xm_t[k][:], pap, g2[:, m:m + 1], xm_t[k][:], ALU.mult, ALU.add, [pb, B_modT2, B_xmt[k]], [B_xmt[k]])
                    dma('sp', XO[m][:, n * 512:(n + 1) * 512], xm_t[k][:], [B_xmt[k]], [B_XO[m][n]])
                    if fin is not None and m >= 1:
                        advance(fin, 4)
                if fin is not None:
                    advance(fin, 1000)
                fin = fin_gen(n)
            advance(fin, 1000)
            P.barrier()

        P.emit()
    return nc


def relay(W, mw=128):
    K, N = W.shape
    return np.ascontiguousarray(W.reshape(K // 128, 128, N // mw, mw).transpose(2, 1, 0, 3))


def vecl(v):
    return np.ascontiguousarray(v.reshape(-1, 128).T)


def host_consts(cfg, is_second):
    NB, NBT = cfg.NB, cfg.NBT
    cst = np.zeros((128, 148), np.float32)
    cst[:, 146] = EPS
    i = np.arange(32)
    cst[0:32, 0] = (ROPE_THETA ** (-(2.0 * (i % 16)) / 32.0)).astype(np.float32)
    cst[:, 1] = 1.0 if is_second else 0.0
    pv = 0.0 if is_second else -1e30
    for ob in range(NB):
        eb = np.full(16, -1e30, np.float32)
        eb[0:NB] = pv
        eb[NB:NB + ob] = 0.0
        cst[:, 2 + 16 * ob:2 + 16 * ob + 16] = eb
    eb = np.full(16, -1e30, np.float32)
    eb[0:NB - 1] = pv
    cst[:, 2 + 16 * 8:2 + 16 * 8 + 16] = eb
    cm = np.zeros((128, 512), np.float32)
    cm[:, 0:128] = np.eye(128, dtype=np.float32)
    for r in range(16):
        cm[r + 16, 128 + r] = -1.0
        cm[r, 128 + 16 + r] = 1.0
    kk = np.arange(128)[:, None]
    qq = np.arange(128)[None, :]
    cm[:, 256:384] = np.where(qq >= kk, 0.0, NEG)
    cm[:, 384:512] = 1.0
    return cst, cm


def prepare_inputs(cfg, x, c, positions, w_ada, b_ada, g_mix, w_in, conv_w, w_attn_out, w_conv_out,
                   w_o, g_ffn, w_up, ffn_conv_w, w_down, g_final):
    D, NH, NE, HALO, AW, CW = cfg.D, cfg.NH, cfg.NE, cfg.HALO, cfg.AW, cfg.CW
    x = np.asarray(x, np.float32)
    w_in0 = np.asarray(w_in[0], np.float32)
    o = 0
    parts = {}
    for name, wdt in (("wq", AW), ("wk", AW), ("wv", AW), ("wcb", CW), ("wcc", CW), ("wcu", CW), ("wga", D), ("wgc", D)):
        parts[name] = relay(w_in0[:, o:o + wdt])
        o += wdt
    shared = dict(parts)
    shared["w_ada"] = np.ascontiguousarray(np.asarray(w_ada[0], np.float32))
    shared["b_ada"] = np.ascontiguousarray(np.asarray(b_ada[0], np.float32)[None, :])
    shared["vecs"] = np.ascontiguousarray(np.concatenate([vecl(np.asarray(g_mix[0], np.float32)),
                                                          vecl(np.asarray(g_ffn[0], np.float32)),
                                                          vecl(np.asarray(g_final, np.float32))], axis=1))
    cwl = np.asarray(conv_w[0], np.float32)
    shared["cw"] = np.ascontiguousarray(cwl.T.reshape(CW // 128, 128, 3).transpose(1, 0, 2).reshape(128, -1))
    shared["wao"] = relay(np.asarray(w_attn_out[0], np.float32))
    shared["wco"] = relay(np.asarray(w_conv_out[0], np.float32))
    shared["wo"] = relay(np.asarray(w_o[0], np.float32))
    shared["wup"] = relay(np.asarray(w_up[0], np.float32))
    fl = np.asarray(ffn_conv_w[0], np.float32)
    shared["fcw"] = np.ascontiguousarray(fl.T.reshape(-1, 128, 3).transpose(1, 0, 2).reshape(128, -1))
    shared["wdn"] = relay(np.asarray(w_down[0], np.float32))
    in_maps = []
    pos = np.asarray(positions, np.int32)
    for core in range(2 * cfg.B):
        b, half = divmod(core, 2)
        xT = x[b].T
        m = dict(shared)
        if half == 0:
            xe = np.zeros((D, NE), np.float32)
            xe[:, HALO:] = xT[:, 0:NH]
            pe = np.zeros((NE,), np.int32)
            pe[HALO:] = pos[b, 0:NH]
        else:
            xe = np.ascontiguousarray(xT[:, NH - HALO:2 * NH])
            pe = pos[b, NH - HALO:2 * NH]
        m["xe"] = xe
        m["xp"] = np.ascontiguousarray(xT[:, 0:NH])
        m["cT"] = vecl(np.asarray(c[b], np.float32))
        m["pos_e"] = np.ascontiguousarray(np.broadcast_to(pe[None, :], (32, NE)))
        m["pos_p"] = np.ascontiguousarray(np.broadcast_to(pos[b, 0:NH][None, :], (32, NH)))
        cst, cm = host_consts(cfg, half == 1)
        m["cst"] = cst
        m["cmat"] = cm
        in_maps.append(m)
    return in_maps


_NC_CACHE = {}


def run(cfg, inputs, debug=False):
    key = (cfg.D, cfg.S, cfg.B, cfg.H, cfg.DFF, debug)
    if key not in _NC_CACHE:
        _NC_CACHE[key] = build(cfg, debug)
    nc = _NC_CACHE[key]
    in_maps = prepare_inputs(cfg, **inputs)
    ncores = 2 * cfg.B
    res = run_bass_kernel_spmd(nc, in_maps, core_ids=list(range(ncores)))
    out = np.empty((cfg.B, cfg.S, cfg.D), np.float32)
    for core in range(ncores):
        b, half = divmod(core, 2)
        out[b, half * cfg.NH:(half + 1) * cfg.NH, :] = np.asarray(res.results[core]["outT"]).T
    return out, res


def kernel(**inputs):
    cfg = Cfg()
    out, _ = run(cfg, inputs)
    return out
```

```python
import contextlib
import math
import os
import numpy as np
import concourse.bass as bass
import concourse.mybir as mybir
from concourse.bass_utils import run_bass_kernel_spmd

F32 = mybir.dt.float32
BF16 = mybir.dt.bfloat16
I32 = mybir.dt.int32
ALU = mybir.AluOpType
AF = mybir.ActivationFunctionType
AX = mybir.AxisListType

QUEUES = ('pe', 'act', 'dve', 'pool', 'sp')
EPOCH = 20000
NSLOT = 8


class Buf:
    __slots__ = ('w', 'r', 'rd', 'name')

    def __init__(self, name=''):
        self.w = None
        self.r = {}
        self.rd = []
        self.name = name


class Op:
    __slots__ = ('eng', 'fn', 'deps', 'sig', 'idx', 'seq', 'dma', 'slot', 'slotval')

    def __init__(self, eng, fn, dma):
        self.eng = eng
        self.fn = fn
        self.deps = []
        self.sig = False
        self.idx = -1
        self.seq = -1
        self.dma = dma
        self.slot = -1
        self.slotval = 0


class Prog:
    def __init__(self, nc):
        self.nc = nc
        self.q = {k: [] for k in QUEUES}
        self.ndma = {k: 0 for k in QUEUES}
        self.pending_dma = []

    def add(self, eng, fn, reads=(), writes=(), dma=False):
        op = Op(eng, fn, dma)
        op.idx = len(self.q[eng])
        deps = {}

        def need(d):
            if d is None or d is op:
                return
            if d.dma:
                deps[id(d)] = d
                return
            if d.eng == 'pe' and eng == 'pe' and not dma:
                return
            k = d.eng
            if k not in deps or deps[k].idx < d.idx:
                deps[k] = d

        for b in reads:
            need(b.w)
        for b in writes:
            need(b.w)
            for r in b.r.values():
                need(r)
            for r in b.rd:
                need(r)
        for b in reads:
            if dma:
                b.rd.append(op)
            else:
                b.r[eng] = op
        for b in writes:
            b.w = op
            b.r = {}
            b.rd = []
        op.deps = list(deps.values())
        for d in op.deps:
            d.sig = True
        if dma:
            i = self.ndma[eng]
            self.ndma[eng] = i + 1
            op.slot = i % NSLOT
            op.slotval = 16 * (i // NSLOT + 1)
            op.sig = True
            self.pending_dma.append(op)
        self.q[eng].append(op)
        return op

    def wait_ops(self, eng, ops):
        op = Op(eng, None, False)
        op.idx = len(self.q[eng])
        op.deps = [o for o in ops if o is not None]
        for d in op.deps:
            d.sig = True
        self.q[eng].append(op)
        return op

    def barrier(self):
        lasts = []
        for k in QUEUES:
            for o in reversed(self.q[k]):
                if not o.dma and o.fn is not None:
                    lasts.append(o)
                    break
        dmas = list(self.pending_dma)
        self.pending_dma = []
        for k in QUEUES:
            self.wait_ops(k, [o for o in lasts if o.eng != k] + dmas)

    def emit(self):
        nc = self.nc
        nsig = {}
        for k in QUEUES:
            s = 0
            for o in self.q[k]:
                if o.sig and not o.dma and o.fn is not None:
                    s += 1
                    o.seq = s
            nsig[k] = s
        with contextlib.ExitStack() as st:
            csem = {}
            for k in QUEUES:
                n = (nsig[k] + EPOCH - 1) // EPOCH
                csem[k] = [st.enter_context(nc.semaphore(f"c_{k}_{i}")) for i in range(max(n, 1))]
            dsem = {}
            for k in QUEUES:
                if self.ndma[k]:
                    dsem[k] = [st.enter_context(nc.semaphore(f"d_{k}_{i}")) for i in range(NSLOT)]
            block = st.enter_context(nc.Block())

            def run(k):
                def body(e):
                    cw = {kk: 0 for kk in QUEUES}
                    dw = {}

                    def wait_for(d):
                        if d.dma:
                            key = (d.eng, d.slot)
                            if dw.get(key, 0) >= d.slotval:
                                return
                            dw[key] = d.slotval
                            e.wait_ge(dsem[d.eng][d.slot], d.slotval)
                        else:
                            if cw[d.eng] >= d.seq:
                                return
                            cw[d.eng] = d.seq
                            ep, v = divmod(d.seq - 1, EPOCH)
                            e.wait_ge(csem[d.eng][ep], v + 1)

                    for o in self.q[k]:
                        for d in o.deps:
                            wait_for(d)
                        if o.fn is None:
                            continue
                        if o.dma:
                            if o.slotval > 16:
                                key = (k, o.slot)
                                if dw.get(key, 0) < o.slotval - 16:
                                    dw[key] = o.slotval - 16
                                    e.wait_ge(dsem[k][o.slot], o.slotval - 16)
                            ins = o.fn(e)
                            ins.then_inc(dsem[k][o.slot], 16)
                        else:
                            ins = o.fn(e)
                            if o.sig:
                                ep, v = divmod(o.seq - 1, EPOCH)
                                ins.then_inc(csem[k][ep], 1)
                return body

            if self.q['sp']:
                block.sync(run('sp'))
            if self.q['pe']:
                block.tensor(run('pe'))
            if self.q['act']:
                block.scalar(run('act'))
            if self.q['dve']:
                block.vector(run('dve'))
            if self.q['pool']:
                block.gpsimd(run('pool'))


class Cfg:
    def __init__(s, D=4096, S=4096, B=4, H=16, DFF=14336):
        s.D, s.S, s.B, s.H, s.DFF = D, S, B, H, DFF
        s.AW = H * 128
        s.CW = H * 128
        s.KC = D // 128
        s.NH = S // 2
        s.HALO = 4
        s.NE = s.NH + s.HALO
        s.NB = s.NH // 256
        s.NT = s.NH // 512
        s.JF = DFF // 128
        s.NBT = 2 * s.NB
        s.INW = 3 * s.AW + 3 * s.CW + 2 * D


ROPE_THETA = 500000.0
EPS = 1e-6
NEG = -30000.0


class Ring:
    def __init__(self, items):
        self.items = items
        self.i = 0

    def next(self):
        it = self.items[self.i % len(self.items)]
        self.i += 1
        return it


def build(cfg, debug=False):
    c = cfg
    D, KC, NH, NE, HALO, H, NB, NT, JF, NBT = c.D, c.KC, c.NH, c.NE, c.HALO, c.H, c.NB, c.NT, c.JF, c.NBT
    HC = H
    nc = bass.Bass("TRN2", target_bir_lowering=False)
    P = Prog(nc)
    skind = "ExternalOutput" if debug else "Internal"

    def din(name, shape, dt=F32):
        return nc.dram_tensor(name, list(shape), dt, kind="ExternalInput").ap()

    def dscr(name, shape, dt):
        return nc.dram_tensor(name, list(shape), dt, kind=skind).ap()

    xe = din("xe", [D, NE])
    xp = din("xp", [D, NH])
    cT = din("cT", [128, KC])
    pos_e = din("pos_e", [32, NE], I32)
    pos_p = din("pos_p", [32, NH], I32)
    cst = din("cst", [128, 148])
    cmat = din("cmat", [128, 512])
    w_ada = din("w_ada", [D, 6 * D])
    b_ada = din("b_ada", [1, 6 * D])
    vecs = din("vecs", [128, 3 * KC])
    wq = din("wq", [H, 128, KC, 128])
    wk = din("wk", [H, 128, KC, 128])
    wv = din("wv", [H, 128, KC, 128])
    wcb = din("wcb", [HC, 128, KC, 128])
    wcc = din("wcc", [HC, 128, KC, 128])
    wcu = din("wcu", [HC, 128, KC, 128])
    wga = din("wga", [KC, 128, KC, 128])
    wgc = din("wgc", [KC, 128, KC, 128])
    cw = din("cw", [128, HC * 3])
    wao = din("wao", [KC, 128, HC, 128])
    wco = din("wco", [KC, 128, HC, 128])
    wo = din("wo", [KC, 128, KC, 128])
    wup = din("wup", [2 * JF, 128, KC, 128])
    fcw = din("fcw", [128, 2 * JF * 3])
    wdn = din("wdn", [KC, 128, JF, 128])
    outT = nc.dram_tensor("outT", [D, NH], F32, kind="ExternalOutput").ap()

    QT = dscr("QT", [H, 128, NE], BF16)
    KT = dscr("KT", [H, 128, 2 * NH], BF16)
    VV = dscr("VV", [H, 2 * NH, 128], BF16)
    CVT = dscr("CVT", [HC, 128, NE], BF16)
    SGA = dscr("SGA", [KC, 128, NE], BF16)
    SGC = dscr("SGC", [KC, 128, NE], BF16)
    MT = dscr("MT", [KC, 128, NE], BF16)
    XM = dscr("XM", [KC, 128, NE], F32)
    ATs = dscr("ATs", [JF, 128, NH], BF16)
    XO = dscr("XO", [KC, 128, NH], F32)
    WDB = dscr("WDB", [KC, 128, JF, 128], BF16)
    B_WDB = [Buf() for _ in range(KC)]
    ATTd = dscr("ATTd", [H, 128, NE], BF16) if debug else None
    B_QT = [Buf() for _ in range(H)]
    B_KT = [[Buf(), Buf()] for _ in range(H)]
    B_VV = [[Buf(), Buf()] for _ in range(H)]
    B_CVT = [Buf() for _ in range(HC)]
    B_SGA = [Buf() for _ in range(KC)]
    B_SGC = [Buf() for _ in range(KC)]
    B_MT = [Buf() for _ in range(KC)]
    B_XM = [Buf() for _ in range(KC)]
    B_ATs = [Buf() for _ in range(JF)]
    B_XO = [[Buf() for _ in range(NT)] for _ in range(KC)]
    B_OUT = Buf()

    def mm(out, lhsT, rhs, start, stop, r, w):
        return P.add('pe', lambda e: e.matmul(out, lhsT, rhs, start=start, stop=stop), r, w)

    def tr(out, in_, ident, r, w):
        return P.add('pe', lambda e: e.transpose(out, in_, ident), r, w)

    def act(out, in_, func, r, w, bias=None, scale=None, accum_out=None):
        kw = {}
        if bias is not None:
            kw['bias'] = bias
        if scale is not None:
            kw['scale'] = scale
        if accum_out is not None:
            kw['accum_out'] = accum_out
        return P.add('act', lambda e: e.activation(out=out, in_=in_, func=func, **kw), r, w)

    def ts(eng, out, in0, s1, s2, op0, op1, r, w):
        if op1 is None:
            return P.add(eng, lambda e: e.tensor_scalar(out=out, in0=in0, scalar1=s1, scalar2=None, op0=op0), r, w)
        return P.add(eng, lambda e: e.tensor_scalar(out=out, in0=in0, scalar1=s1, scalar2=s2, op0=op0, op1=op1), r, w)

    def tt(eng, out, in0, in1, op, r, w):
        return P.add(eng, lambda e: e.tensor_tensor(out=out, in0=in0, in1=in1, op=op), r, w)

    def stt(eng, out, in0, scalar, in1, op0, op1, r, w):
        eng = 'dve'
        return P.add(eng, lambda e: e.scalar_tensor_tensor(out=out, in0=in0, scalar=scalar, in1=in1, op0=op0, op1=op1), r, w)

    def cp(eng, out, in_, r, w):
        if eng == 'act':
            return P.add('act', lambda e: e.copy(out=out, in_=in_), r, w)
        return P.add(eng, lambda e: e.tensor_copy(out=out, in_=in_), r, w)

    def mset(eng, ap, val, w):
        return P.add(eng, lambda e: e.memset(ap, val), (), w)

    def dma(q, out, in_, r, w):
        return P.add(q, lambda e: e.dma_start(out=out, in_=in_), r, w, dma=True)

    tiles_ext = [(0, HALO)] + [(HALO + 512 * i, 512) for i in range(NT)]
    tiles_own = [(HALO + 512 * i, 512) for i in range(NT)]
    tiles_prev = [(512 * i, 512) for i in range(NT)]

    with contextlib.ExitStack() as gst:
        def sb(name, shape, dt, st=None):
            return (st or gst).enter_context(nc.sbuf_tensor(name, list(shape), dt))

        ps = [gst.enter_context(nc.psum_tensor(f"ps{i}", [128, 512], F32)) for i in range(8)]
        B_ps = [Buf(f"ps{i}") for i in range(8)]
        main_ring = Ring(list(range(7)))
        MINI = 16
        B_mini = [B_ps[7] for _ in range(512 // MINI)]
        mini_ring = Ring(list(range(512 // MINI)))

        busy_banks = set()

        held_banks = set()

        def main_bank():
            while True:
                i = main_ring.next()
                if i not in busy_banks and i not in held_banks:
                    return i

        def psum_for(n):
            if n <= MINI:
                i = mini_ring.next()
                return ps[7][:, i * MINI:i * MINI + n], B_mini[i]
            i = main_bank()
            return ps[i][:, 0:n], B_ps[i]

        cst_sb = sb("cst_sb", [128, 148], F32)
        cm_sb = sb("cm_sb", [128, 512], BF16)
        ident = cm_sb[:, 0:128]
        permT = cm_sb[:, 128:256]
        maskneg = cm_sb[:, 256:384]
        ones = cm_sb[:, 384:512]
        vec_sb = sb("vec_sb", [128, 3 * KC], F32)
        modT = sb("modT", [128, 6 * KC], F32)
        a1 = sb("a1", [128, KC], F32)
        a2 = sb("a2", [128, KC], F32)
        kmT = sb("kmT", [128, H, NBT], F32)
        kmTb = sb("kmTb", [128, H, NBT], BF16)
        B_cst, B_cm, B_vec, B_modT, B_a1, B_a2 = Buf(), Buf(), Buf(), Buf(), Buf(), Buf()
        ones32 = sb("ones32", [128, 128], F32)
        B_ones32 = Buf()
        B_km = [Buf() for _ in range(H)]
        B_kmb = [Buf() for _ in range(H)]
        WR = 3
        B_wr = [Buf() for _ in range(WR)]
        wring = Ring(list(range(WR)))

        invf = cst_sb[0:32, 0:1]
        hflag = cst_sb[:, 1:2]
        epsc = cst_sb[:, 146:147]

        def EB(idx):
            return cst_sb[:, 2 + 16 * idx: 2 + 16 * idx + NBT]

        s1 = modT[:, 0 * KC:1 * KC]
        g1 = modT[:, 2 * KC:3 * KC]
        s2 = modT[:, 3 * KC:4 * KC]
        g2 = modT[:, 5 * KC:6 * KC]

        mset('dve', ones32[:], 1.0, [B_ones32])
        dma('sp', cst_sb[:], cst, (), [B_cst])
        dma('pool', cm_sb[:], cmat, (), [B_cm])
        dma('sp', vec_sb[:], vecs, (), [B_vec])

        w_items = []
        for h in range(H):
            w_items += [(wk[h], KC), (wv[h], KC)]
        for h in range(H):
            w_items += [(wq[h], KC), (wk[h], KC), (wv[h], KC)]
        for j in range(HC):
            w_items += [(wcc[j], KC), (wcu[j], KC), (wcb[j], KC)]
        for m in range(KC):
            w_items.append((wga[m], KC))
        for m in range(KC):
            w_items.append((wgc[m], KC))
        for m in range(KC):
            w_items += [(wao[m], HC), (wco[m], HC)]
        for m in range(KC):
            w_items.append((wo[m], KC))
        for j in range(JF):
            w_items += [(wup[j], KC), (wup[JF + j], KC)]
        w_state = {'issued': 0, 'taken': 0, 'slot': {}}

        def load_w(wdram_m, kcn):
            t = w_state['taken']
            assert w_items[t][0] is wdram_m or True
            ahead = int(os.environ.get("KPREF", "3"))
            while w_state['issued'] < min(t + ahead, len(w_items)):
                ii = w_state['issued']
                ap_, kn_ = w_items[ii]
                i = wring.next()
                dma('pool', wring_t[i][:, 0:kn_, :], ap_, (), [B_wr[i]])
                w_state['slot'][ii] = i
                w_state['issued'] = ii + 1
            i = w_state['slot'][t]
            assert w_items[t][1] == kcn
            w_state['taken'] = t + 1
            return wring_t[i], B_wr[i]

        pending = []

        def fill_step():
            while pending:
                try:
                    next(pending[0])
                    return
                except StopIteration:
                    pending.pop(0)

        def after_proj(post):
            if post is None:
                while pending:
                    fill_step()
                return
            pending.append(post())

        def proj(wt, wb, kcn, src, srcb, tiles, evac):
            regs = [psum_for(n) for (_, n) in tiles]
            order = sorted(range(len(tiles)), key=lambda i: (tiles[i][1] <= MINI, i))
            for (pap_, pb_) in regs:
                for bi_ in range(7):
                    if pb_ is B_ps[bi_]:
                        busy_banks.add(bi_)
            for kc in range(kcn):
                for i in order:
                    (c0, n), (pap, pb) = tiles[i], regs[i]
                    mm(pap, wt[:, kc, :], src[:, kc, c0:c0 + n], kc == 0, kc == kcn - 1, [wb, srcb[kc]], [pb])
                if kc >= 2:
                    fill_step()
            while pending:
                fill_step()
            busy_banks.clear()
            for i in ([order[-1]] + order[:-1]) if tiles[order[-1]][1] <= MINI else order:
                (c0, n), (pap, pb) = tiles[i], regs[i]
                evac(c0, n, pap, pb)

        c_act = sb("c_act", [128, KC], BF16)
        one11 = sb("one11", [1, 1], F32)
        B_wa2 = [Buf() for _ in range(3)]
        B_brow2, B_mrow2, B_modT2 = Buf(), Buf(), Buf()
        p0b_bufs = {}
        B_ca, B_one = Buf(), Buf()
        wst = contextlib.ExitStack()
        wring_t = [sb(f"wr{i}", [128, max(KC, HC), 128], BF16, wst) for i in range(WR)]
        with contextlib.ExitStack() as st:
            c_sb = sb("c_sb", [128, KC], F32, st)
            CWID = 2048
            wa_t = [sb(f"wa{i}", [128, CWID], BF16, st) for i in range(3)]
            B_wa = [Buf() for _ in range(3)]
            wa_ring = Ring([0, 1, 2])
            brow = sb("brow", [1, CWID], F32, st)
            mrow = sb("mrow", [1, CWID], F32, st)
            B_c, B_brow, B_mrow = Buf(), Buf(), Buf()
            dma('sp', c_sb[:], cT, (), [B_c])
            act(c_act[:], c_sb[:], AF.Silu, [B_c], [B_ca])
            mset('dve', one11[:], 1.0, [B_one])
            ncol = 2 * D
            modps = ps[7][:, 0:2 * KC]
            B_modps = B_ps[7]
            for c0 in range(0, ncol, CWID):
                wdt = min(CWID, ncol - c0)
                nb_ = (wdt + 511) // 512
                banks = [main_ring.next() for _ in range(nb_)]
                dma('sp', brow[:, 0:wdt], b_ada[:, c0:c0 + wdt], (), [B_brow])
                for kc in range(KC):
                    i = wa_ring.next()
                    dma('pool', wa_t[i][:, 0:wdt], w_ada[kc * 128:(kc + 1) * 128, c0:c0 + wdt], (), [B_wa[i]])
                    for bi, bk in enumerate(banks):
                        n = min(512, wdt - bi * 512)
                        mm(ps[bk][0:1, 0:n], c_act[:, kc:kc + 1], wa_t[i][:, bi * 512:bi * 512 + n],
                           kc == 0, kc == KC - 1, [B_ca, B_wa[i]], [B_ps[bk]])
                for bi, bk in enumerate(banks):
                    n = min(512, wdt - bi * 512)
                    tt('dve', mrow[:, bi * 512:bi * 512 + n], ps[bk][0:1, 0:n], brow[:, bi * 512:bi * 512 + n], ALU.add,
                       [B_ps[bk], B_brow], [B_mrow])
                for j in range(wdt // 128):
                    col = (c0 // 128) + j
                    mm(modps[:, col:col + 1], mrow[0:1, j * 128:(j + 1) * 128], one11[0:1, 0:1], True, True,
                       [B_mrow, B_one], [B_modps])
            cp('dve', modT[:, 0:2 * KC], modps, [B_modps], [B_modT])
            stt('dve', a1[:], modT[:, 1 * KC:2 * KC], 1.0, vec_sb[:, 0:KC], ALU.add, ALU.mult, [B_modT, B_vec], [B_a1])
            P.barrier()

        def p0b_gen():
            wa2_t, brow2, mrow2 = p0b_bufs['wa2'], p0b_bufs['brow2'], p0b_bufs['mrow2']
            G = min(4, KC)
            grp = [(c0, k0) for c0 in range(2 * D, 6 * D, 512) for k0 in range(0, KC, G)]
            NR = len(wa2_t)

            def issue(gidx):
                if gidx < len(grp):
                    c0_, k0_ = grp[gidx]
                    i_ = gidx % NR
                    dma('pool', wa2_t[i_][:], w_ada[k0_ * 128:(k0_ + G) * 128, c0_:c0_ + 512].rearrange("(k p) c -> p k c", p=128),
                        (), [B_wa2[i_]])
            for g0 in range(NR - 1):
                issue(g0)
            for gidx, (c0, k0) in enumerate(grp):
                issue(gidx + NR - 1)
                i = gidx % NR
                if k0 == 0:
                    dma('sp', brow2[:], b_ada[:, c0:c0 + 512], (), [B_brow2])
                for kk in range(G):
                    kc = k0 + kk
                    mm(ps[3][0:1, 0:512], c_act[:, kc:kc + 1], wa2_t[i][:, kk, :], kc == 0, kc == KC - 1, [B_ca, B_wa2[i]], [B_ps[3]])
                    yield
                if k0 + G >= KC:
                    tt('dve', mrow2[:], ps[3][0:1, 0:512], brow2[:], ALU.add, [B_ps[3], B_brow2], [B_mrow2])
                    for q in range(4):
                        mm(ps[7][:, 260 + q:261 + q], mrow2[0:1, q * 128:(q + 1) * 128], one11[0:1, 0:1], True, True,
                           [B_mrow2, B_one], [B_ps[7]])
                    col = c0 // 128
                    cp('dve', modT[:, col:col + 4], ps[7][:, 260:264], [B_ps[7]], [B_modT2])
                    yield

        def make_tables(st, pos_dram, N, name, tiles):
            cosT = sb(name + "cos", [32, N], F32, st)
            sinT = sb(name + "sin", [32, N], F32, st)
            B_cos, B_sin = Buf(), Buf()
            C1 = 6.28125
            C2 = 2 * math.pi - C1
            PI_LO = 3.1415925
            with contextlib.ExitStack() as st2:
                pi_ = sb(name + "pi", [32, 512], I32, st2)
                ang = sb(name + "ang", [32, 512], F32, st2)
                aa = sb(name + "aa", [32, 512], F32, st2)
                tq = sb(name + "tq", [32, 512], F32, st2)
                B_pi, B_ang, B_aa, B_tq = Buf(), Buf(), Buf(), Buf()
                for (c0, n) in tiles:
                    dma('sp', pi_[:, 0:n], pos_dram[:, c0:c0 + n], (), [B_pi])
                    cp('dve', ang[:, 0:n], pi_[:, 0:n], [B_pi], [B_ang])
                    ts('dve', ang[:, 0:n], ang[:, 0:n], invf, None, ALU.mult, None, [B_ang, B_cst], [B_ang])
                    for (dst, B_dst, shift) in ((sinT, B_sin, 0.0), (cosT, B_cos, math.pi / 2)):
                        ts('dve', aa[:, 0:n], ang[:, 0:n], shift, None, ALU.add, None, [B_ang], [B_aa])
                        ts('dve', tq[:, 0:n], aa[:, 0:n], 1.0 / (2 * math.pi), None, ALU.mult, None, [B_aa], [B_tq])
                        cp('dve', pi_[:, 0:n], tq[:, 0:n], [B_tq], [B_pi])
                        cp('dve', tq[:, 0:n], pi_[:, 0:n], [B_pi], [B_tq])
                        stt('dve', aa[:, 0:n], tq[:, 0:n], -C1, aa[:, 0:n], ALU.mult, ALU.add, [B_tq, B_aa], [B_aa])
                        stt('dve', aa[:, 0:n], tq[:, 0:n], -C2, aa[:, 0:n], ALU.mult, ALU.add, [B_tq, B_aa], [B_aa])
                        ts('dve', aa[:, 0:n], aa[:, 0:n], -PI_LO, PI_LO, ALU.max, ALU.min, [B_aa], [B_aa])
                        act(dst[:, c0:c0 + n], aa[:, 0:n], AF.Sin, [B_aa], [B_dst])
                P.barrier()
            return cosT, sinT, B_cos, B_sin

        def make_h(st, x_dram, N, tiles, a_vec, B_a, s_vec, name, src_is_scratch_bufs=None, rstd_out=None):
            hT = sb(name + "hT", [128, KC, N], BF16, st)
            B_h = [Buf() for _ in range(KC)]
            with contextlib.ExitStack() as st2:
                xs = [sb(name + f"xs{i}", [128, N], F32, st2) for i in range(3)]
                B_xs = [Buf(), Buf(), Buf()]
                sq = [sb(name + f"sq{i}", [128, N], BF16, st2) for i in range(2)]
                B_sq = [Buf(), Buf()]
                rstd = sb(name + "rstd", [128, N], F32, st2)
                B_rstd = Buf()
                regs = [psum_for(n) for (_, n) in tiles]
                for kc in range(KC):
                    i = kc % 3
                    j2 = kc % 2
                    rb = [src_is_scratch_bufs[kc]] if src_is_scratch_bufs else []
                    dma('sp' if kc % 2 == 0 else 'pool', xs[i][:], x_dram[kc], rb, [B_xs[i]])
                    act(sq[j2][:], xs[i][:], AF.Square, [B_xs[i]], [B_sq[j2]])
                    for (c0, n), (pap, pb) in zip(tiles, regs):
                        mm(pap, ones, sq[j2][:, c0:c0 + n], kc == 0, kc == KC - 1, [B_cm, B_sq[j2]], [pb])
                for (c0, n), (pap, pb) in zip(tiles, regs):
                    act(rstd[:, c0:c0 + n], pap, AF.Sqrt, [pb, B_cst], [B_rstd], bias=epsc, scale=1.0 / D)
                P.add('dve', lambda e: e.reciprocal(out=rstd[:], in_=rstd[:]), [B_rstd], [B_rstd])
                for kc in range(KC):
                    i = (kc + KC) % 3
                    rb = [src_is_scratch_bufs[kc]] if src_is_scratch_bufs else []
                    dma('sp' if kc % 2 == 0 else 'pool', xs[i][:], x_dram[kc], rb, [B_xs[i]])
                    stt('dve', xs[i][:], xs[i][:], a_vec[:, kc:kc + 1], rstd[:], ALU.mult, ALU.mult,
                        [B_xs[i], B_a, B_rstd], [B_xs[i]])
                    act(hT[:, kc, :], xs[i][:], AF.Identity, [B_xs[i], B_modT], [B_h[kc]], bias=s_vec[:, kc:kc + 1])
                P.barrier()
            return hT, B_h

        def rope_gen(stg, B_st, tiles, src_c0, cosT, sinT, B_cos, B_sin, rt, B_rt):
            for ti, (c0, n) in enumerate(tiles):
                pap, pb = psum_for(n)
                d0 = c0 - src_c0
                mm(pap, permT, stg[:, d0:d0 + n], True, True, [B_cm] + B_st(ti), [pb])
                k = ti % 2
                cp('act', rt[k][:, 0:n], pap[0:32, :], [pb], [B_rt[k]])
                tt('dve', rt[k][:, 0:n], rt[k][:, 0:n], sinT[:, c0:c0 + n], ALU.mult, [B_rt[k], B_sin], [B_rt[k]])
                tt('dve', rt[k + 2][:, 0:n], stg[0:32, d0:d0 + n], cosT[:, c0:c0 + n], ALU.mult, B_st(ti) + [B_cos], [B_rt[k + 2]])
                tt('dve', stg[0:32, d0:d0 + n], rt[k][:, 0:n], rt[k + 2][:, 0:n], ALU.add, [B_rt[k], B_rt[k + 2]], B_st(ti))
                yield

        def head_kv(st_tiles, hT, B_h, cosT, sinT, B_cos, B_sin, h, col0, half, stg_ring, vt_ring, rt, B_rt, tiles, src_c0):
            wt, wb = load_w(wk[h], KC)
            kst, B_kst = stg_ring.next()
            tix = {c0: ti for ti, (c0, n) in enumerate(tiles)}

            def evk(c0, n, pap, pb):
                cp('act', kst[:, c0 - src_c0:c0 - src_c0 + n], pap, [pb], [B_kst[tix[c0]], B_kst[tix[c0] + 1]])
            proj(wt, wb, KC, hT, B_h, tiles, evk)

            def post_k():
                yield from rope_gen(kst, lambda ti: [B_kst[ti], B_kst[ti + 1]], tiles, src_c0, cosT, sinT, B_cos, B_sin, rt, B_rt)
                nt_ = 5
                P.add('dve', lambda e: e.tensor_reduce(out=kmT[:, h, half * NB:(half + 1) * NB],
                                                       in_=kst[:, 0:NH].rearrange("p (b k) -> p b k", k=256),
                                                       axis=AX.X, op=ALU.add), B_kst[0:nt_], [B_km[h]])
                ts('dve', kmT[:, h, half * NB:(half + 1) * NB], kmT[:, h, half * NB:(half + 1) * NB], 1.0 / 256, None,
                   ALU.mult, None, [B_km[h]], [B_km[h]])
                dma('sp', KT[h][:, col0:col0 + NH], kst[:, 0:NH], B_kst[0:nt_], [B_KT[h][half]])
                if half == 1:
                    cp('dve', kmTb[:, h, :], kmT[:, h, :], [B_km[h]], [B_kmb[h]])
                yield
            after_proj(post_k)
            wt, wb = load_w(wv[h], KC)
            vst, B_vst = stg_ring.next()

            def evv(c0, n, pap, pb):
                cp('act', vst[:, c0 - src_c0:c0 - src_c0 + n], pap, [pb], [B_vst[tix[c0]], B_vst[tix[c0] + 1]])
            proj(wt, wb, KC, hT, B_h, tiles, evv)

            def post_v():
                vt, B_vt = vt_ring.next()
                for g in range(NH // 1024):
                    i = main_bank()
                    held_banks.add(i)
                    pb16 = ps[i].bitcast(BF16)
                    for t in range(8):
                        tok = g * 1024 + t * 128
                        tr(pb16[:, t * 128:(t + 1) * 128], vst[:, tok:tok + 128], ident, [B_vst[tok // 512], B_vst[tok // 512 + 1], B_cm], [B_ps[i]])
                        if t % 2 == 1 and t < 7:
                            yield
                    cp('dve', vt[:, g * 8:(g + 1) * 8, :], pb16[:, 0:1024].rearrange("p (t d) -> p t d", d=128), [B_ps[i]], [B_vt])
                    held_banks.discard(i)
                    yield
                dma('sp', VV[h][col0:col0 + NH, :].rearrange("(t p) d -> p t d", p=128), vt[:], [B_vt], [B_VV[h][half]])
                yield
            after_proj(post_v)

        with contextlib.ExitStack() as st:
            xp_c = [xp[kc * 128:(kc + 1) * 128, :] for kc in range(KC)]
            hTp, B_hp = make_h(st, xp_c, NH, tiles_prev, a1, B_a1, s1, "p1")
            cosP, sinP, B_cosP, B_sinP = make_tables(st, pos_p, NH, "tp", tiles_prev)
            stg = [sb(f"p1stg{i}", [128, NH], BF16, st) for i in range(2)]
            stg_ring = Ring([(stg[i], [Buf() for _ in range(5)]) for i in range(2)])
            vts = [sb(f"p1vt{i}", [128, NH // 128, 128], BF16, st) for i in range(2)]
            vt_ring = Ring([(vts[i], Buf()) for i in range(2)])
            rt = [sb(f"p1rt{i}", [32, 512], F32, st) for i in range(4)]
            B_rt = [Buf() for _ in range(4)]
            for h in range(H):
                head_kv(st, hTp, B_hp, cosP, sinP, B_cosP, B_sinP, h, 0, 0, stg_ring, vt_ring, rt, B_rt, tiles_prev, 0)
            after_proj(None)
            P.barrier()

        with contextlib.ExitStack() as st:
            xe_c = [xe[kc * 128:(kc + 1) * 128, :] for kc in range(KC)]
            hTe, B_he = make_h(st, xe_c, NE, tiles_ext, a1, B_a1, s1, "p2")
            with contextlib.ExitStack() as st2:
                cosE, sinE, B_cosE, B_sinE = make_tables(st2, pos_e, NE, "te", tiles_ext)
                stg = [sb(f"p2stg{i}", [128, NE], BF16, st2) for i in range(2)]
                stg_ring = Ring([(stg[i], [Buf() for _ in range(5)]) for i in range(2)])
                vts = [sb(f"p2vt{i}", [128, NH // 128, 128], BF16, st2) for i in range(2)]
                vt_ring = Ring([(vts[i], Buf()) for i in range(2)])
                rt = [sb(f"p2rt{i}", [32, 512], F32, st2) for i in range(4)]
                B_rt = [Buf() for _ in range(4)]
                for h in range(H):
                    wt, wb = load_w(wq[h], KC)
                    qst, B_qst = stg_ring.next()

                    tixe = {c0: ti for ti, (c0, n) in enumerate(tiles_ext)}

                    def evq(c0, n, pap, pb, qst=qst, B_qst=B_qst):
                        cp('act', qst[:, c0:c0 + n], pap, [pb], [B_qst[tixe[c0]]])
                    proj(wt, wb, KC, hTe, B_he, tiles_ext, evq)

                    def post_q(qst=qst, B_qst=B_qst, h=h):
                        yield from rope_gen(qst, lambda ti: [B_qst[ti]], tiles_ext, 0, cosE, sinE, B_cosE, B_sinE, rt, B_rt)
                        dma('sp', QT[h], qst[:], B_qst[0:5], [B_QT[h]])
                        yield
                    after_proj(post_q)
                    head_kv(st2, hTe, B_he, cosE, sinE, B_cosE, B_sinE, h, NH, 1, stg_ring, vt_ring, rt, B_rt, tiles_own, HALO)
                after_proj(None)
                P.barrier()
            with contextlib.ExitStack() as st2:
                ccst = sb("ccst", [128, NE], F32, st2)
                ust = sb("ust", [128, NE], F32, st2)
                yst = sb("yst", [128, NE], F32, st2)
                cvst = [sb(f"cvst{i}", [128, NE], BF16, st2) for i in range(2)]
                cw_sb = sb("cw_sb", [128, HC * 3], F32, st2)
                B_cc, B_u, B_y, B_cw = Buf(), Buf(), Buf(), Buf()
                B_cv = [Buf(), Buf()]
                dma('sp', cw_sb[:], cw, (), [B_cw])
                mset('dve', yst[:], 0.0, [B_y])
                for j in range(HC):
                    wt, wb = load_w(wcc[j], KC)
                    proj(wt, wb, KC, hTe, B_he, tiles_ext,
                         lambda c0, n, pap, pb: cp('act', ccst[:, c0:c0 + n], pap, [pb], [B_cc]))
                    wt, wb = load_w(wcu[j], KC)
                    proj(wt, wb, KC, hTe, B_he, tiles_ext,
                         lambda c0, n, pap, pb: tt('dve', ust[:, c0:c0 + n], pap, ccst[:, c0:c0 + n], ALU.mult, [pb, B_cc], [B_u]))
                    ts('dve', ust[:, 0:HALO], ust[:, 0:HALO], hflag, None, ALU.mult, None, [B_u, B_cst], [B_u])
                    ts('dve', yst[:, 2:NE], ust[:, 2:NE], cw_sb[:, 3 * j + 2:3 * j + 3], None, ALU.mult, None, [B_u, B_cw], [B_y])
                    stt('pool', yst[:, 2:NE], ust[:, 1:NE - 1], cw_sb[:, 3 * j + 1:3 * j + 2], yst[:, 2:NE], ALU.mult, ALU.add,
                        [B_u, B_cw, B_y], [B_y])
                    stt('pool', yst[:, 2:NE], ust[:, 0:NE - 2], cw_sb[:, 3 * j:3 * j + 1], yst[:, 2:NE], ALU.mult, ALU.add,
                        [B_u, B_cw, B_y], [B_y])
                    wt, wb = load_w(wcb[j], KC)
                    k = j % 2
                    proj(wt, wb, KC, hTe, B_he, tiles_ext,
                         lambda c0, n, pap, pb, k=k: tt('dve', cvst[k][:, c0:c0 + n], pap, yst[:, c0:c0 + n], ALU.mult, [pb, B_y], [B_cv[k]]))
                    dma('sp', CVT[j], cvst[k][:], [B_cv[k]], [B_CVT[j]])
                P.barrier()
            with contextlib.ExitStack() as st2:
                gst_ = [sb(f"gst{i}", [128, NE], BF16, st2) for i in range(2)]
                B_g = [Buf(), Buf()]
                idx = 0
                for (wd_, dst, B_dst) in ((wga, SGA, B_SGA), (wgc, SGC, B_SGC)):
                    for m in range(KC):
                        wt, wb = load_w(wd_[m], KC)
                        k = idx % 2
                        idx += 1
                        proj(wt, wb, KC, hTe, B_he, tiles_ext,
                             lambda c0, n, pap, pb, k=k: act(gst_[k][:, c0:c0 + n], pap, AF.Sigmoid, [pb], [B_g[k]]))
                        dma('sp', dst[m], gst_[k][:], [B_g[k]], [B_dst[m]])
                P.barrier()

        st34 = contextlib.ExitStack()
        attT = sb("attT", [128, H, NE], BF16, st34)
        B_att = [Buf() for _ in range(H)]
        with contextlib.ExitStack() as st:
            NKT = 2 * NH // 128
            q_t = [sb(f"q_t{i}", [128, NE], BF16, st) for i in range(2)]
            k_t = [sb(f"k_t{i}", [128, 2 * NH], BF16, st) for i in range(2)]
            v_t = [sb(f"v_t{i}", [128, NKT, 129], BF16, st) for i in range(2)]
            B_q = [Buf(), Buf()]
            B_k = [Buf(), Buf()]
            B_v = [Buf(), Buf()]
            NPT = 4
            pt_t = [sb(f"pt{i}", [128, 512], BF16, st) for i in range(NPT)]
            B_pt = [Buf() for _ in range(NPT)]
            pt_ring = Ring(list(range(NPT)))
            acc = [sb(f"acc{i}", [128, 4, 129], F32, st) for i in range(2)]
            B_acc = [[Buf() for _ in range(4)] for _ in range(2)]
            gm = sb("gm", [128, 4, NBT], F32, st)
            m8 = sb("m8", [128, 4, 8], F32, st)
            sel = [sb(f"sel{i}", [128, 4, NBT], F32, st) for i in range(2)]
            B_gm, B_m8 = Buf(), Buf()
            B_sel = [Buf(), Buf()]
            rcp = sb("rcp", [128, 8], F32, st)
            B_rcp = Buf()
            onrm = [sb(f"onrm{i}", [128, 128], BF16, st) for i in range(2)]
            B_on = [Buf(), Buf()]
            S_ring = Ring([0, 1, 2])
            O_ring = Ring([(4, 5), (6, 7)])
            for i in range(2):
                mset('dve', v_t[i][:, :, 128:129], 1.0, [B_v[i]])
            SCALE = 1.0 / math.sqrt(128.0)
            grp_i = 0
            p0b_bufs['wa2'] = [sb(f"wa2_{i}", [128, min(4, KC), 512], BF16, st) for i in range(3)]
            p0b_bufs['brow2'] = sb("brow2", [1, 512], F32, st)
            p0b_bufs['mrow2'] = sb("mrow2", [1, 512], F32, st)
            p0b = p0b_gen()
            p0b_total = (4 * D // 512) * (KC + 1)
            p0b_state = {'done': 0, 'blocks': 0}
            blocks_total = H * sum([NB] + [NB + 2 * g + 2 for g in range(NT)])

            def p0b_advance(final=False):
                p0b_state['blocks'] += 1
                target = p0b_total if final else min(p0b_total, (p0b_state['blocks'] * p0b_total) // blocks_total + 2)
                while p0b_state['done'] < target:
                    try:
                        next(p0b)
                    except StopIteration:
                        p0b_state['done'] = p0b_total
                        break
                    p0b_state['done'] += 1
            def group_info(gkind, qc0, qn):
                if gkind == "halo":
                    return [(qc0, qn)], -1, list(range(NB - 1)), [(NB - 1, ["diag_hi"])]
                g = (qc0 - HALO) // 512
                return ([(qc0 + 128 * i, 128) for i in range(4)], g, list(range(NB + 2 * g)),
                        [(NB + 2 * g, ["diag_lo", "diag_hi", "past", "past"]),
                         (NB + 2 * g + 1, ["none", "none", "diag_lo", "diag_hi"])])

            def load_head(h):
                hb = h % 2
                dma('sp', q_t[hb][:], QT[h], [B_QT[h]], [B_q[hb]])
                dma('sp', k_t[hb][:], KT[h], [B_KT[h][0], B_KT[h][1]], [B_k[hb]])
                dma('sp', v_t[hb][:, :, 0:128], VV[h].rearrange("(t p) d -> p t d", p=128), [B_VV[h][0], B_VV[h][1]], [B_v[hb]])

            def emit_sel(w, gi):
                h, gkind, qc0, qn = w
                hb = h % 2
                qh = q_t[hb]
                qtiles, g, _, _ = group_info(gkind, qc0, qn)
                gpap = ps[7][:, 320:320 + 4 * NBT]
                for i, (tq0, tn) in enumerate(qtiles):
                    mm(gpap[0:tn, i * NBT:(i + 1) * NBT], qh[:, tq0:tq0 + tn], kmTb[:, h, :], True, True,
                       [B_q[hb], B_kmb[h]], [B_ps[7]])
                for i, (tq0, tn) in enumerate(qtiles):
                    ebi = 8 if gkind == "halo" else (2 * g + i // 2)
                    tt('dve', gm[0:tn, i, :], gpap[0:tn, i * NBT:(i + 1) * NBT], EB(ebi)[0:tn, :], ALU.add,
                       [B_ps[7], B_cst], [B_gm])
                    P.add('dve', lambda e, i=i, tn=tn: e.max(out=m8[0:tn, i, :], in_=gm[0:tn, i, :]), [B_gm], [B_m8])
                    ts('dve', m8[0:tn, i, 2:3], m8[0:tn, i, 2:3], -1e29, None, ALU.max, None, [B_m8], [B_m8])
                    ts('dve', sel[gi][0:tn, i, :], gm[0:tn, i, :], m8[0:tn, i, 2:3], None, ALU.is_ge, None,
                       [B_gm, B_m8], [B_sel[gi]])

            def emit_final(w, gi):
                h, gkind, qc0, qn = w
                qtiles, _, _, _ = group_info(gkind, qc0, qn)
                for i, (tq0, tn) in enumerate(qtiles):
                    P.add('dve', lambda e, i=i, tn=tn, gi=gi: e.reciprocal(out=rcp[0:tn, gi * 4 + i:gi * 4 + i + 1], in_=acc[gi][0:tn, i, 128:129]),
                          [B_acc[gi][i]], [B_rcp])
                    k = i % 2
                    ts('dve', onrm[k][0:tn, :], acc[gi][0:tn, i, 0:128], rcp[0:tn, gi * 4 + i:gi * 4 + i + 1], None, ALU.mult, None,
                       [B_acc[gi][i], B_rcp], [B_on[k]])
                    tp16 = ps[7].bitcast(BF16)
                    tpo = tp16[:, 768 + k * 128: 768 + k * 128 + tn]
                    tr(tpo, onrm[k][0:tn, :], ident[0:tn, 0:tn], [B_on[k], B_cm], [B_ps[7]])
                    cp('act', attT[:, h, tq0:tq0 + tn], tpo, [B_ps[7]], [B_att[h]])
                if debug and gkind == "own" and qc0 == HALO + 512 * (NT - 1):
                    dma('sp', ATTd[h], attT[:, h, :], [B_att[h]], [Buf()])

            groups = [("halo", 0, HALO)] + [("own", HALO + 512 * g, 512) for g in range(NT)]
            work = [(h, gk, qc0, qn) for h in range(H) for (gk, qc0, qn) in groups]
            load_head(0)
            emit_sel(work[0], 0)
            for widx, w in enumerate(work):
                h, gkind, qc0, qn = w
                hb = h % 2
                qh, kh, vh = q_t[hb], k_t[hb], v_t[hb]
                gi = widx % 2
                if True:
                    qtiles, g, past, specials = group_info(gkind, qc0, qn)
                    nqt = len(qtiles)
                    first = [True] * nqt

                    def stage_a(j, modes):
                        pts = []
                        for kt in range(2):
                            ktile = 2 * j + kt
                            need_q = []
                            for i, md in enumerate(modes):
                                if md == "past":
                                    need_q.append(i)
                                elif md == "diag_lo" and kt == 0:
                                    need_q.append(i)
                                elif md == "diag_hi":
                                    need_q.append(i)
                            if not need_q:
                                pts.append(None)
                                continue
                            q_lo = qtiles[need_q[0]][0]
                            q_hi = qtiles[need_q[-1]][0] + qtiles[need_q[-1]][1]
                            sb_i = S_ring.next()
                            sp_ = ps[sb_i]
                            off = q_lo - qc0
                            diag_q = [i for i in need_q if (modes[i] == "diag_lo" and kt == 0) or (modes[i] == "diag_hi" and kt == 1)]
                            mm(sp_[:, off:off + (q_hi - q_lo)], kh[:, ktile * 128:(ktile + 1) * 128], qh[:, q_lo:q_hi],
                               True, len(diag_q) == 0, [B_k[hb], B_q[hb]], [B_ps[sb_i]])
                            for di, i in enumerate(diag_q):
                                tq0, tn = qtiles[i]
                                if gkind == "halo":
                                    mrhs = maskneg[:, 128 - HALO:128]
                                else:
                                    mrhs = maskneg[:, 0:tn]
                                mm(sp_[:, tq0 - qc0:tq0 - qc0 + tn], ident, mrhs, False, di == len(diag_q) - 1,
                                   [B_cm], [B_ps[sb_i]])
                            pi = pt_ring.next()
                            act(pt_t[pi][:, off:off + (q_hi - q_lo)], sp_[:, off:off + (q_hi - q_lo)], AF.Exp,
                                [B_ps[sb_i]], [B_pt[pi]], scale=SCALE)
                            pts.append((pi, need_q))
                        return pts

                    def stage_b(j, modes, pts):
                        obanks = O_ring.next()
                        for i, (tq0, tn) in enumerate(qtiles):
                            md = modes[i]
                            if md == "none":
                                continue
                            kts = [kt for kt in range(2) if pts[kt] is not None and i in pts[kt][1]]
                            ob = obanks[i // 2]
                            oc = (i % 2) * 129
                            for n_, kt in enumerate(kts):
                                pi = pts[kt][0]
                                mm(ps[ob][0:tn, oc:oc + 129], pt_t[pi][:, tq0 - qc0:tq0 - qc0 + tn], vh[:, 2 * j + kt, :],
                                   n_ == 0, n_ == len(kts) - 1, [B_pt[pi], B_v[hb]], [B_ps[ob]])
                        for i, (tq0, tn) in enumerate(qtiles):
                            md = modes[i]
                            if md == "none":
                                continue
                            ob = obanks[i // 2]
                            oc = (i % 2) * 129
                            if md == "past":
                                sc = sel[gi][0:tn, i, j:j + 1]
                                rd = [B_ps[ob], B_sel[gi]]
                            else:
                                sc = 1.0
                                rd = [B_ps[ob]]
                            if first[i]:
                                ts('dve', acc[gi][0:tn, i, :], ps[ob][0:tn, oc:oc + 129], sc, None, ALU.mult, None,
                                   rd, [B_acc[gi][i]])
                                first[i] = False
                            else:
                                stt('dve', acc[gi][0:tn, i, :], ps[ob][0:tn, oc:oc + 129], sc, acc[gi][0:tn, i, :], ALU.mult, ALU.add,
                                    rd + [B_acc[gi][i]], [B_acc[gi][i]])

                    blocks = [(j, ["past"] * nqt) for j in past] + list(specials)
                    prev = None
                    for bi, (j, modes) in enumerate(blocks):
                        pts_ = stage_a(j, modes)
                        p0b_advance()
                        if bi == 0 and widx > 0:
                            emit_final(work[widx - 1], 1 - gi)
                        if bi == 0 and gkind == "halo" and h + 1 < H:
                            load_head(h + 1)
                        if bi == min(1, len(blocks) - 1) and widx + 1 < len(work):
                            emit_sel(work[widx + 1], 1 - gi)
                        if prev is not None:
                            stage_b(*prev)
                        prev = (j, modes, pts_)
                    stage_b(*prev)
            emit_final(work[-1], (len(work) - 1) % 2)
            p0b_advance(final=True)
            for _ in p0b:
                pass
            stt('dve', a2[:], modT[:, 4 * KC:5 * KC], 1.0, vec_sb[:, KC:2 * KC], ALU.add, ALU.mult, [B_modT2, B_vec], [B_a2])
            P.barrier()

        with contextlib.ExitStack() as st:
            cv_sb = sb("cv_sb", [128, HC, NE], BF16, st)
            B_cvs = [Buf() for _ in range(HC)]
            for j in range(HC):
                dma('sp', cv_sb[:, j, :], CVT[j], [B_CVT[j]], [B_cvs[j]])
            sg_t = [sb(f"sg_t{i}", [128, NE], BF16, st) for i in range(4)]
            B_sg = [Buf() for _ in range(4)]
            t1 = sb("t1", [128, NE], F32, st)
            B_t1 = Buf()
            t2 = sb("t2", [128, NE], F32, st)
            B_t2 = Buf()
            mst = [sb(f"mst{i}", [128, NE], BF16, st) for i in range(2)]
            B_mst = [Buf(), Buf()]
            for m in range(KC):
                k = m % 2
                dma('sp', sg_t[2 * k][:], SGA[m], [B_SGA[m]], [B_sg[2 * k]])
                dma('sp', sg_t[2 * k + 1][:], SGC[m], [B_SGC[m]], [B_sg[2 * k + 1]])
                wt, wb = load_w(wao[m], HC)
                proj(wt, wb, HC, attT, B_att, tiles_ext,
                     lambda c0, n, pap, pb, k=k: tt('dve', t1[:, c0:c0 + n], pap, sg_t[2 * k][:, c0:c0 + n], ALU.mult,
                                                   [pb, B_sg[2 * k]], [B_t1]))
                wt, wb = load_w(wco[m], HC)
                proj(wt, wb, HC, cv_sb, B_cvs, tiles_ext,
                     lambda c0, n, pap, pb, k=k: tt('dve', t2[:, c0:c0 + n], pap, sg_t[2 * k + 1][:, c0:c0 + n], ALU.mult,
                                                   [pb, B_sg[2 * k + 1]], [B_t2]))
                tt('pool', mst[k][:], t1[:], t2[:], ALU.add, [B_t1, B_t2], [B_mst[k]])
                dma('sp', MT[m], mst[k][:], [B_mst[k]], [B_MT[m]])
            P.barrier()
        st34.close()
        RSTD2 = dscr("RSTD2", [128, NE], F32)
        B_RSTD2 = Buf()
        with contextlib.ExitStack() as st:
            m_sb = sb("m_sb", [128, KC, NE], BF16, st)
            B_ms = [Buf() for _ in range(KC)]
            for kc in range(KC):
                dma('sp', m_sb[:, kc, :], MT[kc], [B_MT[kc]], [B_ms[kc]])
            xs = [sb(f"p4xs{i}", [128, NE], F32, st) for i in range(2)]
            B_xs = [Buf(), Buf()]
            accq = sb("accq", [128, NE], F32, st)
            sqt = sb("sqt", [128, NE], F32, st)
            B_accq, B_sqt = Buf(), Buf()
            mset('pool', accq[:], 0.0, [B_accq])
            xe_c = [xe[kc * 128:(kc + 1) * 128, :] for kc in range(KC)]
            for m in range(KC):
                k = m % 2
                dma('sp', xs[k][:], xe_c[m], (), [B_xs[k]])
                wt, wb = load_w(wo[m], KC)
                proj(wt, wb, KC, m_sb, B_ms, tiles_ext,
                     lambda c0, n, pap, pb, k=k, m=m: stt('dve', xs[k][:, c0:c0 + n], pap, g1[:, m:m + 1], xs[k][:, c0:c0 + n],
                                                         ALU.mult, ALU.add, [pb, B_modT2, B_xs[k]], [B_xs[k]]))
                dma('sp', XM[m], xs[k][:], [B_xs[k]], [B_XM[m]])
                tt('pool', sqt[:], xs[k][:], xs[k][:], ALU.mult, [B_xs[k]], [B_sqt])
                tt('pool', accq[:], accq[:], sqt[:], ALU.add, [B_sqt, B_accq], [B_accq])
            for (c0, n) in tiles_ext:
                pap, pb = psum_for(n)
                mm(pap, ones32[:], accq[:, c0:c0 + n], True, True, [B_ones32, B_accq], [pb])
                act(sqt[:, c0:c0 + n], pap, AF.Sqrt, [pb, B_sqt, B_cst], [B_sqt], bias=epsc, scale=1.0 / D)
            P.add('dve', lambda e: e.reciprocal(out=sqt[:], in_=sqt[:]), [B_sqt], [B_sqt])
            dma('sp', RSTD2, sqt[:], [B_sqt], [B_RSTD2])
            P.barrier()

        with contextlib.ExitStack() as st:
            h2 = sb("h2", [128, KC, NE], BF16, st)
            B_h2 = [Buf() for _ in range(KC)]
            with contextlib.ExitStack() as st2:
                xs = [sb(f"p5xs{i}", [128, NE], F32, st2) for i in range(3)]
                B_xs = [Buf(), Buf(), Buf()]
                rstd2 = sb("rstd2", [128, NE], F32, st2)
                B_rstd2 = Buf()
                dma('sp', rstd2[:], RSTD2, [B_RSTD2], [B_rstd2])
                for kc in range(KC):
                    i = kc % 3
                    dma('sp', xs[i][:], XM[kc], [B_XM[kc]], [B_xs[i]])
                    stt('dve', xs[i][:], xs[i][:], a2[:, kc:kc + 1], rstd2[:], ALU.mult, ALU.mult,
                        [B_xs[i], B_a2, B_rstd2], [B_xs[i]])
                    act(h2[:, kc, :], xs[i][:], AF.Identity, [B_xs[i], B_modT2], [B_h2[kc]], bias=s2[:, kc:kc + 1])
                P.barrier()
            gpre = sb("gpre", [128, NE], F32, st)
            gcv = sb("gcv", [128, NE], F32, st)
            vpre = sb("vpre", [128, NE], F32, st)
            vcv = gpre
            ast = [sb(f"ast{i}", [128, NH], BF16, st) for i in range(2)]
            fcw_sb = sb("fcw_sb", [128, 2 * JF * 3], F32, st)
            B_gpre, B_gcv, B_vpre, B_fcw = Buf(), Buf(), Buf(), Buf()
            B_vcv = B_gpre
            B_ast = [Buf(), Buf()]
            dma('sp', fcw_sb[:], fcw, (), [B_fcw])

            def conv3(eng, dst, src, B_dst, B_src, ch):
                w0 = fcw_sb[:, 3 * ch:3 * ch + 1]
                w1 = fcw_sb[:, 3 * ch + 1:3 * ch + 2]
                w2 = fcw_sb[:, 3 * ch + 2:3 * ch + 3]
                ts(eng, dst[:, HALO:NE], src[:, HALO:NE], w2, None, ALU.mult, None, [B_src, B_fcw], [B_dst])
                stt(eng, dst[:, HALO:NE], src[:, HALO - 1:NE - 1], w1, dst[:, HALO:NE], ALU.mult, ALU.add, [B_src, B_fcw, B_dst], [B_dst])
                stt(eng, dst[:, HALO:NE], src[:, HALO - 2:NE - 2], w0, dst[:, HALO:NE], ALU.mult, ALU.add, [B_src, B_fcw, B_dst], [B_dst])

            for j in range(JF):
                k = j % 2
                if j % max(1, JF // KC) == 0 and j // max(1, JF // KC) < KC:
                    m_ = j // max(1, JF // KC)
                    dma('pool', WDB[m_], wdn[m_], (), [B_WDB[m_]])
                wt, wb = load_w(wup[j], KC)
                proj(wt, wb, KC, h2, B_h2, tiles_ext,
                     lambda c0, n, pap, pb: cp('act', gpre[:, c0:c0 + n], pap, [pb], [B_gpre]))
                ts('dve', gpre[:, 0:HALO], gpre[:, 0:HALO], hflag, None, ALU.mult, None, [B_gpre, B_cst], [B_gpre])
                conv3('dve', gcv, gpre, B_gcv, B_gpre, j)
                act(gcv[:, HALO:NE], gcv[:, HALO:NE], AF.Silu, [B_gcv], [B_gcv])
                wt, wb = load_w(wup[JF + j], KC)
                proj(wt, wb, KC, h2, B_h2, tiles_ext,
                     lambda c0, n, pap, pb: cp('act', vpre[:, c0:c0 + n], pap, [pb], [B_vpre]))
                ts('pool', vpre[:, 0:HALO], vpre[:, 0:HALO], hflag, None, ALU.mult, None, [B_vpre, B_cst], [B_vpre])
                conv3('dve', vcv, vpre, B_vcv, B_vpre, JF + j)
                tt('pool', ast[k][:], gcv[:, HALO:NE], vcv[:, HALO:NE], ALU.mult, [B_gcv, B_vcv], [B_ast[k]])
                dma('sp', ATs[j], ast[k][:], [B_ast[k]], [B_ATs[j]])
            P.barrier()

        wst.close()
        with contextlib.ExitStack() as st:
            a_sb = sb("a_sb", [128, JF, 512], BF16, st)
            B_as = [Buf() for _ in range(JF)]
            wd_t = [sb(f"wd{i}", [128, JF, 128], BF16, st) for i in range(2)]
            B_wd = [Buf(), Buf()]
            xm_t = [sb(f"xm_t{i}", [128, 512], F32, st) for i in range(2)]
            B_xmt = [Buf(), Buf()]
            NXO = 3
            xo_t = [sb(f"xo_t{i}", [128, 512], F32, st) for i in range(NXO)]
            B_xot = [Buf() for _ in range(NXO)]
            xo_ring = Ring(list(range(NXO)))
            sq7 = sb("sq7", [128, 512], F32, st)
            acc7s = [sb(f"acc7_{i}", [128, 512], F32, st) for i in range(2)]
            B_acc7s = [Buf(), Buf()]
            rstd3 = sb("rstd3", [128, 512], F32, st)
            ot = [sb(f"p7ot{i}", [128, 512], F32, st) for i in range(2)]
            B_sq7, B_r3 = Buf(), Buf()
            B_ot = [Buf(), Buf()]

            def fin_gen(n):
                acc7 = acc7s[n % 2]
                B_acc7 = B_acc7s[n % 2]
                yield
                yield
                pap, pb = psum_for(512)
                mm(pap, ones32[:], acc7[:], True, True, [B_ones32, B_acc7], [pb])
                act(rstd3[:], pap, AF.Sqrt, [pb, B_cst], [B_r3], bias=epsc, scale=1.0 / D)
                P.add('dve', lambda e: e.reciprocal(out=rstd3[:], in_=rstd3[:]), [B_r3], [B_r3])
                yield
                for kc in range(KC):
                    k = xo_ring.next()
                    o = kc % 2
                    dma('act', xo_t[k][:], XO[kc][:, n * 512:(n + 1) * 512], [B_XO[kc][n]], [B_xot[k]])
                    stt('dve', ot[o][:], xo_t[k][:], vec_sb[:, 2 * KC + kc:2 * KC + kc + 1], rstd3[:], ALU.mult, ALU.mult,
                        [B_xot[k], B_vec, B_r3], [B_ot[o]])
                    dma('act', outT[kc * 128:(kc + 1) * 128, n * 512:(n + 1) * 512], ot[o][:], [B_ot[o]], [B_OUT])
                    yield

            def advance(gen, steps):
                for _ in range(steps):
                    try:
                        next(gen)
                    except StopIteration:
                        return

            it = 0
            fin = None
            for n in range(NT):
                for j in range(JF):
                    dma('sp', a_sb[:, j, :], ATs[j][:, n * 512:(n + 1) * 512], [B_ATs[j]], [B_as[j]])
                for m in range(KC):
                    k = it % 2
                    it += 1
                    dma('pool', wd_t[k][:], WDB[m], [B_WDB[m]], [B_wd[k]])
                    dma('sp', xm_t[k][:], XM[m][:, HALO + n * 512:HALO + (n + 1) * 512], [B_XM[m]], [B_xmt[k]])
                    pap, pb = psum_for(512)
                    for j in range(JF):
                        mm(pap, wd_t[k][:, j, :], a_sb[:, j, :], j == 0, j == JF - 1, [B_wd[k], B_as[j]], [pb])
                    stt('dve', xm_t[k][:], pap, g2[:, m:m + 1], xm_t[k][:], ALU.mult, ALU.add, [pb, B_modT2, B_xmt[k]], [B_xmt[k]])
                    dma('sp', XO[m][:, n * 512:(n + 1) * 512], xm_t[k][:], [B_xmt[k]], [B_XO[m][n]])
                    if m == 0:
                        tt('dve', acc7s[n % 2][:], xm_t[k][:], xm_t[k][:], ALU.mult, [B_xmt[k]], [B_acc7s[n % 2]])
                    else:
                        tt('dve', sq7[:], xm_t[k][:], xm_t[k][:], ALU.mult, [B_xmt[k]], [B_sq7])
                        tt('dve', acc7s[n % 2][:], acc7s[n % 2][:], sq7[:], ALU.add, [B_sq7, B_acc7s[n % 2]], [B_acc7s[n % 2]])
                    if fin is not None and m >= 1:
                        advance(fin, 4)
                if fin is not None:
                    advance(fin, 1000)
                fin = fin_gen(n)
            advance(fin, 1000)
            P.barrier()

        P.emit()
    return nc


def relay(W, mw=128):
    K, N = W.shape
    return np.ascontiguousarray(W.reshape(K // 128, 128, N // mw, mw).transpose(2, 1, 0, 3))


def vecl(v):
    return np.ascontiguousarray(v.reshape(-1, 128).T)


def host_consts(cfg, is_second):
    NB, NBT = cfg.NB, cfg.NBT
    cst = np.zeros((128, 148), np.float32)
    cst[:, 146] = EPS
    i = np.arange(32)
    cst[0:32, 0] = (ROPE_THETA ** (-(2.0 * (i % 16)) / 32.0)).astype(np.float32)
    cst[:, 1] = 1.0 if is_second else 0.0
    pv = 0.0 if is_second else -1e30
    for ob in range(NB):
        eb = np.full(16, -1e30, np.float32)
        eb[0:NB] = pv
        eb[NB:NB + ob] = 0.0
        cst[:, 2 + 16 * ob:2 + 16 * ob + 16] = eb
    eb = np.full(16, -1e30, np.float32)
    eb[0:NB - 1] = pv
    cst[:, 2 + 16 * 8:2 + 16 * 8 + 16] = eb
    cm = np.zeros((128, 512), np.float32)
    cm[:, 0:128] = np.eye(128, dtype=np.float32)
    for r in range(16):
        cm[r + 16, 128 + r] = -1.0
        cm[r, 128 + 16 + r] = 1.0
    kk = np.arange(128)[:, None]
    qq = np.arange(128)[None, :]
    cm[:, 256:384] = np.where(qq >= kk, 0.0, NEG)
    cm[:, 384:512] = 1.0
    return cst, cm


def prepare_inputs(cfg, x, c, positions, w_ada, b_ada, g_mix, w_in, conv_w, w_attn_out, w_conv_out,
                   w_o, g_ffn, w_up, ffn_conv_w, w_down, g_final):
    D, NH, NE, HALO, AW, CW = cfg.D, cfg.NH, cfg.NE, cfg.HALO, cfg.AW, cfg.CW
    x = np.asarray(x, np.float32)
    w_in0 = np.asarray(w_in[0], np.float32)
    o = 0
    parts = {}
    for name, wdt in (("wq", AW), ("wk", AW), ("wv", AW), ("wcb", CW), ("wcc", CW), ("wcu", CW), ("wga", D), ("wgc", D)):
        parts[name] = relay(w_in0[:, o:o + wdt])
        o += wdt
    shared = dict(parts)
    shared["w_ada"] = np.ascontiguousarray(np.asarray(w_ada[0], np.float32))
    shared["b_ada"] = np.ascontiguousarray(np.asarray(b_ada[0], np.float32)[None, :])
    shared["vecs"] = np.ascontiguousarray(np.concatenate([vecl(np.asarray(g_mix[0], np.float32)),
                                                          vecl(np.asarray(g_ffn[0], np.float32)),
                                                          vecl(np.asarray(g_final, np.float32))], axis=1))
    cwl = np.asarray(conv_w[0], np.float32)
    shared["cw"] = np.ascontiguousarray(cwl.T.reshape(CW // 128, 128, 3).transpose(1, 0, 2).reshape(128, -1))
    shared["wao"] = relay(np.asarray(w_attn_out[0], np.float32))
    shared["wco"] = relay(np.asarray(w_conv_out[0], np.float32))
    shared["wo"] = relay(np.asarray(w_o[0], np.float32))
    shared["wup"] = relay(np.asarray(w_up[0], np.float32))
    fl = np.asarray(ffn_conv_w[0], np.float32)
    shared["fcw"] = np.ascontiguousarray(fl.T.reshape(-1, 128, 3).transpose(1, 0, 2).reshape(128, -1))
    shared["wdn"] = relay(np.asarray(w_down[0], np.float32))
    in_maps = []
    pos = np.asarray(positions, np.int32)
    for core in range(2 * cfg.B):
        b, half = divmod(core, 2)
        xT = x[b].T
        m = dict(shared)
        if half == 0:
            xe = np.zeros((D, NE), np.float32)
            xe[:, HALO:] = xT[:, 0:NH]
            pe = np.zeros((NE,), np.int32)
            pe[HALO:] = pos[b, 0:NH]
        else:
            xe = np.ascontiguousarray(xT[:, NH - HALO:2 * NH])
            pe = pos[b, NH - HALO:2 * NH]
        m["xe"] = xe
        m["xp"] = np.ascontiguousarray(xT[:, 0:NH])
        m["cT"] = vecl(np.asarray(c[b], np.float32))
        m["pos_e"] = np.ascontiguousarray(np.broadcast_to(pe[None, :], (32, NE)))
        m["pos_p"] = np.ascontiguousarray(np.broadcast_to(pos[b, 0:NH][None, :], (32, NH)))
        cst, cm = host_consts(cfg, half == 1)
        m["cst"] = cst
        m["cmat"] = cm
        in_maps.append(m)
    return in_maps


_NC_CACHE = {}


def run(cfg, inputs, debug=False):
    key = (cfg.D, cfg.S, cfg.B, cfg.H, cfg.DFF, debug)
    if key not in _NC_CACHE:
        _NC_CACHE[key] = build(cfg, debug)
    nc = _NC_CACHE[key]
    in_maps = prepare_inputs(cfg, **inputs)
    ncores = 2 * cfg.B
    res = run_bass_kernel_spmd(nc, in_maps, core_ids=list(range(ncores)))
    out = np.empty((cfg.B, cfg.S, cfg.D), np.float32)
    for core in range(ncores):
        b, half = divmod(core, 2)
        out[b, half * cfg.NH:(half + 1) * cfg.NH, :] = np.asarray(res.results[core]["outT"]).T
    return out, res


def kernel(**inputs):
    cfg = Cfg()
    out, _ = run(cfg, inputs)
    return out
```

```python
import contextlib
import math
import os
import numpy as np
import concourse.bass as bass
import concourse.mybir as mybir
from concourse.bass_utils import run_bass_kernel_spmd

F32 = mybir.dt.float32
BF16 = mybir.dt.bfloat16
I32 = mybir.dt.int32
ALU = mybir.AluOpType
AF = mybir.ActivationFunctionType
AX = mybir.AxisListType

QUEUES = ('pe', 'act', 'dve', 'pool', 'sp')
EPOCH = 20000
NSLOT = 8


class Buf:
    __slots__ = ('w', 'r', 'rd', 'name')

    def __init__(self, name=''):
        self.w = None
        self.r = {}
        self.rd = []
        self.name = name


class Op:
    __slots__ = ('eng', 'fn', 'deps', 'sig', 'idx', 'seq', 'dma', 'slot', 'slotval')

    def __init__(self, eng, fn, dma):
        self.eng = eng
        self.fn = fn
        self.deps = []
        self.sig = False
        self.idx = -1
        self.seq = -1
        self.dma = dma
        self.slot = -1
        self.slotval = 0


class Prog:
    def __init__(self, nc):
        self.nc = nc
        self.q = {k: [] for k in QUEUES}
        self.ndma = {k: 0 for k in QUEUES}
        self.pending_dma = []

    def add(self, eng, fn, reads=(), writes=(), dma=False):
        op = Op(eng, fn, dma)
        op.idx = len(self.q[eng])
        deps = {}

        def need(d):
            if d is None or d is op:
                return
            if d.dma:
                deps[id(d)] = d
                return
            if d.eng == 'pe' and eng == 'pe' and not dma:
                return
            k = d.eng
            if k not in deps or deps[k].idx < d.idx:
                deps[k] = d

        for b in reads:
            need(b.w)
        for b in writes:
            need(b.w)
            for r in b.r.values():
                need(r)
            for r in b.rd:
                need(r)
        for b in reads:
            if dma:
                b.rd.append(op)
            else:
                b.r[eng] = op
        for b in writes:
            b.w = op
            b.r = {}
            b.rd = []
        op.deps = list(deps.values())
        for d in op.deps:
            d.sig = True
        if dma:
            i = self.ndma[eng]
            self.ndma[eng] = i + 1
            op.slot = i % NSLOT
            op.slotval = 16 * (i // NSLOT + 1)
            op.sig = True
            self.pending_dma.append(op)
        self.q[eng].append(op)
        return op

    def wait_ops(self, eng, ops):
        op = Op(eng, None, False)
        op.idx = len(self.q[eng])
        op.deps = [o for o in ops if o is not None]
        for d in op.deps:
            d.sig = True
        self.q[eng].append(op)
        return op

    def barrier(self):
        lasts = []
        for k in QUEUES:
            for o in reversed(self.q[k]):
                if not o.dma and o.fn is not None:
                    lasts.append(o)
                    break
        dmas = list(self.pending_dma)
        self.pending_dma = []
        for k in QUEUES:
            self.wait_ops(k, [o for o in lasts if o.eng != k] + dmas)

    def emit(self):
        nc = self.nc
        nsig = {}
        for k in QUEUES:
            s = 0
            for o in self.q[k]:
                if o.sig and not o.dma and o.fn is not None:
                    s += 1
                    o.seq = s
            nsig[k] = s
        with contextlib.ExitStack() as st:
            csem = {}
            for k in QUEUES:
                n = (nsig[k] + EPOCH - 1) // EPOCH
                csem[k] = [st.enter_context(nc.semaphore(f"c_{k}_{i}")) for i in range(max(n, 1))]
            dsem = {}
            for k in QUEUES:
                if self.ndma[k]:
                    dsem[k] = [st.enter_context(nc.semaphore(f"d_{k}_{i}")) for i in range(NSLOT)]
            block = st.enter_context(nc.Block())

            def run(k):
                def body(e):
                    cw = {kk: 0 for kk in QUEUES}
                    dw = {}

                    def wait_for(d):
                        if d.dma:
                            key = (d.eng, d.slot)
                            if dw.get(key, 0) >= d.slotval:
                                return
                            dw[key] = d.slotval
                            e.wait_ge(dsem[d.eng][d.slot], d.slotval)
                        else:
                            if cw[d.eng] >= d.seq:
                                return
                            cw[d.eng] = d.seq
                            ep, v = divmod(d.seq - 1, EPOCH)
                            e.wait_ge(csem[d.eng][ep], v + 1)

                    for o in self.q[k]:
                        for d in o.deps:
                            wait_for(d)
                        if o.fn is None:
                            continue
                        if o.dma:
                            if o.slotval > 16:
                                key = (k, o.slot)
                                if dw.get(key, 0) < o.slotval - 16:
                                    dw[key] = o.slotval - 16
                                    e.wait_ge(dsem[k][o.slot], o.slotval - 16)
                            ins = o.fn(e)
                            ins.then_inc(dsem[k][o.slot], 16)
                        else:
                            ins = o.fn(e)
                            if o.sig:
                                ep, v = divmod(o.seq - 1, EPOCH)
                                ins.then_inc(csem[k][ep], 1)
                return body

            if self.q['sp']:
                block.sync(run('sp'))
            if self.q['pe']:
                block.tensor(run('pe'))
            if self.q['act']:
                block.scalar(run('act'))
            if self.q['dve']:
                block.vector(run('dve'))
            if self.q['pool']:
                block.gpsimd(run('pool'))


class Cfg:
    def __init__(s, D=4096, S=4096, B=4, H=16, DFF=14336):
        s.D, s.S, s.B, s.H, s.DFF = D, S, B, H, DFF
        s.AW = H * 128
        s.CW = H * 128
        s.KC = D // 128
        s.NH = S // 2
        s.HALO = 4
        s.NE = s.NH + s.HALO
        s.NB = s.NH // 256
        s.NT = s.NH // 512
        s.JF = DFF // 128
        s.NBT = 2 * s.NB
        s.INW = 3 * s.AW + 3 * s.CW + 2 * D


ROPE_THETA = 500000.0
EPS = 1e-6
NEG = -30000.0


class Ring:
    def __init__(self, items):
        self.items = items
        self.i = 0

    def next(self):
        it = self.items[self.i % len(self.items)]
        self.i += 1
        return it


def build(cfg, debug=False):
    c = cfg
    D, KC, NH, NE, HALO, H, NB, NT, JF, NBT = c.D, c.KC, c.NH, c.NE, c.HALO, c.H, c.NB, c.NT, c.JF, c.NBT
    HC = H
    nc = bass.Bass("TRN2", target_bir_lowering=False)
    P = Prog(nc)
    skind = "ExternalOutput" if debug else "Internal"

    def din(name, shape, dt=F32):
        return nc.dram_tensor(name, list(shape), dt, kind="ExternalInput").ap()

    def dscr(name, shape, dt):
        return nc.dram_tensor(name, list(shape), dt, kind=skind).ap()

    xe = din("xe", [D, NE])
    xp = din("xp", [D, NH])
    cT = din("cT", [128, KC])
    pos_e = din("pos_e", [32, NE], I32)
    pos_p = din("pos_p", [32, NH], I32)
    cst = din("cst", [128, 148])
    cmat = din("cmat", [128, 512])
    w_ada = din("w_ada", [D, 6 * D])
    b_ada = din("b_ada", [1, 6 * D])
    vecs = din("vecs", [128, 3 * KC])
    wq = din("wq", [H, 128, KC, 128])
    wk = din("wk", [H, 128, KC, 128])
    wv = din("wv", [H, 128, KC, 128])
    wcb = din("wcb", [HC, 128, KC, 128])
    wcc = din("wcc", [HC, 128, KC, 128])
    wcu = din("wcu", [HC, 128, KC, 128])
    wga = din("wga", [KC, 128, KC, 128])
    wgc = din("wgc", [KC, 128, KC, 128])
    cw = din("cw", [128, HC * 3])
    wao = din("wao", [KC, 128, HC, 128])
    wco = din("wco", [KC, 128, HC, 128])
    wo = din("wo", [KC, 128, KC, 128])
    wup = din("wup", [2 * JF, 128, KC, 128])
    fcw = din("fcw", [128, 2 * JF * 3])
    wdn = din("wdn", [KC, 128, JF, 128])
    outT = nc.dram_tensor("outT", [D, NH], F32, kind="ExternalOutput").ap()

    QT = dscr("QT", [H, 128, NE], BF16)
    KT = dscr("KT", [H, 128, 2 * NH], BF16)
    VV = dscr("VV", [H, 2 * NH, 128], BF16)
    CVT = dscr("CVT", [HC, 128, NE], BF16)
    SGA = dscr("SGA", [KC, 128, NE], BF16)
    SGC = dscr("SGC", [KC, 128, NE], BF16)
    MT = dscr("MT", [KC, 128, NE], BF16)
    XM = dscr("XM", [KC, 128, NE], F32)
    ATs = dscr("ATs", [JF, 128, NH], BF16)
    XO = dscr("XO", [KC, 128, NH], F32)
    WDB = dscr("WDB", [KC, 128, JF, 128], BF16)
    B_WDB = [Buf() for _ in range(KC)]
    ATTd = dscr("ATTd", [H, 128, NE], BF16) if debug else None
    B_QT = [Buf() for _ in range(H)]
    B_KT = [[Buf(), Buf()] for _ in range(H)]
    B_VV = [[Buf(), Buf()] for _ in range(H)]
    B_CVT = [Buf() for _ in range(HC)]
    B_SGA = [Buf() for _ in range(KC)]
    B_SGC = [Buf() for _ in range(KC)]
    B_MT = [Buf() for _ in range(KC)]
    B_XM = [Buf() for _ in range(KC)]
    B_ATs = [Buf() for _ in range(JF)]
    B_XO = [[Buf() for _ in range(NT)] for _ in range(KC)]
    B_OUT = Buf()

    def mm(out, lhsT, rhs, start, stop, r, w):
        return P.add('pe', lambda e: e.matmul(out, lhsT, rhs, start=start, stop=stop), r, w)

    def tr(out, in_, ident, r, w):
        return P.add('pe', lambda e: e.transpose(out, in_, ident), r, w)

    def act(out, in_, func, r, w, bias=None, scale=None, accum_out=None):
        kw = {}
        if bias is not None:
            kw['bias'] = bias
        if scale is not None:
            kw['scale'] = scale
        if accum_out is not None:
            kw['accum_out'] = accum_out
        return P.add('act', lambda e: e.activation(out=out, in_=in_, func=func, **kw), r, w)

    def ts(eng, out, in0, s1, s2, op0, op1, r, w):
        if op1 is None:
            return P.add(eng, lambda e: e.tensor_scalar(out=out, in0=in0, scalar1=s1, scalar2=None, op0=op0), r, w)
        return P.add(eng, lambda e: e.tensor_scalar(out=out, in0=in0, scalar1=s1, scalar2=s2, op0=op0, op1=op1), r, w)

    def tt(eng, out, in0, in1, op, r, w):
        return P.add(eng, lambda e: e.tensor_tensor(out=out, in0=in0, in1=in1, op=op), r, w)

    def stt(eng, out, in0, scalar, in1, op0, op1, r, w):
        eng = 'dve'
        return P.add(eng, lambda e: e.scalar_tensor_tensor(out=out, in0=in0, scalar=scalar, in1=in1, op0=op0, op1=op1), r, w)

    def cp(eng, out, in_, r, w):
        if eng == 'act':
            return P.add('act', lambda e: e.copy(out=out, in_=in_), r, w)
        return P.add(eng, lambda e: e.tensor_copy(out=out, in_=in_), r, w)

    def mset(eng, ap, val, w):
        return P.add(eng, lambda e: e.memset(ap, val), (), w)

    def dma(q, out, in_, r, w):
        return P.add(q, lambda e: e.dma_start(out=out, in_=in_), r, w, dma=True)

    tiles_ext = [(0, HALO)] + [(HALO + 512 * i, 512) for i in range(NT)]
    tiles_own = [(HALO + 512 * i, 512) for i in range(NT)]
    tiles_prev = [(512 * i, 512) for i in range(NT)]

    with contextlib.ExitStack() as gst:
        def sb(name, shape, dt, st=None):
            return (st or gst).enter_context(nc.sbuf_tensor(name, list(shape), dt))

        ps = [gst.enter_context(nc.psum_tensor(f"ps{i}", [128, 512], F32)) for i in range(8)]
        B_ps = [Buf(f"ps{i}") for i in range(8)]
        main_ring = Ring(list(range(7)))
        MINI = 16
        B_mini = [B_ps[7] for _ in range(512 // MINI)]
        mini_ring = Ring(list(range(512 // MINI)))

        busy_banks = set()

        held_banks = set()

        def main_bank():
            while True:
                i = main_ring.next()
                if i not in busy_banks and i not in held_banks:
                    return i

        def psum_for(n):
            if n <= MINI:
                i = mini_ring.next()
                return ps[7][:, i * MINI:i * MINI + n], B_mini[i]
            i = main_bank()
            return ps[i][:, 0:n], B_ps[i]

        cst_sb = sb("cst_sb", [128, 148], F32)
        cm_sb = sb("cm_sb", [128, 512], BF16)
        ident = cm_sb[:, 0:128]
        permT = cm_sb[:, 128:256]
        maskneg = cm_sb[:, 256:384]
        ones = cm_sb[:, 384:512]
        vec_sb = sb("vec_sb", [128, 3 * KC], F32)
        modT = sb("modT", [128, 6 * KC], F32)
        a1 = sb("a1", [128, KC], F32)
        a2 = sb("a2", [128, KC], F32)
        kmT = sb("kmT", [128, H, NBT], F32)
        kmTb = sb("kmTb", [128, H, NBT], BF16)
        B_cst, B_cm, B_vec, B_modT, B_a1, B_a2 = Buf(), Buf(), Buf(), Buf(), Buf(), Buf()
        ones32 = sb("ones32", [128, 128], F32)
        B_ones32 = Buf()
        B_km = [Buf() for _ in range(H)]
        B_kmb = [Buf() for _ in range(H)]
        WR = 3
        B_wr = [Buf() for _ in range(WR)]
        wring = Ring(list(range(WR)))

        invf = cst_sb[0:32, 0:1]
        hflag = cst_sb[:, 1:2]
        epsc = cst_sb[:, 146:147]

        def EB(idx):
            return cst_sb[:, 2 + 16 * idx: 2 + 16 * idx + NBT]

        s1 = modT[:, 0 * KC:1 * KC]
        g1 = modT[:, 2 * KC:3 * KC]
        s2 = modT[:, 3 * KC:4 * KC]
        g2 = modT[:, 5 * KC:6 * KC]

        mset('dve', ones32[:], 1.0, [B_ones32])
        dma('sp', cst_sb[:], cst, (), [B_cst])
        dma('pool', cm_sb[:], cmat, (), [B_cm])
        dma('sp', vec_sb[:], vecs, (), [B_vec])

        w_items = []
        for h in range(H):
            w_items += [(wk[h], KC), (wv[h], KC)]
        for h in range(H):
            w_items += [(wq[h], KC), (wk[h], KC), (wv[h], KC)]
        for j in range(HC):
            w_items += [(wcc[j], KC), (wcu[j], KC), (wcb[j], KC)]
        for m in range(KC):
            w_items.append((wga[m], KC))
        for m in range(KC):
            w_items.append((wgc[m], KC))
        for m in range(KC):
            w_items += [(wao[m], HC), (wco[m], HC)]
        for m in range(KC):
            w_items.append((wo[m], KC))
        for j in range(JF):
            w_items += [(wup[j], KC), (wup[JF + j], KC)]
        w_state = {'issued': 0, 'taken': 0, 'slot': {}}

        def load_w(wdram_m, kcn):
            t = w_state['taken']
            assert w_items[t][0] is wdram_m or True
            ahead = int(os.environ.get("KPREF", "3"))
            while w_state['issued'] < min(t + ahead, len(w_items)):
                ii = w_state['issued']
                ap_, kn_ = w_items[ii]
                i = wring.next()
                dma('pool', wring_t[i][:, 0:kn_, :], ap_, (), [B_wr[i]])
                w_state['slot'][ii] = i
                w_state['issued'] = ii + 1
            i = w_state['slot'][t]
            assert w_items[t][1] == kcn
            w_state['taken'] = t + 1
            return wring_t[i], B_wr[i]

        pending = []

        def fill_step():
            while pending:
                try:
                    next(pending[0])
                    return
                except StopIteration:
                    pending.pop(0)

        def after_proj(post):
            if post is None:
                while pending:
                    fill_step()
                return
            pending.append(post())

        def proj(wt, wb, kcn, src, srcb, tiles, evac):
            regs = [psum_for(n) for (_, n) in tiles]
            order = sorted(range(len(tiles)), key=lambda i: (tiles[i][1] <= MINI, i))
            for (pap_, pb_) in regs:
                for bi_ in range(7):
                    if pb_ is B_ps[bi_]:
                        busy_banks.add(bi_)
            for kc in range(kcn):
                for i in order:
                    (c0, n), (pap, pb) = tiles[i], regs[i]
                    mm(pap, wt[:, kc, :], src[:, kc, c0:c0 + n], kc == 0, kc == kcn - 1, [wb, srcb[kc]], [pb])
                if kc >= 2:
                    fill_step()
            while pending:
                fill_step()
            busy_banks.clear()
            for i in ([order[-1]] + order[:-1]) if tiles[order[-1]][1] <= MINI else order:
                (c0, n), (pap, pb) = tiles[i], regs[i]
                evac(c0, n, pap, pb)

        c_act = sb("c_act", [128, KC], BF16)
        one11 = sb("one11", [1, 1], F32)
        B_wa2 = [Buf() for _ in range(3)]
        B_brow2, B_mrow2, B_modT2 = Buf(), Buf(), Buf()
        p0b_bufs = {}
        B_ca, B_one = Buf(), Buf()
        wst = contextlib.ExitStack()
        wring_t = [sb(f"wr{i}", [128, max(KC, HC), 128], BF16, wst) for i in range(WR)]
        with contextlib.ExitStack() as st:
            c_sb = sb("c_sb", [128, KC], F32, st)
            CWID = 2048
            wa_t = [sb(f"wa{i}", [128, CWID], BF16, st) for i in range(3)]
            B_wa = [Buf() for _ in range(3)]
            wa_ring = Ring([0, 1, 2])
            brow = sb("brow", [1, CWID], F32, st)
            mrow = sb("mrow", [1, CWID], F32, st)
            B_c, B_brow, B_mrow = Buf(), Buf(), Buf()
            dma('sp', c_sb[:], cT, (), [B_c])
            act(c_act[:], c_sb[:], AF.Silu, [B_c], [B_ca])
            mset('dve', one11[:], 1.0, [B_one])
            ncol = 2 * D
            modps = ps[7][:, 0:2 * KC]
            B_modps = B_ps[7]
            for c0 in range(0, ncol, CWID):
                wdt = min(CWID, ncol - c0)
                nb_ = (wdt + 511) // 512
                banks = [main_ring.next() for _ in range(nb_)]
                dma('sp', brow[:, 0:wdt], b_ada[:, c0:c0 + wdt], (), [B_brow])
                for kc in range(KC):
                    i = wa_ring.next()
                    dma('pool', wa_t[i][:, 0:wdt], w_ada[kc * 128:(kc + 1) * 128, c0:c0 + wdt], (), [B_wa[i]])
                    for bi, bk in enumerate(banks):
                        n = min(512, wdt - bi * 512)
                        mm(ps[bk][0:1, 0:n], c_act[:, kc:kc + 1], wa_t[i][:, bi * 512:bi * 512 + n],
                           kc == 0, kc == KC - 1, [B_ca, B_wa[i]], [B_ps[bk]])
                for bi, bk in enumerate(banks):
                    n = min(512, wdt - bi * 512)
                    tt('dve', mrow[:, bi * 512:bi * 512 + n], ps[bk][0:1, 0:n], brow[:, bi * 512:bi * 512 + n], ALU.add,
                       [B_ps[bk], B_brow], [B_mrow])
                for j in range(wdt // 128):
                    col = (c0 // 128) + j
                    mm(modps[:, col:col + 1], mrow[0:1, j * 128:(j + 1) * 128], one11[0:1, 0:1], True, True,
                       [B_mrow, B_one], [B_modps])
            cp('dve', modT[:, 0:2 * KC], modps, [B_modps], [B_modT])
            stt('dve', a1[:], modT[:, 1 * KC:2 * KC], 1.0, vec_sb[:, 0:KC], ALU.add, ALU.mult, [B_modT, B_vec], [B_a1])
            P.barrier()

        def p0b_gen():
            wa2_t, brow2, mrow2 = p0b_bufs['wa2'], p0b_bufs['brow2'], p0b_bufs['mrow2']
            G = min(4, KC)
            grp = [(c0, k0) for c0 in range(2 * D, 6 * D, 512) for k0 in range(0, KC, G)]
            NR = len(wa2_t)

            def issue(gidx):
                if gidx < len(grp):
                    c0_, k0_ = grp[gidx]
                    i_ = gidx % NR
                    dma('pool', wa2_t[i_][:], w_ada[k0_ * 128:(k0_ + G) * 128, c0_:c0_ + 512].rearrange("(k p) c -> p k c", p=128),
                        (), [B_wa2[i_]])
            for g0 in range(NR - 1):
                issue(g0)
            for gidx, (c0, k0) in enumerate(grp):
                issue(gidx + NR - 1)
                i = gidx % NR
                if k0 == 0:
                    dma('sp', brow2[:], b_ada[:, c0:c0 + 512], (), [B_brow2])
                for kk in range(G):
                    kc = k0 + kk
                    mm(ps[3][0:1, 0:512], c_act[:, kc:kc + 1], wa2_t[i][:, kk, :], kc == 0, kc == KC - 1, [B_ca, B_wa2[i]], [B_ps[3]])
                    yield
                if k0 + G >= KC:
                    tt('dve', mrow2[:], ps[3][0:1, 0:512], brow2[:], ALU.add, [B_ps[3], B_brow2], [B_mrow2])
                    for q in range(4):
                        mm(ps[7][:, 260 + q:261 + q], mrow2[0:1, q * 128:(q + 1) * 128], one11[0:1, 0:1], True, True,
                           [B_mrow2, B_one], [B_ps[7]])
                    col = c0 // 128
                    cp('dve', modT[:, col:col + 4], ps[7][:, 260:264], [B_ps[7]], [B_modT2])
                    yield

        def make_tables(st, pos_dram, N, name, tiles):
            cosT = sb(name + "cos", [32, N], F32, st)
            sinT = sb(name + "sin", [32, N], F32, st)
            B_cos, B_sin = Buf(), Buf()
            C1 = 6.28125
            C2 = 2 * math.pi - C1
            PI_LO = 3.1415925
            with contextlib.ExitStack() as st2:
                pi_ = sb(name + "pi", [32, 512], I32, st2)
                ang = sb(name + "ang", [32, 512], F32, st2)
                aa = sb(name + "aa", [32, 512], F32, st2)
                tq = sb(name + "tq", [32, 512], F32, st2)
                B_pi, B_ang, B_aa, B_tq = Buf(), Buf(), Buf(), Buf()
                for (c0, n) in tiles:
                    dma('sp', pi_[:, 0:n], pos_dram[:, c0:c0 + n], (), [B_pi])
                    cp('dve', ang[:, 0:n], pi_[:, 0:n], [B_pi], [B_ang])
                    ts('dve', ang[:, 0:n], ang[:, 0:n], invf, None, ALU.mult, None, [B_ang, B_cst], [B_ang])
                    for (dst, B_dst, shift) in ((sinT, B_sin, 0.0), (cosT, B_cos, math.pi / 2)):
                        ts('dve', aa[:, 0:n], ang[:, 0:n], shift, None, ALU.add, None, [B_ang], [B_aa])
                        ts('dve', tq[:, 0:n], aa[:, 0:n], 1.0 / (2 * math.pi), None, ALU.mult, None, [B_aa], [B_tq])
                        cp('dve', pi_[:, 0:n], tq[:, 0:n], [B_tq], [B_pi])
                        cp('dve', tq[:, 0:n], pi_[:, 0:n], [B_pi], [B_tq])
                        stt('dve', aa[:, 0:n], tq[:, 0:n], -C1, aa[:, 0:n], ALU.mult, ALU.add, [B_tq, B_aa], [B_aa])
                        stt('dve', aa[:, 0:n], tq[:, 0:n], -C2, aa[:, 0:n], ALU.mult, ALU.add, [B_tq, B_aa], [B_aa])
                        ts('dve', aa[:, 0:n], aa[:, 0:n], -PI_LO, PI_LO, ALU.max, ALU.min, [B_aa], [B_aa])
                        act(dst[:, c0:c0 + n], aa[:, 0:n], AF.Sin, [B_aa], [B_dst])
                P.barrier()
            return cosT, sinT, B_cos, B_sin

        def make_h(st, x_dram, N, tiles, a_vec, B_a, s_vec, name, src_is_scratch_bufs=None, rstd_out=None):
            hT = sb(name + "hT", [128, KC, N], BF16, st)
            B_h = [Buf() for _ in range(KC)]
            with contextlib.ExitStack() as st2:
                xs = [sb(name + f"xs{i}", [128, N], F32, st2) for i in range(3)]
                B_xs = [Buf(), Buf(), Buf()]
                sq = [sb(name + f"sq{i}", [128, N], BF16, st2) for i in range(2)]
                B_sq = [Buf(), Buf()]
                rstd = sb(name + "rstd", [128, N], F32, st2)
                B_rstd = Buf()
                regs = [psum_for(n) for (_, n) in tiles]
                for kc in range(KC):
                    i = kc % 3
                    j2 = kc % 2
                    rb = [src_is_scratch_bufs[kc]] if src_is_scratch_bufs else []
                    dma('sp' if kc % 2 == 0 else 'pool', xs[i][:], x_dram[kc], rb, [B_xs[i]])
                    act(sq[j2][:], xs[i][:], AF.Square, [B_xs[i]], [B_sq[j2]])
                    for (c0, n), (pap, pb) in zip(tiles, regs):
                        mm(pap, ones, sq[j2][:, c0:c0 + n], kc == 0, kc == KC - 1, [B_cm, B_sq[j2]], [pb])
                for (c0, n), (pap, pb) in zip(tiles, regs):
                    act(rstd[:, c0:c0 + n], pap, AF.Sqrt, [pb, B_cst], [B_rstd], bias=epsc, scale=1.0 / D)
                P.add('dve', lambda e: e.reciprocal(out=rstd[:], in_=rstd[:]), [B_rstd], [B_rstd])
                for kc in range(KC):
                    i = (kc + KC) % 3
                    rb = [src_is_scratch_bufs[kc]] if src_is_scratch_bufs else []
                    dma('sp' if kc % 2 == 0 else 'pool', xs[i][:], x_dram[kc], rb, [B_xs[i]])
                    stt('dve', xs[i][:], xs[i][:], a_vec[:, kc:kc + 1], rstd[:], ALU.mult, ALU.mult,
                        [B_xs[i], B_a, B_rstd], [B_xs[i]])
                    act(hT[:, kc, :], xs[i][:], AF.Identity, [B_xs[i], B_modT], [B_h[kc]], bias=s_vec[:, kc:kc + 1])
                P.barrier()
            return hT, B_h

        def rope_gen(stg, B_st, tiles, src_c0, cosT, sinT, B_cos, B_sin, rt, B_rt):
            for ti, (c0, n) in enumerate(tiles):
                pap, pb = psum_for(n)
                d0 = c0 - src_c0
                mm(pap, permT, stg[:, d0:d0 + n], True, True, [B_cm] + B_st(ti), [pb])
                k = ti % 2
                cp('act', rt[k][:, 0:n], pap[0:32, :], [pb], [B_rt[k]])
                tt('dve', rt[k][:, 0:n], rt[k][:, 0:n], sinT[:, c0:c0 + n], ALU.mult, [B_rt[k], B_sin], [B_rt[k]])
                tt('dve', rt[k + 2][:, 0:n], stg[0:32, d0:d0 + n], cosT[:, c0:c0 + n], ALU.mult, B_st(ti) + [B_cos], [B_rt[k + 2]])
                tt('dve', stg[0:32, d0:d0 + n], rt[k][:, 0:n], rt[k + 2][:, 0:n], ALU.add, [B_rt[k], B_rt[k + 2]], B_st(ti))
                yield

        def head_kv(st_tiles, hT, B_h, cosT, sinT, B_cos, B_sin, h, col0, half, stg_ring, vt_ring, rt, B_rt, tiles, src_c0):
            wt, wb = load_w(wk[h], KC)
            kst, B_kst = stg_ring.next()
            tix = {c0: ti for ti, (c0, n) in enumerate(tiles)}

            def evk(c0, n, pap, pb):
                cp('act', kst[:, c0 - src_c0:c0 - src_c0 + n], pap, [pb], [B_kst[tix[c0]], B_kst[tix[c0] + 1]])
            proj(wt, wb, KC, hT, B_h, tiles, evk)

            def post_k():
                yield from rope_gen(kst, lambda ti: [B_kst[ti], B_kst[ti + 1]], tiles, src_c0, cosT, sinT, B_cos, B_sin, rt, B_rt)
                nt_ = 5
                P.add('dve', lambda e: e.tensor_reduce(out=kmT[:, h, half * NB:(half + 1) * NB],
                                                       in_=kst[:, 0:NH].rearrange("p (b k) -> p b k", k=256),
                                                       axis=AX.X, op=ALU.add), B_kst[0:nt_], [B_km[h]])
                ts('dve', kmT[:, h, half * NB:(half + 1) * NB], kmT[:, h, half * NB:(half + 1) * NB], 1.0 / 256, None,
                   ALU.mult, None, [B_km[h]], [B_km[h]])
                dma('sp', KT[h][:, col0:col0 + NH], kst[:, 0:NH], B_kst[0:nt_], [B_KT[h][half]])
                if half == 1:
                    cp('dve', kmTb[:, h, :], kmT[:, h, :], [B_km[h]], [B_kmb[h]])
                yield
            after_proj(post_k)
            wt, wb = load_w(wv[h], KC)
            vst, B_vst = stg_ring.next()

            def evv(c0, n, pap, pb):
                cp('act', vst[:, c0 - src_c0:c0 - src_c0 + n], pap, [pb], [B_vst[tix[c0]], B_vst[tix[c0] + 1]])
            proj(wt, wb, KC, hT, B_h, tiles, evv)

            def post_v():
                vt, B_vt = vt_ring.next()
                for g in range(NH // 1024):
                    i = main_bank()
                    held_banks.add(i)
                    pb16 = ps[i].bitcast(BF16)
                    for t in range(8):
                        tok = g * 1024 + t * 128
                        tr(pb16[:, t * 128:(t + 1) * 128], vst[:, tok:tok + 128], ident, [B_vst[tok // 512], B_vst[tok // 512 + 1], B_cm], [B_ps[i]])
                        if t % 2 == 1 and t < 7:
                            yield
                    cp('dve', vt[:, g * 8:(g + 1) * 8, :], pb16[:, 0:1024].rearrange("p (t d) -> p t d", d=128), [B_ps[i]], [B_vt])
                    held_banks.discard(i)
                    yield
                dma('sp', VV[h][col0:col0 + NH, :].rearrange("(t p) d -> p t d", p=128), vt[:], [B_vt], [B_VV[h][half]])
                yield
            after_proj(post_v)

        with contextlib.ExitStack() as st:
            xp_c = [xp[kc * 128:(kc + 1) * 128, :] for kc in range(KC)]
            hTp, B_hp = make_h(st, xp_c, NH, tiles_prev, a1, B_a1, s1, "p1")
            cosP, sinP, B_cosP, B_sinP = make_tables(st, pos_p, NH, "tp", tiles_prev)
            stg = [sb(f"p1stg{i}", [128, NH], BF16, st) for i in range(2)]
            stg_ring = Ring([(stg[i], [Buf() for _ in range(5)]) for i in range(2)])
            vts = [sb(f"p1vt{i}", [128, NH // 128, 128], BF16, st) for i in range(2)]
            vt_ring = Ring([(vts[i], Buf()) for i in range(2)])
            rt = [sb(f"p1rt{i}", [32, 512], F32, st) for i in range(4)]
            B_rt = [Buf() for _ in range(4)]
            for h in range(H):
                head_kv(st, hTp, B_hp, cosP, sinP, B_cosP, B_sinP, h, 0, 0, stg_ring, vt_ring, rt, B_rt, tiles_prev, 0)
            after_proj(None)
            P.barrier()

        with contextlib.ExitStack() as st:
            xe_c = [xe[kc * 128:(kc + 1) * 128, :] for kc in range(KC)]
            hTe, B_he = make_h(st, xe_c, NE, tiles_ext, a1, B_a1, s1, "p2")
            with contextlib.ExitStack() as st2:
                cosE, sinE, B_cosE, B_sinE = make_tables(st2, pos_e, NE, "te", tiles_ext)
                stg = [sb(f"p2stg{i}", [128, NE], BF16, st2) for i in range(2)]
                stg_ring = Ring([(stg[i], [Buf() for _ in range(5)]) for i in range(2)])
                vts = [sb(f"p2vt{i}", [128, NH // 128, 128], BF16, st2) for i in range(2)]
                vt_ring = Ring([(vts[i], Buf()) for i in range(2)])
                rt = [sb(f"p2rt{i}", [32, 512], F32, st2) for i in range(4)]
                B_rt = [Buf() for _ in range(4)]
                for h in range(H):
                    wt, wb = load_w(wq[h], KC)
                    qst, B_qst = stg_ring.next()

                    tixe = {c0: ti for ti, (c0, n) in enumerate(tiles_ext)}

                    def evq(c0, n, pap, pb, qst=qst, B_qst=B_qst):
                        cp('act', qst[:, c0:c0 + n], pap, [pb], [B_qst[tixe[c0]]])
                    proj(wt, wb, KC, hTe, B_he, tiles_ext, evq)

                    def post_q(qst=qst, B_qst=B_qst, h=h):
                        yield from rope_gen(qst, lambda ti: [B_qst[ti]], tiles_ext, 0, cosE, sinE, B_cosE, B_sinE, rt, B_rt)
                        dma('sp', QT[h], qst[:], B_qst[0:5], [B_QT[h]])
                        yield
                    after_proj(post_q)
                    head_kv(st2, hTe, B_he, cosE, sinE, B_cosE, B_sinE, h, NH, 1, stg_ring, vt_ring, rt, B_rt, tiles_own, HALO)
                after_proj(None)
                P.barrier()
            with contextlib.ExitStack() as st2:
                ccst = sb("ccst", [128, NE], F32, st2)
                ust = sb("ust", [128, NE], F32, st2)
                yst = sb("yst", [128, NE], F32, st2)
                cvst = [sb(f"cvst{i}", [128, NE], BF16, st2) for i in range(2)]
                cw_sb = sb("cw_sb", [128, HC * 3], F32, st2)
                B_cc, B_u, B_y, B_cw = Buf(), Buf(), Buf(), Buf()
                B_cv = [Buf(), Buf()]
                dma('sp', cw_sb[:], cw, (), [B_cw])
                mset('dve', yst[:], 0.0, [B_y])
                for j in range(HC):
                    wt, wb = load_w(wcc[j], KC)
                    proj(wt, wb, KC, hTe, B_he, tiles_ext,
                         lambda c0, n, pap, pb: cp('act', ccst[:, c0:c0 + n], pap, [pb], [B_cc]))
                    wt, wb = load_w(wcu[j], KC)
                    proj(wt, wb, KC, hTe, B_he, tiles_ext,
                         lambda c0, n, pap, pb: tt('dve', ust[:, c0:c0 + n], pap, ccst[:, c0:c0 + n], ALU.mult, [pb, B_cc], [B_u]))
                    ts('dve', ust[:, 0:HALO], ust[:, 0:HALO], hflag, None, ALU.mult, None, [B_u, B_cst], [B_u])
                    ts('dve', yst[:, 2:NE], ust[:, 2:NE], cw_sb[:, 3 * j + 2:3 * j + 3], None, ALU.mult, None, [B_u, B_cw], [B_y])
                    stt('pool', yst[:, 2:NE], ust[:, 1:NE - 1], cw_sb[:, 3 * j + 1:3 * j + 2], yst[:, 2:NE], ALU.mult, ALU.add,
                        [B_u, B_cw, B_y], [B_y])
                    stt('pool', yst[:, 2:NE], ust[:, 0:NE - 2], cw_sb[:, 3 * j:3 * j + 1], yst[:, 2:NE], ALU.mult, ALU.add,
                        [B_u, B_cw, B_y], [B_y])
                    wt, wb = load_w(wcb[j], KC)
                    k = j % 2
                    proj(wt, wb, KC, hTe, B_he, tiles_ext,
                         lambda c0, n, pap, pb, k=k: tt('dve', cvst[k][:, c0:c0 + n], pap, yst[:, c0:c0 + n], ALU.mult, [pb, B_y], [B_cv[k]]))
                    dma('sp', CVT[j], cvst[k][:], [B_cv[k]], [B_CVT[j]])
                P.barrier()
            with contextlib.ExitStack() as st2:
                gst_ = [sb(f"gst{i}", [128, NE], BF16, st2) for i in range(2)]
                B_g = [Buf(), Buf()]
                idx = 0
                for (wd_, dst, B_dst) in ((wga, SGA, B_SGA), (wgc, SGC, B_SGC)):
                    for m in range(KC):
                        wt, wb = load_w(wd_[m], KC)
                        k = idx % 2
                        idx += 1
                        proj(wt, wb, KC, hTe, B_he, tiles_ext,
                             lambda c0, n, pap, pb, k=k: act(gst_[k][:, c0:c0 + n], pap, AF.Sigmoid, [pb], [B_g[k]]))
                        dma('sp', dst[m], gst_[k][:], [B_g[k]], [B_dst[m]])
                P.barrier()

        st34 = contextlib.ExitStack()
        attT = sb("attT", [128, H, NE], BF16, st34)
        B_att = [Buf() for _ in range(H)]
        with contextlib.ExitStack() as st:
            NKT = 2 * NH // 128
            q_t = [sb(f"q_t{i}", [128, NE], BF16, st) for i in range(2)]
            k_t = [sb(f"k_t{i}", [128, 2 * NH], BF16, st) for i in range(2)]
            v_t = [sb(f"v_t{i}", [128, NKT, 129], BF16, st) for i in range(2)]
            B_q = [Buf(), Buf()]
            B_k = [Buf(), Buf()]
            B_v = [Buf(), Buf()]
            NPT = 4
            pt_t = [sb(f"pt{i}", [128, 512], BF16, st) for i in range(NPT)]
            B_pt = [Buf() for _ in range(NPT)]
            pt_ring = Ring(list(range(NPT)))
            acc = [sb(f"acc{i}", [128, 4, 129], F32, st) for i in range(2)]
            B_acc = [[Buf() for _ in range(4)] for _ in range(2)]
            gm = sb("gm", [128, 4, NBT], F32, st)
            m8 = sb("m8", [128, 4, 8], F32, st)
            sel = [sb(f"sel{i}", [128, 4, NBT], F32, st) for i in range(2)]
            B_gm, B_m8 = Buf(), Buf()
            B_sel = [Buf(), Buf()]
            rcp = sb("rcp", [128, 8], F32, st)
            B_rcp = Buf()
            onrm = [sb(f"onrm{i}", [128, 128], BF16, st) for i in range(2)]
            B_on = [Buf(), Buf()]
            S_ring = Ring([0, 1, 2])
            O_ring = Ring([(4, 5), (6, 7)])
            for i in range(2):
                mset('dve', v_t[i][:, :, 128:129], 1.0, [B_v[i]])
            SCALE = 1.0 / math.sqrt(128.0)
            grp_i = 0
            p0b_bufs['wa2'] = [sb(f"wa2_{i}", [128, min(4, KC), 512], BF16, st) for i in range(3)]
            p0b_bufs['brow2'] = sb("brow2", [1, 512], F32, st)
            p0b_bufs['mrow2'] = sb("mrow2", [1, 512], F32, st)
            p0b = p0b_gen()
            p0b_total = (4 * D // 512) * (KC + 1)
            p0b_state = {'done': 0, 'blocks': 0}
            blocks_total = H * sum([NB] + [NB + 2 * g + 2 for g in range(NT)])

            def p0b_advance(final=False):
                p0b_state['blocks'] += 1
                target = p0b_total if final else min(p0b_total, (p0b_state['blocks'] * p0b_total) // blocks_total + 2)
                while p0b_state['done'] < target:
                    try:
                        next(p0b)
                    except StopIteration:
                        p0b_state['done'] = p0b_total
                        break
                    p0b_state['done'] += 1
            def group_info(gkind, qc0, qn):
                if gkind == "halo":
                    return [(qc0, qn)], -1, list(range(NB - 1)), [(NB - 1, ["diag_hi"])]
                g = (qc0 - HALO) // 512
                return ([(qc0 + 128 * i, 128) for i in range(4)], g, list(range(NB + 2 * g)),
                        [(NB + 2 * g, ["diag_lo", "diag_hi", "past", "past"]),
                         (NB + 2 * g + 1, ["none", "none", "diag_lo", "diag_hi"])])

            def load_head(h):
                hb = h % 2
                dma('sp', q_t[hb][:], QT[h], [B_QT[h]], [B_q[hb]])
                dma('sp', k_t[hb][:], KT[h], [B_KT[h][0], B_KT[h][1]], [B_k[hb]])
                dma('sp', v_t[hb][:, :, 0:128], VV[h].rearrange("(t p) d -> p t d", p=128), [B_VV[h][0], B_VV[h][1]], [B_v[hb]])

            def emit_sel(w, gi):
                h, gkind, qc0, qn = w
                hb = h % 2
                qh = q_t[hb]
                qtiles, g, _, _ = group_info(gkind, qc0, qn)
                gpap = ps[7][:, 320:320 + 4 * NBT]
                for i, (tq0, tn) in enumerate(qtiles):
                    mm(gpap[0:tn, i * NBT:(i + 1) * NBT], qh[:, tq0:tq0 + tn], kmTb[:, h, :], True, True,
                       [B_q[hb], B_kmb[h]], [B_ps[7]])
                for i, (tq0, tn) in enumerate(qtiles):
                    ebi = 8 if gkind == "halo" else (2 * g + i // 2)
                    tt('dve', gm[0:tn, i, :], gpap[0:tn, i * NBT:(i + 1) * NBT], EB(ebi)[0:tn, :], ALU.add,
                       [B_ps[7], B_cst], [B_gm])
                    P.add('dve', lambda e, i=i, tn=tn: e.max(out=m8[0:tn, i, :], in_=gm[0:tn, i, :]), [B_gm], [B_m8])
                    ts('dve', m8[0:tn, i, 2:3], m8[0:tn, i, 2:3], -1e29, None, ALU.max, None, [B_m8], [B_m8])
                    ts('dve', sel[gi][0:tn, i, :], gm[0:tn, i, :], m8[0:tn, i, 2:3], None, ALU.is_ge, None,
                       [B_gm, B_m8], [B_sel[gi]])

            def emit_final(w, gi):
                h, gkind, qc0, qn = w
                qtiles, _, _, _ = group_info(gkind, qc0, qn)
                for i, (tq0, tn) in enumerate(qtiles):
                    P.add('dve', lambda e, i=i, tn=tn, gi=gi: e.reciprocal(out=rcp[0:tn, gi * 4 + i:gi * 4 + i + 1], in_=acc[gi][0:tn, i, 128:129]),
                          [B_acc[gi][i]], [B_rcp])
                    k = i % 2
                    ts('dve', onrm[k][0:tn, :], acc[gi][0:tn, i, 0:128], rcp[0:tn, gi * 4 + i:gi * 4 + i + 1], None, ALU.mult, None,
                       [B_acc[gi][i], B_rcp], [B_on[k]])
                    tp16 = ps[7].bitcast(BF16)
                    tpo = tp16[:, 768 + k * 128: 768 + k * 128 + tn]
                    tr(tpo, onrm[k][0:tn, :], ident[0:tn, 0:tn], [B_on[k], B_cm], [B_ps[7]])
                    cp('act', attT[:, h, tq0:tq0 + tn], tpo, [B_ps[7]], [B_att[h]])
                if debug and gkind == "own" and qc0 == HALO + 512 * (NT - 1):
                    dma('sp', ATTd[h], attT[:, h, :], [B_att[h]], [Buf()])

            groups = [("halo", 0, HALO)] + [("own", HALO + 512 * g, 512) for g in range(NT)]
            work = [(h, gk, qc0, qn) for h in range(H) for (gk, qc0, qn) in groups]
            load_head(0)
            emit_sel(work[0], 0)
            for widx, w in enumerate(work):
                h, gkind, qc0, qn = w
                hb = h % 2
                qh, kh, vh = q_t[hb], k_t[hb], v_t[hb]
                gi = widx % 2
                if True:
                    qtiles, g, past, specials = group_info(gkind, qc0, qn)
                    nqt = len(qtiles)
                    first = [True] * nqt

                    def stage_a(j, modes):
                        pts = []
                        for kt in range(2):
                            ktile = 2 * j + kt
                            need_q = []
                            for i, md in enumerate(modes):
                                if md == "past":
                                    need_q.append(i)
                                elif md == "diag_lo" and kt == 0:
                                    need_q.append(i)
                                elif md == "diag_hi":
                                    need_q.append(i)
                            if not need_q:
                                pts.append(None)
                                continue
                            q_lo = qtiles[need_q[0]][0]
                            q_hi = qtiles[need_q[-1]][0] + qtiles[need_q[-1]][1]
                            sb_i = S_ring.next()
                            sp_ = ps[sb_i]
                            off = q_lo - qc0
                            diag_q = [i for i in need_q if (modes[i] == "diag_lo" and kt == 0) or (modes[i] == "diag_hi" and kt == 1)]
                            mm(sp_[:, off:off + (q_hi - q_lo)], kh[:, ktile * 128:(ktile + 1) * 128], qh[:, q_lo:q_hi],
                               True, len(diag_q) == 0, [B_k[hb], B_q[hb]], [B_ps[sb_i]])
                            for di, i in enumerate(diag_q):
                                tq0, tn = qtiles[i]
                                if gkind == "halo":
                                    mrhs = maskneg[:, 128 - HALO:128]
                                else:
                                    mrhs = maskneg[:, 0:tn]
                                mm(sp_[:, tq0 - qc0:tq0 - qc0 + tn], ident, mrhs, False, di == len(diag_q) - 1,
                                   [B_cm], [B_ps[sb_i]])
                            pi = pt_ring.next()
                            act(pt_t[pi][:, off:off + (q_hi - q_lo)], sp_[:, off:off + (q_hi - q_lo)], AF.Exp,
                                [B_ps[sb_i]], [B_pt[pi]], scale=SCALE)
                            pts.append((pi, need_q))
                        return pts

                    def stage_b(j, modes, pts):
                        obanks = O_ring.next()
                        for i, (tq0, tn) in enumerate(qtiles):
                            md = modes[i]
                            if md == "none":
                                continue
                            kts = [kt for kt in range(2) if pts[kt] is not None and i in pts[kt][1]]
                            ob = obanks[i // 2]
                            oc = (i % 2) * 129
                            for n_, kt in enumerate(kts):
                                pi = pts[kt][0]
                                mm(ps[ob][0:tn, oc:oc + 129], pt_t[pi][:, tq0 - qc0:tq0 - qc0 + tn], vh[:, 2 * j + kt, :],
                                   n_ == 0, n_ == len(kts) - 1, [B_pt[pi], B_v[hb]], [B_ps[ob]])
                        for i, (tq0, tn) in enumerate(qtiles):
                            md = modes[i]
                            if md == "none":
                                continue
                            ob = obanks[i // 2]
                            oc = (i % 2) * 129
                            if md == "past":
                                sc = sel[gi][0:tn, i, j:j + 1]
                                rd = [B_ps[ob], B_sel[gi]]
                            else:
                                sc = 1.0
                                rd = [B_ps[ob]]
                            if first[i]:
                                ts('dve', acc[gi][0:tn, i, :], ps[ob][0:tn, oc:oc + 129], sc, None, ALU.mult, None,
                                   rd, [B_acc[gi][i]])
                                first[i] = False
                            else:
                                stt('dve', acc[gi][0:tn, i, :], ps[ob][0:tn, oc:oc + 129], sc, acc[gi][0:tn, i, :], ALU.mult, ALU.add,
                                    rd + [B_acc[gi][i]], [B_acc[gi][i]])

                    blocks = [(j, ["past"] * nqt) for j in past] + list(specials)
                    prev = None
                    for bi, (j, modes) in enumerate(blocks):
                        pts_ = stage_a(j, modes)
                        p0b_advance()
                        if bi == 0 and widx > 0:
                            emit_final(work[widx - 1], 1 - gi)
                        if bi == 0 and gkind == "halo" and h + 1 < H:
                            load_head(h + 1)
                        if bi == min(1, len(blocks) - 1) and widx + 1 < len(work):
                            emit_sel(work[widx + 1], 1 - gi)
                        if prev is not None:
                            stage_b(*prev)
                        prev = (j, modes, pts_)
                    stage_b(*prev)
            emit_final(work[-1], (len(work) - 1) % 2)
            p0b_advance(final=True)
            for _ in p0b:
                pass
            stt('dve', a2[:], modT[:, 4 * KC:5 * KC], 1.0, vec_sb[:, KC:2 * KC], ALU.add, ALU.mult, [B_modT2, B_vec], [B_a2])
            P.barrier()

        with contextlib.ExitStack() as st:
            cv_sb = sb("cv_sb", [128, HC, NE], BF16, st)
            B_cvs = [Buf() for _ in range(HC)]
            for j in range(HC):
                dma('sp', cv_sb[:, j, :], CVT[j], [B_CVT[j]], [B_cvs[j]])
            sg_t = [sb(f"sg_t{i}", [128, NE], BF16, st) for i in range(4)]
            B_sg = [Buf() for _ in range(4)]
            t1 = sb("t1", [128, NE], F32, st)
            B_t1 = Buf()
            t2 = sb("t2", [128, NE], F32, st)
            B_t2 = Buf()
            mst = [sb(f"mst{i}", [128, NE], BF16, st) for i in range(2)]
            B_mst = [Buf(), Buf()]
            for m in range(KC):
                k = m % 2
                dma('sp', sg_t[2 * k][:], SGA[m], [B_SGA[m]], [B_sg[2 * k]])
                dma('sp', sg_t[2 * k + 1][:], SGC[m], [B_SGC[m]], [B_sg[2 * k + 1]])
                wt, wb = load_w(wao[m], HC)
                proj(wt, wb, HC, attT, B_att, tiles_ext,
                     lambda c0, n, pap, pb, k=k: tt('dve', t1[:, c0:c0 + n], pap, sg_t[2 * k][:, c0:c0 + n], ALU.mult,
                                                   [pb, B_sg[2 * k]], [B_t1]))
                wt, wb = load_w(wco[m], HC)
                proj(wt, wb, HC, cv_sb, B_cvs, tiles_ext,
                     lambda c0, n, pap, pb, k=k: tt('dve', t2[:, c0:c0 + n], pap, sg_t[2 * k + 1][:, c0:c0 + n], ALU.mult,
                                                   [pb, B_sg[2 * k + 1]], [B_t2]))
                tt('pool', mst[k][:], t1[:], t2[:], ALU.add, [B_t1, B_t2], [B_mst[k]])
                dma('sp', MT[m], mst[k][:], [B_mst[k]], [B_MT[m]])
            P.barrier()
        st34.close()
        RSTD2 = dscr("RSTD2", [128, NE], F32)
        B_RSTD2 = Buf()
        with contextlib.ExitStack() as st:
            m_sb = sb("m_sb", [128, KC, NE], BF16, st)
            B_ms = [Buf() for _ in range(KC)]
            for kc in range(KC):
                dma('sp', m_sb[:, kc, :], MT[kc], [B_MT[kc]], [B_ms[kc]])
            xs = [sb(f"p4xs{i}", [128, NE], F32, st) for i in range(2)]
            B_xs = [Buf(), Buf()]
            accq = sb("accq", [128, NE], F32, st)
            sqt = sb("sqt", [128, NE], F32, st)
            B_accq, B_sqt = Buf(), Buf()
            mset('pool', accq[:], 0.0, [B_accq])
            xe_c = [xe[kc * 128:(kc + 1) * 128, :] for kc in range(KC)]
            for m in range(KC):
                k = m % 2
                dma('sp', xs[k][:], xe_c[m], (), [B_xs[k]])
                wt, wb = load_w(wo[m], KC)
                proj(wt, wb, KC, m_sb, B_ms, tiles_ext,
                     lambda c0, n, pap, pb, k=k, m=m: stt('dve', xs[k][:, c0:c0 + n], pap, g1[:, m:m + 1], xs[k][:, c0:c0 + n],
                                                         ALU.mult, ALU.add, [pb, B_modT2, B_xs[k]], [B_xs[k]]))
                dma('sp', XM[m], xs[k][:], [B_xs[k]], [B_XM[m]])
                tt('pool', sqt[:], xs[k][:], xs[k][:], ALU.mult, [B_xs[k]], [B_sqt])
                tt('pool', accq[:], accq[:], sqt[:], ALU.add, [B_sqt, B_accq], [B_accq])
            for (c0, n) in tiles_ext:
                pap, pb = psum_for(n)
                mm(pap, ones32[:], accq[:, c0:c0 + n], True, True, [B_ones32, B_accq], [pb])
                act(sqt[:, c0:c0 + n], pap, AF.Sqrt, [pb, B_sqt, B_cst], [B_sqt], bias=epsc, scale=1.0 / D)
            P.add('dve', lambda e: e.reciprocal(out=sqt[:], in_=sqt[:]), [B_sqt], [B_sqt])
            dma('sp', RSTD2, sqt[:], [B_sqt], [B_RSTD2])
            P.barrier()

        with contextlib.ExitStack() as st:
            h2 = sb("h2", [128, KC, NE], BF16, st)
            B_h2 = [Buf() for _ in range(KC)]
            with contextlib.ExitStack() as st2:
                xs = [sb(f"p5xs{i}", [128, NE], F32, st2) for i in range(3)]
                B_xs = [Buf(), Buf(), Buf()]
                rstd2 = sb("rstd2", [128, NE], F32, st2)
                B_rstd2 = Buf()
                dma('sp', rstd2[:], RSTD2, [B_RSTD2], [B_rstd2])
                for kc in range(KC):
                    i = kc % 3
                    dma('sp' if kc % 2 == 0 else 'pool', xs[i][:], XM[kc], [B_XM[kc]], [B_xs[i]])
                    stt('dve', xs[i][:], xs[i][:], a2[:, kc:kc + 1], rstd2[:], ALU.mult, ALU.mult,
                        [B_xs[i], B_a2, B_rstd2], [B_xs[i]])
                    act(h2[:, kc, :], xs[i][:], AF.Identity, [B_xs[i], B_modT2], [B_h2[kc]], bias=s2[:, kc:kc + 1])
                P.barrier()
            gpre = sb("gpre", [128, NE], F32, st)
            gcv = sb("gcv", [128, NE], F32, st)
            vpre = sb("vpre", [128, NE], F32, st)
            vcv = gpre
            ast = [sb(f"ast{i}", [128, NH], BF16, st) for i in range(2)]
            fcw_sb = sb("fcw_sb", [128, 2 * JF * 3], F32, st)
            B_gpre, B_gcv, B_vpre, B_fcw = Buf(), Buf(), Buf(), Buf()
            B_vcv = B_gpre
            B_ast = [Buf(), Buf()]
            dma('sp', fcw_sb[:], fcw, (), [B_fcw])

            def conv3(eng, dst, src, B_dst, B_src, ch):
                w0 = fcw_sb[:, 3 * ch:3 * ch + 1]
                w1 = fcw_sb[:, 3 * ch + 1:3 * ch + 2]
                w2 = fcw_sb[:, 3 * ch + 2:3 * ch + 3]
                ts(eng, dst[:, HALO:NE], src[:, HALO:NE], w2, None, ALU.mult, None, [B_src, B_fcw], [B_dst])
                stt(eng, dst[:, HALO:NE], src[:, HALO - 1:NE - 1], w1, dst[:, HALO:NE], ALU.mult, ALU.add, [B_src, B_fcw, B_dst], [B_dst])
                stt(eng, dst[:, HALO:NE], src[:, HALO - 2:NE - 2], w0, dst[:, HALO:NE], ALU.mult, ALU.add, [B_src, B_fcw, B_dst], [B_dst])

            for j in range(JF):
                k = j % 2
                if j % max(1, JF // KC) == 0 and j // max(1, JF // KC) < KC:
                    m_ = j // max(1, JF // KC)
                    dma('pool', WDB[m_], wdn[m_], (), [B_WDB[m_]])
                wt, wb = load_w(wup[j], KC)
                proj(wt, wb, KC, h2, B_h2, tiles_ext,
                     lambda c0, n, pap, pb: cp('act', gpre[:, c0:c0 + n], pap, [pb], [B_gpre]))
                ts('dve', gpre[:, 0:HALO], gpre[:, 0:HALO], hflag, None, ALU.mult, None, [B_gpre, B_cst], [B_gpre])
                conv3('dve', gcv, gpre, B_gcv, B_gpre, j)
                act(gcv[:, HALO:NE], gcv[:, HALO:NE], AF.Silu, [B_gcv], [B_gcv])
                wt, wb = load_w(wup[JF + j], KC)
                proj(wt, wb, KC, h2, B_h2, tiles_ext,
                     lambda c0, n, pap, pb: cp('act', vpre[:, c0:c0 + n], pap, [pb], [B_vpre]))
                ts('pool', vpre[:, 0:HALO], vpre[:, 0:HALO], hflag, None, ALU.mult, None, [B_vpre, B_cst], [B_vpre])
                conv3('dve', vcv, vpre, B_vcv, B_vpre, JF + j)
                tt('pool', ast[k][:], gcv[:, HALO:NE], vcv[:, HALO:NE], ALU.mult, [B_gcv, B_vcv], [B_ast[k]])
                dma('sp', ATs[j], ast[k][:], [B_ast[k]], [B_ATs[j]])
            P.barrier()

        wst.close()
        with contextlib.ExitStack() as st:
            a_sb = sb("a_sb", [128, JF, 512], BF16, st)
            B_as = [Buf() for _ in range(JF)]
            wd_t = [sb(f"wd{i}", [128, JF, 128], BF16, st) for i in range(2)]
            B_wd = [Buf(), Buf()]
            xm_t = [sb(f"xm_t{i}", [128, 512], F32, st) for i in range(2)]
            B_xmt = [Buf(), Buf()]
            NXO = 3
            xo_t = [sb(f"xo_t{i}", [128, 512], F32, st) for i in range(NXO)]
            B_xot = [Buf() for _ in range(NXO)]
            xo_ring = Ring(list(range(NXO)))
            sq7 = sb("sq7", [128, 512], F32, st)
            acc7s = [sb(f"acc7_{i}", [128, 512], F32, st) for i in range(2)]
            B_acc7s = [Buf(), Buf()]
            rstd3 = sb("rstd3", [128, 512], F32, st)
            ot = [sb(f"p7ot{i}", [128, 512], F32, st) for i in range(2)]
            B_sq7, B_r3 = Buf(), Buf()
            B_ot = [Buf(), Buf()]

            def fin_gen(n):
                acc7 = acc7s[n % 2]
                B_acc7 = B_acc7s[n % 2]
                yield
                yield
                pap, pb = psum_for(512)
                mm(pap, ones32[:], acc7[:], True, True, [B_ones32, B_acc7], [pb])
                act(rstd3[:], pap, AF.Sqrt, [pb, B_cst], [B_r3], bias=epsc, scale=1.0 / D)
                P.add('dve', lambda e: e.reciprocal(out=rstd3[:], in_=rstd3[:]), [B_r3], [B_r3])
                yield
                for kc in range(KC):
                    k = xo_ring.next()
                    o = kc % 2
                    dma('act', xo_t[k][:], XO[kc][:, n * 512:(n + 1) * 512], [B_XO[kc][n]], [B_xot[k]])
                    stt('dve', ot[o][:], xo_t[k][:], vec_sb[:, 2 * KC + kc:2 * KC + kc + 1], rstd3[:], ALU.mult, ALU.mult,
                        [B_xot[k], B_vec, B_r3], [B_ot[o]])
                    dma('act', outT[kc * 128:(kc + 1) * 128, n * 512:(n + 1) * 512], ot[o][:], [B_ot[o]], [B_OUT])
                    yield

            def advance(gen, steps):
                for _ in range(steps):
                    try:
                        next(gen)
                    except StopIteration:
                        return

            it = 0
            fin = None
            for n in range(NT):
                for j in range(JF):
                    dma('sp', a_sb[:, j, :], ATs[j][:, n * 512:(n + 1) * 512], [B_ATs[j]], [B_as[j]])
                for m in range(KC):
                    k = it % 2
                    it += 1
                    dma('pool', wd_t[k][:], WDB[m], [B_WDB[m]], [B_wd[k]])
                    dma('sp', xm_t[k][:], XM[m][:, HALO + n * 512:HALO + (n + 1) * 512], [B_XM[m]], [B_xmt[k]])
                    pap, pb = psum_for(512)
                    for j in range(JF):
                        mm(pap, wd_t[k][:, j, :], a_sb[:, j, :], j == 0, j == JF - 1, [B_wd[k], B_as[j]], [pb])
                    stt('dve', xm_t[k][:], pap, g2[:, m:m + 1], xm_t[k][:], ALU.mult, ALU.add, [pb, B_modT2, B_xmt[k]], [B_xmt[k]])
                    dma('sp', XO[m][:, n * 512:(n + 1) * 512], xm_t[k][:], [B_xmt[k]], [B_XO[m][n]])
                    if m == 0:
                        tt('dve', acc7s[n % 2][:], xm_t[k][:], xm_t[k][:], ALU.mult, [B_xmt[k]], [B_acc7s[n % 2]])
                    else:
                        tt('dve', sq7[:], xm_t[k][:], xm_t[k][:], ALU.mult, [B_xmt[k]], [B_sq7])
                        tt('dve', acc7s[n % 2][:], acc7s[n % 2][:], sq7[:], ALU.add, [B_sq7, B_acc7s[n % 2]], [B_acc7s[n % 2]])
                    if fin is not None and m >= 1:
                        advance(fin, 4)
                if fin is not None:
                    advance(fin, 1000)
                fin = fin_gen(n)
            advance(fin, 1000)
            P.barrier()

        P.emit()
    return nc


def relay(W, mw=128):
    K, N = W.shape
    return np.ascontiguousarray(W.reshape(K // 128, 128, N // mw, mw).transpose(2, 1, 0, 3))


def vecl(v):
    return np.ascontiguousarray(v.reshape(-1, 128).T)


def host_consts(cfg, is_second):
    NB, NBT = cfg.NB, cfg.NBT
    cst = np.zeros((128, 148), np.float32)
    cst[:, 146] = EPS
    i = np.arange(32)
    cst[0:32, 0] = (ROPE_THETA ** (-(2.0 * (i % 16)) / 32.0)).astype(np.float32)
    cst[:, 1] = 1.0 if is_second else 0.0
    pv = 0.0 if is_second else -1e30
    for ob in range(NB):
        eb = np.full(16, -1e30, np.float32)
        eb[0:NB] = pv
        eb[NB:NB + ob] = 0.0
        cst[:, 2 + 16 * ob:2 + 16 * ob + 16] = eb
    eb = np.full(16, -1e30, np.float32)
    eb[0:NB - 1] = pv
    cst[:, 2 + 16 * 8:2 + 16 * 8 + 16] = eb
    cm = np.zeros((128, 512), np.float32)
    cm[:, 0:128] = np.eye(128, dtype=np.float32)
    for r in range(16):
        cm[r + 16, 128 + r] = -1.0
        cm[r, 128 + 16 + r] = 1.0
    kk = np.arange(128)[:, None]
    qq = np.arange(128)[None, :]
    cm[:, 256:384] = np.where(qq >= kk, 0.0, NEG)
    cm[:, 384:512] = 1.0
    return cst, cm


def prepare_inputs(cfg, x, c, positions, w_ada, b_ada, g_mix, w_in, conv_w, w_attn_out, w_conv_out,
                   w_o, g_ffn, w_up, ffn_conv_w, w_down, g_final):
    D, NH, NE, HALO, AW, CW = cfg.D, cfg.NH, cfg.NE, cfg.HALO, cfg.AW, cfg.CW
    x = np.asarray(x, np.float32)
    w_in0 = np.asarray(w_in[0], np.float32)
    o = 0
    parts = {}
    for name, wdt in (("wq", AW), ("wk", AW), ("wv", AW), ("wcb", CW), ("wcc", CW), ("wcu", CW), ("wga", D), ("wgc", D)):
        parts[name] = relay(w_in0[:, o:o + wdt])
        o += wdt
    shared = dict(parts)
    shared["w_ada"] = np.ascontiguousarray(np.asarray(w_ada[0], np.float32))
    shared["b_ada"] = np.ascontiguousarray(np.asarray(b_ada[0], np.float32)[None, :])
    shared["vecs"] = np.ascontiguousarray(np.concatenate([vecl(np.asarray(g_mix[0], np.float32)),
                                                          vecl(np.asarray(g_ffn[0], np.float32)),
                                                          vecl(np.asarray(g_final, np.float32))], axis=1))
    cwl = np.asarray(conv_w[0], np.float32)
    shared["cw"] = np.ascontiguousarray(cwl.T.reshape(CW // 128, 128, 3).transpose(1, 0, 2).reshape(128, -1))
    shared["wao"] = relay(np.asarray(w_attn_out[0], np.float32))
    shared["wco"] = relay(np.asarray(w_conv_out[0], np.float32))
    shared["wo"] = relay(np.asarray(w_o[0], np.float32))
    shared["wup"] = relay(np.asarray(w_up[0], np.float32))
    fl = np.asarray(ffn_conv_w[0], np.float32)
    shared["fcw"] = np.ascontiguousarray(fl.T.reshape(-1, 128, 3).transpose(1, 0, 2).reshape(128, -1))
    shared["wdn"] = relay(np.asarray(w_down[0], np.float32))
    in_maps = []
    pos = np.asarray(positions, np.int32)
    for core in range(2 * cfg.B):
        b, half = divmod(core, 2)
        xT = x[b].T
        m = dict(shared)
        if half == 0:
            xe = np.zeros((D, NE), np.float32)
            xe[:, HALO:] = xT[:, 0:NH]
            pe = np.zeros((NE,), np.int32)
            pe[HALO:] = pos[b, 0:NH]
        else:
            xe = np.ascontiguousarray(xT[:, NH - HALO:2 * NH])
            pe = pos[b, NH - HALO:2 * NH]
        m["xe"] = xe
        m["xp"] = np.ascontiguousarray(xT[:, 0:NH])
        m["cT"] = vecl(np.asarray(c[b], np.float32))
        m["pos_e"] = np.ascontiguousarray(np.broadcast_to(pe[None, :], (32, NE)))
        m["pos_p"] = np.ascontiguousarray(np.broadcast_to(pos[b, 0:NH][None, :], (32, NH)))
        cst, cm = host_consts(cfg, half == 1)
        m["cst"] = cst
        m["cmat"] = cm
        in_maps.append(m)
    return in_maps


_NC_CACHE = {}


def run(cfg, inputs, debug=False):
    key = (cfg.D, cfg.S, cfg.B, cfg.H, cfg.DFF, debug)
    if key not in _NC_CACHE:
        _NC_CACHE[key] = build(cfg, debug)
    nc = _NC_CACHE[key]
    in_maps = prepare_inputs(cfg, **inputs)
    ncores = 2 * cfg.B
    res = run_bass_kernel_spmd(nc, in_maps, core_ids=list(range(ncores)))
    out = np.empty((cfg.B, cfg.S, cfg.D), np.float32)
    for core in range(ncores):
        b, half = divmod(core, 2)
        out[b, half * cfg.NH:(half + 1) * cfg.NH, :] = np.asarray(res.results[core]["outT"]).T
    return out, res


def kernel(**inputs):
    cfg = Cfg()
    out, _ = run(cfg, inputs)
    return out
```
